# Optimizing a Trainium2 kernel written in Bass

```python
import math
import jax, jax.numpy as jnp
from jax import lax
import numpy as np

D_MODEL = 4096
BATCH = 16
SEQ = 256
DEPTH = 2
DEC_BATCH = 4
DEC_SEQ = 2048
PAST_LEN = 256

GRID_W = 64
DA_D = 128
DA_W = D_MODEL // 2
DA_HEADS = DA_W // (2 * DA_D)
RW_W = D_MODEL // 4
RW_HEAD = 64
RW_HEADS = RW_W // RW_HEAD
RW_DECAY_R = 64
RW_A_R = 64
RW_GATE_R = 160
RW_IN = 3 * RW_W + 2 * RW_DECAY_R + 2 * RW_A_R + RW_GATE_R
RW_GN_EPS = 64e-5
CM_W = D_MODEL // 4
CM_GROUPS = 4
CHUNK = 128
MIX_W = DA_W + RW_W + CM_W
P_IN = 3 * DA_W + RW_IN + 2 * CM_W
D_FF = 11008
CONV_W = 3
Q_BLOCK = 128
ROPE_THETA = 10000.0
NORM_EPS = 1e-6

kernel_name = 'hybrid_diffattn_rwkv7_chunkmlp_prefix_ctx_step'


def _rmsnorm(x, g):
    xf = x.astype(jnp.float32)
    y = xf * lax.rsqrt(jnp.mean(xf * xf, axis=-1, keepdims=True) + NORM_EPS)
    return y.astype(x.dtype) * g


def _split(x, sizes):
    idx = [int(i) for i in np.cumsum(sizes)[:-1]]
    return jnp.split(x, idx, axis=-1)


def _centred_dwconv(h, w):
    T = h.shape[1]
    pad = CONV_W // 2
    hp = jnp.pad(h, ((0, 0), (pad, pad), (0, 0)))
    return sum(hp[:, j:j + T] * w[j] for j in range(CONV_W))


def _rope_1d(x, pos):
    n = x.shape[-1] // 2
    inv = ROPE_THETA ** (-jnp.arange(n, dtype=jnp.float32) / n)
    ang = pos.astype(jnp.float32)[:, None] * inv[None, :]
    cos = jnp.cos(ang)[None, :, None, None, :]
    sin = jnp.sin(ang)[None, :, None, None, :]
    xf = x.astype(jnp.float32)
    x1, x2 = xf[..., :n], xf[..., n:]
    return jnp.concatenate([x1 * cos - x2 * sin, x2 * cos + x1 * sin], axis=-1)


def _axial_rope(x):
    T = x.shape[1]
    rows = T // GRID_W
    row = jnp.repeat(jnp.arange(rows), GRID_W)
    col = jnp.tile(jnp.arange(GRID_W), rows)
    half = x.shape[-1] // 2
    y = jnp.concatenate([_rope_1d(x[..., :half], row), _rope_1d(x[..., half:], col)], axis=-1)
    return y.astype(x.dtype)


def _diff_attention(q, k, v, lam):
    B, T, H = q.shape[:3]
    nb = T // Q_BLOCK
    qb = q.reshape(B, nb, Q_BLOCK, H, 2, DA_D).swapaxes(0, 1)
    scale = DA_D ** -0.5

    def one_block(qblk):
        s = jnp.einsum('bqhcd,bkhcd->cbhqk', qblk, k, preferred_element_type=jnp.float32) * scale
        p = jax.nn.softmax(s, axis=-1)
        a = p[0] - lam * p[1]
        return jnp.einsum('bhqk,bkhe->bqhe', a.astype(v.dtype), v)

    o = lax.map(one_block, qb)
    return o.swapaxes(0, 1).reshape(B, T, H, 2 * DA_D)


def _wkv_scan(s0, r, w, kk, a, k, v, reverse):
    def step(S, inp):
        r_t, w_t, kk_t, a_t, k_t, v_t = inp
        sa = jnp.einsum('bhvk,bhk->bhv', S, kk_t)
        S = (S * w_t[:, :, None, :]
             - sa[..., None] * (kk_t * a_t)[:, :, None, :]
             + v_t[..., None] * k_t[:, :, None, :])
        y = jnp.einsum('bhvk,bhk->bhv', S, r_t)
        return S, y

    xs = tuple(t.swapaxes(0, 1) for t in (r, w, kk, a, k, v))
    S, ys = lax.scan(step, s0.astype(jnp.float32), xs, reverse=reverse)
    return S, ys.swapaxes(0, 1)


def _rwkv7(z, s0_f, s0_b, lp):
    B, T, _ = z.shape
    odt = z.dtype
    z = z.astype(jnp.float32)
    r, k, v, dec, aa, gl = _split(z, [RW_W, RW_W, RW_W, 2 * RW_DECAY_R, 2 * RW_A_R, RW_GATE_R])
    dec = dec.reshape(B, T, 2, RW_DECAY_R)
    aa = aa.reshape(B, T, 2, RW_A_R)
    wl = lp['rw_w0'] + jnp.einsum('btdr,drc->btdc', jnp.tanh(dec), lp['rw_w2'])
    decay = jnp.exp(-jnp.exp(-jax.nn.softplus(-wl) - 0.5))
    a = jax.nn.sigmoid(lp['rw_a0'] + jnp.einsum('btdr,drc->btdc', aa, lp['rw_a2']))
    gate = jax.nn.sigmoid(gl) @ lp['rw_g2']
    heads = lambda t: t.reshape(*t.shape[:-1], RW_HEADS, RW_HEAD)
    kk = heads(k * lp['rw_k_k'])
    kk = kk * lax.rsqrt(jnp.sum(kk * kk, axis=-1, keepdims=True) + 1e-12)
    kd = heads(k[:, :, None, :] * (1.0 + (a - 1.0) * lp['rw_k_a']))
    wd, ad = heads(decay), heads(a)
    rh, vh = heads(r), heads(v)
    s_f, y_f = _wkv_scan(s0_f, rh, wd[:, :, 0], kk, ad[:, :, 0], kd[:, :, 0], vh, False)
    s_b, y_b = _wkv_scan(s0_b, rh, wd[:, :, 1], kk, ad[:, :, 1], kd[:, :, 1], vh, True)
    y = y_f + y_b
    mu = jnp.mean(y, axis=-1, keepdims=True)
    var = jnp.mean(jnp.square(y - mu), axis=-1, keepdims=True)
    y = ((y - mu) * lax.rsqrt(var + RW_GN_EPS)).reshape(B, T, RW_W) * lp['rw_gn_g'] + lp['rw_gn_b']
    bonus = jnp.sum(rh * (kd[:, :, 0] + kd[:, :, 1]) * lp['rw_r_k'], axis=-1, keepdims=True) * vh
    y = (y + bonus.reshape(B, T, RW_W)) * gate
    return y.astype(odt), s_f.astype(odt), s_b.astype(odt)


def _chunk_mlp(uv, gain, ws, bs):
    u, v = jnp.split(uv, 2, axis=-1)
    B, T, _ = u.shape
    z = _rmsnorm(v, gain).reshape(B, T // CHUNK, CHUNK, CM_GROUPS, CM_W // CM_GROUPS)
    z = jnp.einsum('gpq,bnqgc->bnpgc', ws, z) + bs.T[:, :, None]
    return u * z.reshape(B, T, CM_W)


def _layer(x, cond, ctx, layer_idx, lp):
    B, T, _ = x.shape
    mod = jax.nn.silu(cond) @ lp['mod_w'] + lp['mod_b']
    sh1, sc1, g1, sh2, sc2, g2 = jnp.split(mod[:, None, :], 6, axis=-1)
    h = _rmsnorm(x, lp['norm1_g']) * (1.0 + sc1) + sh1
    proj = h @ lp['w_in']
    q, k, v, z_rw, uv = _split(proj, [DA_W, DA_W, DA_W, RW_IN, 2 * CM_W])
    q = q.reshape(B, T, DA_HEADS, 2, DA_D)
    k = k.reshape(B, T, DA_HEADS, 2, DA_D)
    v = v.reshape(B, T, DA_HEADS, 2 * DA_D)
    if ctx is None:
        k_all, v_all = k, v
        s0_f = jnp.zeros((B, RW_HEADS, RW_HEAD, RW_HEAD), jnp.float32)
        s0_b = s0_f
    else:
        k_ctx, v_ctx, s0_f, s0_b = ctx
        q = _axial_rope(q)
        k_all = jnp.concatenate([_axial_rope(k), k_ctx.astype(k.dtype)], axis=1)
        v_all = jnp.concatenate([v, v_ctx.astype(v.dtype)], axis=1)
    lam_init = 0.8 - 0.6 * math.exp(-0.3 * layer_idx)
    lq1, lk1, lq2, lk2 = lp['da_lambda'].astype(jnp.float32)
    lam = jnp.exp(jnp.sum(lq1 * lk1)) - jnp.exp(jnp.sum(lq2 * lk2)) + lam_init
    y_da = _diff_attention(q, k_all, v_all, lam)
    y_da = (_rmsnorm(y_da, lp['da_subln_g']) * (1.0 - lam_init)).reshape(B, T, DA_W)
    y_rw, s_f, s_b = _rwkv7(_centred_dwconv(z_rw, lp['rw_conv_w']), s0_f, s0_b, lp)
    y_cm = _chunk_mlp(uv, lp['cm_norm_g'], lp['cm_ws'], lp['cm_bs'])
    mix = jnp.concatenate([y_da.astype(x.dtype), y_rw.astype(x.dtype), y_cm.astype(x.dtype)], axis=-1)
    x = x + g1 * (mix @ lp['w_out'])
    h = _rmsnorm(x, lp['norm2_g']) * (1.0 + sc2) + sh2
    up = _centred_dwconv(h @ lp['ffn_up'], lp['ffn_conv_w']) + lp['ffn_conv_b']
    ga, gb = jnp.split(up, 2, axis=-1)
    x = x + g2 * ((jax.nn.silu(ga) * gb) @ lp['ffn_down'])
    return x, (k, v, s_f, s_b)


def setup_inputs(seed: int = 0) -> dict:
    key = jax.random.key(seed)
    ks = iter(jax.random.split(key, 48))
    nrm = lambda shape, s: jax.random.normal(next(ks), shape, jnp.float32) * s
    gain = lambda shape: 1.0 + nrm(shape, 0.02)
    L = DEPTH
    d = {}
    d['x_prompt'] = nrm((BATCH, SEQ, D_MODEL), 1.0)
    d['x_sample'] = nrm((DEC_BATCH, DEC_SEQ, D_MODEL), 1.0)
    d['cache_da_k'] = nrm((DEC_BATCH, L, PAST_LEN, DA_HEADS, 2, DA_D), 1.0)
    d['cache_da_v'] = nrm((DEC_BATCH, L, PAST_LEN, DA_HEADS, 2 * DA_D), 1.0)
    d['state_rwkv'] = nrm((DEC_BATCH, L, 2, RW_HEADS, RW_HEAD, RW_HEAD), 0.3)
    d['c'] = nrm((DEC_BATCH, D_MODEL), 1.0)
    d['c_ctx'] = nrm((D_MODEL,), 1.0)
    d['mod_w'] = nrm((L, D_MODEL, 6 * D_MODEL), 0.5 * D_MODEL ** -0.5)
    d['mod_b'] = nrm((L, 6 * D_MODEL), 0.02)
    d['norm1_g'] = gain((L, D_MODEL))
    d['norm2_g'] = gain((L, D_MODEL))
    d['w_in'] = nrm((L, D_MODEL, P_IN), D_MODEL ** -0.5)
    d['da_lambda'] = nrm((L, 4, DA_D), 0.1)
    d['da_subln_g'] = gain((L, 2 * DA_D))
    d['rw_conv_w'] = nrm((L, CONV_W, RW_IN), CONV_W ** -0.5)
    d['rw_w0'] = nrm((L, 2, RW_W), 0.5) - 1.0
    d['rw_w2'] = nrm((L, 2, RW_DECAY_R, RW_W), 0.5 * RW_DECAY_R ** -0.5)
    d['rw_a0'] = nrm((L, 2, RW_W), 0.1)
    d['rw_a2'] = nrm((L, 2, RW_A_R, RW_W), 0.5 * RW_A_R ** -0.5)
    d['rw_g2'] = nrm((L, RW_GATE_R, RW_W), RW_GATE_R ** -0.5)
    d['rw_k_k'] = 0.85 + nrm((L, RW_W), 0.02)
    d['rw_k_a'] = gain((L, RW_W))
    d['rw_r_k'] = nrm((L, RW_HEADS, RW_HEAD), 0.1)
    d['rw_gn_g'] = gain((L, RW_W))
    d['rw_gn_b'] = nrm((L, RW_W), 0.02)
    d['cm_norm_g'] = gain((L, CM_W))
    d['cm_ws'] = nrm((L, CM_GROUPS, CHUNK, CHUNK), CHUNK ** -0.5)
    d['cm_bs'] = nrm((L, CM_GROUPS, CHUNK), 0.02)
    d['w_out'] = nrm((L, MIX_W, D_MODEL), MIX_W ** -0.5)
    d['ffn_up'] = nrm((L, D_MODEL, 2 * D_FF), D_MODEL ** -0.5)
    d['ffn_conv_w'] = nrm((L, CONV_W, 2 * D_FF), CONV_W ** -0.5)
    d['ffn_conv_b'] = nrm((L, 2 * D_FF), 0.02)
    d['ffn_down'] = nrm((L, D_FF, D_MODEL), D_FF ** -0.5)
    d['final_norm_g'] = gain((D_MODEL,))
    return d


def reference(x_prompt, x_sample, cache_da_k, cache_da_v, state_rwkv, c, c_ctx,
              mod_w, mod_b, norm1_g, norm2_g, w_in, da_lambda, da_subln_g,
              rw_conv_w, rw_w0, rw_w2, rw_a0, rw_a2, rw_g2, rw_k_k, rw_k_a, rw_r_k,
              rw_gn_g, rw_gn_b, cm_norm_g, cm_ws, cm_bs, w_out,
              ffn_up, ffn_conv_w, ffn_conv_b, ffn_down, final_norm_g):
    xp, xs = x_prompt, x_sample
    new_k, new_v, new_s = [], [], []
    for l in range(DEPTH):
        lp = dict(mod_w=mod_w[l], mod_b=mod_b[l], norm1_g=norm1_g[l], norm2_g=norm2_g[l],
                  w_in=w_in[l], da_lambda=da_lambda[l], da_subln_g=da_subln_g[l],
                  rw_conv_w=rw_conv_w[l], rw_w0=rw_w0[l], rw_w2=rw_w2[l], rw_a0=rw_a0[l],
                  rw_a2=rw_a2[l], rw_g2=rw_g2[l], rw_k_k=rw_k_k[l], rw_k_a=rw_k_a[l],
                  rw_r_k=rw_r_k[l], rw_gn_g=rw_gn_g[l], rw_gn_b=rw_gn_b[l],
                  cm_norm_g=cm_norm_g[l], cm_ws=cm_ws[l], cm_bs=cm_bs[l], w_out=w_out[l],
                  ffn_up=ffn_up[l], ffn_conv_w=ffn_conv_w[l], ffn_conv_b=ffn_conv_b[l],
                  ffn_down=ffn_down[l])
        xp, (k_l, v_l, sf_l, sb_l) = _layer(xp, c_ctx[None, :], None, l, lp)
        new_k.append(k_l)
        new_v.append(v_l)
        new_s.append(jnp.stack([sf_l, sb_l], axis=1))
        ctx = (cache_da_k[:, l], cache_da_v[:, l], state_rwkv[:, l, 0], state_rwkv[:, l, 1])
        xs, _ = _layer(xs, c, ctx, l, lp)
    y_prompt = _rmsnorm(xp, final_norm_g)
    y_sample = _rmsnorm(xs, final_norm_g)
    new_cache_da_k = jnp.stack(new_k, axis=1)
    new_cache_da_v = jnp.stack(new_v, axis=1)
    new_state_rwkv = jnp.stack(new_s, axis=1)
    return (y_prompt, y_sample, new_cache_da_k, new_cache_da_v, new_state_rwkv)
```

```python
import math
from contextlib import ExitStack
import numpy as np
import concourse.bass as bass
import concourse.mybir as mybir
from concourse.bass_utils import run_bass_kernel_spmd

F32 = mybir.dt.float32
BF16 = mybir.dt.bfloat16
AF = mybir.ActivationFunctionType
ALU = mybir.AluOpType
AX = mybir.AxisListType

EPOCH = 30000
SAME_ENGINE_SYNC = True


class Buf:
    __slots__ = ("name", "w", "r")

    def __init__(self, name=""):
        self.name = name
        self.w = None
        self.r = {}


class Tile:
    def __init__(self, t, name=""):
        self.t = t
        self.b = Buf(name)

    def __getitem__(self, k):
        return self.t[k]


def _bufs(xs):
    out = []
    for x in xs:
        if x is None:
            continue
        out.append(x.b if isinstance(x, Tile) else x)
    return out


class Sched:
    def __init__(self, nc, ndma=8):
        self.nc = nc
        self.eng = {"pe": nc.tensor, "dve": nc.vector, "act": nc.scalar,
                    "pool": nc.gpsimd, "sp": nc.sync}
        self.csem = {}
        self.cnt = {}
        self.epoch = {}
        self.seen = {e: {} for e in self.eng}
        for e in self.eng:
            self.epoch[e] = 0
            self.cnt[e] = 0
            self.csem[(e, 0)] = nc.alloc_semaphore(f"c_{e}_0")
        self.ndma = ndma
        self.dsem = {}
        self.dval = {}
        self.dnext = {}
        self.n_instr = 0

    def _wait(self, e, tok):
        if tok is None:
            return
        if tok[0] == "c":
            _, fe, ep, c = tok
            if fe == e and (e == "pe" or not SAME_ENGINE_SYNC):
                return
            key = ("c", fe, ep)
            if self.seen[e].get(key, 0) >= c:
                return
            self.eng[e].wait_ge(self.csem[(fe, ep)], c)
            self.seen[e][key] = c
        else:
            _, q, i, v = tok
            key = ("d", q, i)
            if self.seen[e].get(key, 0) >= v:
                return
            self.eng[e].wait_ge(self.dsem[(q, i)], v)
            self.seen[e][key] = v

    def _deps(self, e, reads, writes):
        for b in reads:
            self._wait(e, b.w)
        for b in writes:
            self._wait(e, b.w)
            for t in b.r.values():
                self._wait(e, t)

    def _mark(self, key, tok, reads, writes):
        for b in reads:
            b.r[key] = tok
        for b in writes:
            b.w = tok
            b.r = {}

    def _next_tok(self, e):
        if self.cnt[e] >= EPOCH:
            self.epoch[e] += 1
            self.cnt[e] = 0
            self.csem[(e, self.epoch[e])] = self.nc.alloc_semaphore(f"c_{e}_{self.epoch[e]}")
        self.cnt[e] += 1
        return ("c", e, self.epoch[e], self.cnt[e])

    def op(self, e, fn, reads=(), writes=()):
        reads = _bufs(reads)
        writes = _bufs(writes)
        self._deps(e, reads, writes)
        tok = self._next_tok(e)
        ins = fn(self.eng[e])
        ins.then_inc(self.csem[(e, tok[2])], 1)
        self._mark(e, tok, reads, writes)
        self.n_instr += 1
        return tok

    def dma(self, q, out, in_, reads=(), writes=(), **kw):
        reads = _bufs(reads)
        writes = _bufs(writes)
        self._deps(q, reads, writes)
        i = self.dnext.get(q, 0)
        self.dnext[q] = (i + 1) % self.ndma
        if (q, i) not in self.dsem:
            self.dsem[(q, i)] = self.nc.alloc_semaphore(f"d_{q}_{i}")
            self.dval[(q, i)] = 0
        if self.dval[(q, i)] > 0:
            self._wait(q, ("d", q, i, self.dval[(q, i)]))
        self.dval[(q, i)] += 16
        tok = ("d", q, i, self.dval[(q, i)])
        self.eng[q].dma_start(out=out, in_=in_, **kw).then_inc(self.dsem[(q, i)], 16)
        self._mark(("d", q, i), tok, reads, writes)
        self.n_instr += 1
        return tok

    def barrier(self):
        toks = []
        for e in self.eng:
            if self.cnt[e] > 0:
                toks.append(("c", e, self.epoch[e], self.cnt[e]))
        for (q, i), v in self.dval.items():
            if v > 0:
                toks.append(("d", q, i, v))
        for e in self.eng:
            for t in toks:
                if t[0] == "c" and t[1] == e and (e == "pe" or not SAME_ENGINE_SYNC):
                    continue
                self._wait(e, t)


class Cfg:
    def __init__(self, D=4096, T=2048, DFF=11008, PAST=256, SEQ=256, GRID_W=64, L=2):
        self.D, self.T, self.DFF, self.PAST, self.SEQ, self.GRID_W, self.L = D, T, DFF, PAST, SEQ, GRID_W, L
        self.DC = D // 128
        self.DA_W = D // 2
        self.H = self.DA_W // 256
        self.RW_W = D // 4
        self.RH = self.RW_W // 64
        self.RG = self.RW_W // 128
        self.RW_IN = 3 * self.RW_W + 128 + 128 + 160
        self.CM_W = D // 4
        self.P_IN = 3 * self.DA_W + self.RW_IN + 2 * self.CM_W
        self.TT = min(512, T)
        self.NT = T // self.TT
        self.NB = T // 128
        self.KF = DFF // 128
        self.NSEG = T // SEQ
        self.RW0 = 3 * self.DA_W
        self.UV0 = self.RW0 + self.RW_IN
        self.NK = T + PAST
        self.NKB = self.NK // 128


NORM_EPS = 1e-6
PER_STREAM = {"xT", "condT", "mprev", "mnext", "kcT", "vc", "qaug", "kaug", "ropeC", "ropeS", "keepc", "s0T"}
MASK_BIG = 1024.0


def build(cfg, debug=False, stop=99, rw=True, rwstop=9, NS=1):
    c = cfg
    D, T, DC, L, TT, NT = c.D, c.T, c.DC, c.L, c.TT, c.NT
    nc = bass.Bass("TRN2", target_bir_lowering=False)
    S = Sched(nc)

    cache = {}
    cur = [0]

    def _mk(name, shape, dt, kind, per):
        key = f"{name}_{cur[0]}" if per else name
        if key not in cache:
            cache[key] = nc.dram_tensor(key, list(shape), dt, kind=kind).ap()
        return cache[key]

    def din(name, shape, dt=F32):
        return _mk(name, shape, dt, "ExternalInput", name in PER_STREAM)

    def dout(name, shape, dt=F32):
        return _mk(name, shape, dt, "ExternalOutput", True)

    def dscr(name, shape, dt=F32):
        return _mk(name, shape, dt, "Internal", False)

    uid = [0]

    def emit_stream():
        xT_in = din("xT", [D, T])
        condT = din("condT", [128, DC])
        ident_d = din("ident", [128, 128])
        mod_w = din("mod_w", [L, D, 6 * D])
        mod_bT = din("mod_bT", [L, 128, 6 * DC])
        n1gT = din("n1gT", [L, 128, DC])
        n2gT = din("n2gT", [L, 128, DC])
        fgT = din("fgT", [128, DC])
        w_in = din("w_in", [L, D, c.P_IN])
        w_out = din("w_out", [L, D, D])
        ffn_up = din("ffn_up", [L, D, 2 * c.DFF])
        ffn_down = din("ffn_down", [L, c.DFF, D])
        ffn_cw = din("ffn_cw", [L, 128, 2 * c.KF, 3])
        ffn_cb = din("ffn_cb", [L, 128, 2 * c.KF])
        mprev_d = din("mprev", [128, T])
        mnext_d = din("mnext", [128, T])
        kcT_d = din("kcT", [L, c.H, 2, 128, c.PAST])
        vc_d = din("vc", [L, c.H, c.PAST, 256])
        qaug_d = din("qaug", [8, T])
        kaug_d = din("kaug", [8, c.NK])
        ropeC_d = din("ropeC", [128, T])
        ropeS_d = din("ropeS", [128, T])
        lam_d = din("lam_rep", [L, 128, 512])
        subg_d = din("subg", [L, 128, 2])
        CMC = c.CM_W // 128
        cmg_d = din("cmg", [L, 128, CMC])
        wsT_d = din("wsT", [L, 4, 128, 128])
        bsr_d = din("bs_rep", [L, 4, 128, 128])
        RG, RH, RW_W = c.RG, c.RH, c.RW_W
        CK = 64
        NCH = T // CK
        NRC = (c.RW_IN + 127) // 128
        rw_cw_d = din("rw_cw", [L, 128, NRC, 3])
        w0T_d = din("w0T", [L, 128, 2, RG])
        a0T_d = din("a0T", [L, 128, 2, RG])
        w2_d = din("rw_w2", [L, 2, 64, RW_W])
        a2_d = din("rw_a2", [L, 2, 64, RW_W])
        g2_d = din("rw_g2", [L, 160, RW_W])
        rwp_d = din("rwp", [L, 128, 5, RG])
        bones_d = din("bones", [128, 128])
        cmask_d = din("cmask", [128, T])
        tri_d = din("tri", [64, 4, 64])
        keep_d = din("keepc", [128, 2, NCH])
        s0T_d = din("s0T", [L, 2, RG, 128, 64])
        statesT = dout("statesT", [c.NSEG, L, 2, RG, 128, 64])

        yT = dout("yT", [D, T])
        kvT = dout("kvT", [L, 2 * c.DA_W, T])
        dbg = {}
        if debug:
            dbg["projT"] = dout("dbg_projT", [c.P_IN, T])
            dbg["modT"] = dout("dbg_modT", [L, 128, 6 * DC])
            dbg["x1"] = dout("dbg_x1", [D, T])

        xs = [dscr("xs0", [D, T]), dscr("xs1", [D, T])]
        projT = dbg["projT"] if debug else dscr("projT", [c.P_IN, T])
        actT = dscr("actT", [c.DFF, T], BF16)
        mixT_d = dscr("mixT_d", [D, T])
        dd = dout if (debug and rwstop < 9) else dscr
        rwA = [dd(f"rwA{d}", [RW_W, T]) for d in range(2)]
        rwB = [dd(f"rwB{d}", [RW_W, T]) for d in range(2)]
        rwK = [dd(f"rwK{d}", [RW_W, T]) for d in range(2)]
        rwR = [dd(f"rwR{d}", [RW_W, T]) for d in range(2)]
        rwP = [dd(f"rwP{d}", [RW_W, NCH]) for d in range(2)]
        rwV = dd("rwV", [RW_W, T])
        rwBon = dd("rwBon", [RW_W, T])
        rwGate = dd("rwGate", [RW_W, T])
        rwY = [dd(f"rwY{d}", [RW_W, T]) for d in range(2)]

        es_glob = ExitStack()


        def alloc(es, name, shape, dt=F32):
            uid[0] += 1
            return Tile(es.enter_context(nc.sbuf_tensor(f"s{uid[0]}_{name}", list(shape), dt)), name)

        def palloc(es, name, shape, dt=F32):
            uid[0] += 1
            return Tile(es.enter_context(nc.psum_tensor(f"p{uid[0]}_{name}", list(shape), dt)), name)

        ident = alloc(es_glob, "ident", [128, 128])
        ones_f = alloc(es_glob, "ones_f", [128, 128])
        ones_b = alloc(es_glob, "ones_b", [128, 128], BF16)
        modT = [alloc(es_glob, f"modT{l}", [128, 6 * DC]) for l in range(L)]
        a1 = [alloc(es_glob, f"a1_{l}", [128, DC]) for l in range(L)]
        a2 = [alloc(es_glob, f"a2_{l}", [128, DC]) for l in range(L)]
        fg = alloc(es_glob, "fg", [128, DC])
        S.dma("sp", ident[:], ident_d, writes=[ident])
        S.dma("sp", fg[:], fgT, writes=[fg])
        S.op("dve", lambda e: e.memset(ones_f[:], 1.0), writes=[ones_f])
        S.op("dve", lambda e: e.memset(ones_b[:], 1.0), writes=[ones_b])

        SH1, SC1, G1, SH2, SC2, G2 = range(6)

        def modcol(l, which, ch):
            return modT[l][:, which * DC + ch: which * DC + ch + 1]

        def phase_mod():
            NBW = 2048
            nblk = 6 * D // NBW
            with ExitStack() as es:
                cnd = alloc(es, "cnd", [128, DC])
                sc = alloc(es, "silu_c", [128, DC])
                one1 = alloc(es, "one1", [1, 1])
                wbuf = [alloc(es, f"mw{i}", [128, NBW]) for i in range(3)]
                row = [alloc(es, f"mrow{i}", [1, NBW]) for i in range(2)]
                mb = alloc(es, "mb", [128, 6 * DC])
                gt = alloc(es, "gt", [128, DC])
                ps = [palloc(es, f"mps{i}", [128, NBW]) for i in range(1)]
                psT = palloc(es, "mpsT", [128, 6 * DC])
                S.dma("sp", cnd[:], condT, writes=[cnd])
                S.op("act", lambda e: e.activation(sc[:], cnd[:], AF.Silu), reads=[cnd], writes=[sc])
                S.op("dve", lambda e: e.memset(one1[:], 1.0), writes=[one1])
                k = 0
                for l in range(L):
                    for nb in range(nblk):
                        p = ps[0]
                        for ch in range(DC):
                            w = wbuf[k % 3]
                            k += 1
                            S.dma("sp", w[:], mod_w[l, ch * 128:(ch + 1) * 128, nb * NBW:(nb + 1) * NBW], writes=[w])
                            for j in range(NBW // 512):
                                S.op("pe", lambda e, j=j, w=w, ch=ch, p=p: e.matmul(
                                    p[0:1, j * 512:(j + 1) * 512], sc[:, ch:ch + 1], w[:, j * 512:(j + 1) * 512],
                                    start=(ch == 0), stop=(ch == DC - 1)), reads=[sc, w], writes=[p])
                        r = row[nb % 2]
                        S.op("act", lambda e, r=r, p=p: e.activation(r[:], p[0:1, :], AF.Copy), reads=[p], writes=[r])
                        for j in range(NBW // 128):
                            col = nb * (NBW // 128) + j
                            S.op("pe", lambda e, r=r, j=j, col=col: e.matmul(
                                psT[:, col:col + 1], r[0:1, j * 128:(j + 1) * 128], one1[0:1, 0:1],
                                start=True, stop=True), reads=[r, one1], writes=[psT])
                    S.dma("sp", mb[:], mod_bT[l], writes=[mb])
                    S.op("dve", lambda e, l=l: e.tensor_tensor(modT[l][:], psT[:], mb[:], ALU.add),
                         reads=[psT, mb], writes=[modT[l]])
                    for (dst, src, which) in ((a1[l], n1gT, SC1), (a2[l], n2gT, SC2)):
                        S.dma("sp", gt[:], src[l], writes=[gt])
                        S.op("dve", lambda e, dst=dst, which=which, l=l: e.scalar_tensor_tensor(
                            dst[:], modT[l][:, which * DC:(which + 1) * DC], 1.0, gt[:], ALU.add, ALU.mult),
                            reads=[modT[l], gt], writes=[dst])
                    if debug:
                        S.dma("sp", dbg["modT"][l], modT[l][:], reads=[modT[l]])
                S.barrier()

        def phase_norm(es_outer, src, a_t, sh_l_which, hT, dst_dram=None):
            with ExitStack() as es:
                rstd = alloc(es, "rstd", [128, T])
                xt = [alloc(es, f"nx{i}", [128, TT]) for i in range(3)]
                sq = [alloc(es, f"nsq{i}", [128, TT]) for i in range(2)]
                ps = [palloc(es, f"nps{i}", [128, TT]) for i in range(2)]
                k = 0
                for tt in range(NT):
                    p = ps[tt % 2]
                    for ch in range(DC):
                        x = xt[k % 3]
                        q = sq[k % 2]
                        k += 1
                        S.dma("sp", x[:], src[ch * 128:(ch + 1) * 128, tt * TT:(tt + 1) * TT], writes=[x])
                        S.op("act", lambda e, q=q, x=x: e.activation(q[:], x[:], AF.Square), reads=[x], writes=[q])
                        S.op("pe", lambda e, p=p, q=q, ch=ch: e.matmul(p[:], ones_f[:], q[:], start=(ch == 0), stop=(ch == DC - 1)),
                             reads=[ones_f, q], writes=[p])
                    sl = slice(tt * TT, (tt + 1) * TT)
                    S.op("act", lambda e, p=p, sl=sl: e.activation(rstd[:, sl], p[:], AF.Sqrt, bias=NORM_EPS, scale=1.0 / D),
                         reads=[p], writes=[rstd])
                    S.op("dve", lambda e, sl=sl: e.reciprocal(rstd[:, sl], rstd[:, sl]), reads=[rstd], writes=[rstd])
                with ExitStack() as es2:
                    x2 = [alloc(es2, f"nxx{i}", [128, T]) for i in range(2)]
                    tm = [alloc(es2, f"ntm{i}", [128, T]) for i in range(2)]
                    for ch in range(DC):
                        x = x2[ch % 2]
                        t_ = tm[ch % 2]
                        S.dma("sp", x[:], src[ch * 128:(ch + 1) * 128, :], writes=[x])
                        S.op("dve", lambda e, t_=t_, x=x: e.tensor_tensor(t_[:], x[:], rstd[:], ALU.mult), reads=[x, rstd], writes=[t_])
                        if dst_dram is None:
                            l, which = sh_l_which
                            S.op("act", lambda e, t_=t_, ch=ch, l=l, which=which: e.activation(
                                hT[:, ch, :], t_[:], AF.Identity, bias=modcol(l, which, ch), scale=a_t[:, ch:ch + 1]),
                                reads=[t_, a_t, modT[l]], writes=[hT])
                        else:
                            S.op("act", lambda e, x=x, t_=t_, ch=ch: e.activation(
                                x[:], t_[:], AF.Identity, bias=0.0, scale=a_t[:, ch:ch + 1]), reads=[t_, a_t], writes=[x])
                            S.dma("sp", dst_dram[ch * 128:(ch + 1) * 128, :], x[:], reads=[x])
                    S.barrier()

        def dense(es, hT, KC, W, cols, epilogue, tag, nw=3, nps=4, pre=None):
            wb = [alloc(es, f"{tag}w{i}", [128, KC, 128], BF16) for i in range(nw)]
            ps = [palloc(es, f"{tag}p{i}", [128, TT]) for i in range(nps)]
            k = 0
            for i, (c0, mw) in enumerate(cols):
                w = wb[i % nw]
                S.dma("pool", w[:, :, 0:mw], W[:, c0:c0 + mw].rearrange("(c p) m -> p c m", p=128), writes=[w])
                if pre is not None:
                    pre(i)
                for tt in range(NT):
                    p = ps[k % nps]
                    k += 1
                    for ch in range(KC):
                        S.op("pe", lambda e, p=p, w=w, ch=ch, mw=mw, tt=tt: e.matmul(
                            p[0:mw, :], w[:, ch, 0:mw], hT[:, ch, tt * TT:(tt + 1) * TT],
                            start=(ch == 0), stop=(ch == KC - 1)), reads=[w, hT], writes=[p])
                    epilogue(i, tt, p, mw)

        def chunks(n0, n):
            out = []
            x = n0
            while x < n0 + n:
                w = min(128, n0 + n - x)
                out.append((x, w))
                x += w
            return out

        def phase_proj(l, hT):
            with ExitStack() as es:
                stage = [alloc(es, f"pst{i}", [128, T]) for i in range(2)]
                cols = chunks(0, c.P_IN)

                def epi(i, tt, p, mw):
                    st = stage[i % 2]
                    eng = "act" if (i * NT + tt) % 2 == 0 else "dve"
                    sl = slice(tt * TT, (tt + 1) * TT)
                    if eng == "act":
                        S.op("act", lambda e: e.activation(st[0:mw, sl], p[0:mw, :], AF.Copy), reads=[p], writes=[st])
                    else:
                        S.op("dve", lambda e: e.tensor_copy(st[0:mw, sl], p[0:mw, :]), reads=[p], writes=[st])
                    if tt == NT - 1:
                        c0 = cols[i][0]
                        S.dma("sp", projT[c0:c0 + mw, :], st[0:mw, :], reads=[st])
                        if c.DA_W <= c0 < 3 * c.DA_W:
                            S.dma("sp", kvT[l, c0 - c.DA_W:c0 - c.DA_W + mw, :], st[0:mw, :], reads=[st])

                dense(es, hT, DC, w_in[l], cols, epi, "pj")
                S.barrier()

        def phase_wout(l, mixT, src, dst):
            with ExitStack() as es:
                xo = [alloc(es, f"wxo{i}", [128, T]) for i in range(2)]
                xn = [alloc(es, f"wxn{i}", [128, T]) for i in range(2)]
                cols = chunks(0, D)

                def pre(i):
                    S.dma("sp", xo[i % 2][:], src[i * 128:(i + 1) * 128, :], writes=[xo[i % 2]])

                def epi(i, tt, p, mw):
                    sl = slice(tt * TT, (tt + 1) * TT)
                    S.op("dve", lambda e: e.scalar_tensor_tensor(
                        xn[i % 2][:, sl], p[:], modcol(l, G1, i), xo[i % 2][:, sl], ALU.mult, ALU.add),
                        reads=[p, modT[l], xo[i % 2]], writes=[xn[i % 2]])
                    if tt == NT - 1:
                        S.dma("sp", dst[i * 128:(i + 1) * 128, :], xn[i % 2][:], reads=[xn[i % 2]])

                dense(es, mixT, DC, w_out[l], cols, epi, "wo", pre=pre)
                S.barrier()

        def phase_ffn_up(l, hT):
            KF = c.KF
            with ExitStack() as es:
                mprev = alloc(es, "mprev", [128, T], BF16)
                mnext = alloc(es, "mnext", [128, T], BF16)
                cw = alloc(es, "fcw", [128, 2 * KF, 3])
                cb = alloc(es, "fcb", [128, 2 * KF])
                up = [alloc(es, f"fup{i}", [128, T]) for i in range(2)]
                o = [alloc(es, f"fo{i}", [128, T]) for i in range(2)]
                xm1 = alloc(es, "fxm", [128, T])
                xm = [xm1, xm1]
                ab = [alloc(es, f"fab{i}", [128, T], BF16) for i in range(2)]
                S.dma("pool", mprev[:], mprev_d, writes=[mprev])
                S.dma("pool", mnext[:], mnext_d, writes=[mnext])
                S.dma("sp", cw[:], ffn_cw[l], writes=[cw])
                S.dma("sp", cb[:], ffn_cb[l], writes=[cb])
                cols = []
                for m in range(KF):
                    cols.append((m * 128, 128))
                    cols.append((c.DFF + m * 128, 128))

                def epi(i, tt, p, mw):
                    br = i % 2
                    m = i // 2
                    cidx = m if br == 0 else KF + m
                    u = up[br]
                    sl = slice(tt * TT, (tt + 1) * TT)
                    S.op("act", lambda e: e.activation(u[:, sl], p[:], AF.Copy), reads=[p], writes=[u])
                    if tt < NT - 1:
                        return
                    ob = o[br]
                    S.op("pool", lambda e: e.tensor_scalar(ob[:], u[:], cw[:, cidx, 1:2], None, ALU.mult), reads=[u, cw], writes=[ob])
                    S.op("dve", lambda e: e.tensor_tensor(xm[0][:, 0:T - 1], u[:, 0:T - 1], mprev[:, 1:T], ALU.mult),
                         reads=[u, mprev], writes=[xm[0]])
                    S.op("dve", lambda e: e.scalar_tensor_tensor(ob[:, 1:T], xm[0][:, 0:T - 1], cw[:, cidx, 0:1], ob[:, 1:T], ALU.mult, ALU.add),
                         reads=[xm[0], cw, ob], writes=[ob])
                    S.op("pool", lambda e: e.tensor_tensor(xm[1][:, 0:T - 1], u[:, 1:T], mnext[:, 0:T - 1], ALU.mult),
                         reads=[u, mnext], writes=[xm[1]])
                    S.op("dve", lambda e: e.scalar_tensor_tensor(ob[:, 0:T - 1], xm[1][:, 0:T - 1], cw[:, cidx, 2:3], ob[:, 0:T - 1], ALU.mult, ALU.add),
                         reads=[xm[1], cw, ob], writes=[ob])
                    if br == 0:
                        S.op("act", lambda e: e.activation(ob[:], ob[:], AF.Silu, bias=cb[:, cidx:cidx + 1]), reads=[ob, cb], writes=[ob])
                    else:
                        a = ab[m % 2]
                        S.op("dve", lambda e: e.scalar_tensor_tensor(a[:], ob[:], cb[:, cidx:cidx + 1], o[0][:], ALU.add, ALU.mult),
                             reads=[ob, cb, o[0]], writes=[a])
                        S.dma("sp", actT[m * 128:(m + 1) * 128, :], a[:], reads=[a])

                dense(es, hT, DC, ffn_up[l], cols, epi, "fu", nw=2)
                S.barrier()

        def phase_ffn_down(l, src, dst):
            KF = c.KF
            with ExitStack() as es:
                at = alloc(es, "fdact", [128, KF, TT], BF16)
                wb = [alloc(es, f"fdw{i}", [128, KF, 128], BF16) for i in range(2)]
                xo = [alloc(es, f"fdxo{i}", [128, TT]) for i in range(3)]
                xn = [alloc(es, f"fdxn{i}", [128, TT]) for i in range(3)]
                ps = [palloc(es, f"fdp{i}", [128, TT]) for i in range(3)]
                k = 0
                for tt in range(NT):
                    sl = slice(tt * TT, (tt + 1) * TT)
                    S.dma("sp", at[:], actT[:, sl].rearrange("(c p) t -> p c t", p=128), writes=[at])
                    for m in range(DC):
                        w = wb[k % 2]
                        p = ps[k % 3]
                        x_o = xo[k % 3]
                        x_n = xn[k % 3]
                        k += 1
                        S.dma("pool", w[:], ffn_down[l][:, m * 128:(m + 1) * 128].rearrange("(c p) m -> p c m", p=128), writes=[w])
                        S.dma("sp", x_o[:], src[m * 128:(m + 1) * 128, sl], writes=[x_o])
                        for ch in range(KF):
                            S.op("pe", lambda e, p=p, w=w, ch=ch: e.matmul(p[:], w[:, ch, :], at[:, ch, :], start=(ch == 0), stop=(ch == KF - 1)),
                                 reads=[w, at], writes=[p])
                        S.op("dve", lambda e, p=p, x_o=x_o, x_n=x_n, m=m: e.scalar_tensor_tensor(
                            x_n[:], p[:], modcol(l, G2, m), x_o[:], ALU.mult, ALU.add), reads=[p, modT[l], x_o], writes=[x_n])
                        S.dma("sp", dst[m * 128:(m + 1) * 128, sl], x_n[:], reads=[x_n])
                S.barrier()


        def phase_attn(l):
            H, NB, NK, NKB = c.H, c.NB, c.NK, c.NKB
            lam_init = 0.8 - 0.6 * math.exp(-0.3 * l)
            scale = 128.0 ** -0.5
            with ExitStack() as es:
                ropeC = alloc(es, "ropeC", [128, T])
                ropeS = alloc(es, "ropeS", [128, T])
                qaug = alloc(es, "qaug", [8, T], BF16)
                kaug = alloc(es, "kaug", [8, NK], BF16)
                lamt = alloc(es, "lamt", [128, 512])
                lamp = alloc(es, "lamp", [128, 512])
                lr = alloc(es, "lr", [128, 4])
                nlam = alloc(es, "nlam", [128, 1])
                sg = alloc(es, "sg", [128, 2])
                identb = alloc(es, "identb", [128, 128], BF16)
                ld = [alloc(es, f"ald{i}", [128, TT]) for i in range(4)]
                qr = [[alloc(es, f"qr{m}{i}", [128, TT], BF16) for i in range(2)] for m in range(2)]
                kr = [alloc(es, f"kr{m}", [128, NK], BF16) for m in range(2)]
                vt = [alloc(es, f"vt{i}", [128, TT]) for i in range(2)]
                vtok = alloc(es, "vtok", [128, NKB, 256], BF16)
                pt = [alloc(es, f"pt{i}", [128, TT], BF16) for i in range(3)]
                ym = [alloc(es, f"ym{m}", [128, 2, TT]) for m in range(2)]
                yd = alloc(es, "yd", [128, 2, TT])
                sq = alloc(es, "asq", [128, 2, TT])
                rden = alloc(es, "rden", [128, TT])
                rs = alloc(es, "ars", [128, TT])
                tq = alloc(es, "atq", [128, TT])
                ob = [alloc(es, f"aob{i}", [128, TT]) for i in range(2)]
                po = [palloc(es, f"apo{i}", [128, TT]) for i in range(2)]
                pden = palloc(es, "apden", [128, TT])
                pss = [palloc(es, f"apss{i}", [128, TT]) for i in range(2)]
                pss2 = palloc(es, "apss2", [128, TT])
                pst = palloc(es, "apst", [128, 128])
                S.dma("sp", ropeC[:], ropeC_d, writes=[ropeC])
                S.dma("sp", ropeS[:], ropeS_d, writes=[ropeS])
                S.dma("pool", qaug[:], qaug_d, writes=[qaug])
                S.dma("pool", kaug[:], kaug_d, writes=[kaug])
                S.dma("sp", lamt[:], lam_d[l], writes=[lamt])
                S.dma("sp", sg[:], subg_d[l], writes=[sg])
                S.op("dve", lambda e: e.tensor_copy(identb[:], ident[:]), reads=[ident], writes=[identb])
                S.op("dve", lambda e: e.tensor_tensor(lamp[:, 0:128], lamt[:, 0:128], lamt[:, 128:256], ALU.mult), reads=[lamt], writes=[lamp])
                S.op("dve", lambda e: e.tensor_tensor(lamp[:, 128:256], lamt[:, 256:384], lamt[:, 384:512], ALU.mult), reads=[lamt], writes=[lamp])
                S.op("dve", lambda e: e.tensor_reduce(lr[:, 0:2], lamp[:, 0:256].rearrange("p (a b) -> p a b", a=2), AX.X, ALU.add), reads=[lamp], writes=[lr])
                S.op("act", lambda e: e.activation(lr[:, 2:4], lr[:, 0:2], AF.Exp), reads=[lr], writes=[lr])
                S.op("dve", lambda e: e.tensor_tensor(nlam[:], lr[:, 3:4], lr[:, 2:3], ALU.subtract), reads=[lr], writes=[nlam])
                S.op("dve", lambda e: e.tensor_scalar(nlam[:], nlam[:], -lam_init, None, ALU.add), reads=[nlam], writes=[nlam])
                S.op("dve", lambda e: e.tensor_scalar(sg[:], sg[:], 1.0 - lam_init, None, ALU.mult), reads=[sg], writes=[sg])

                def rope_tile(rows0, tt, dst_ap, dst_tile, k):
                    sl = slice(tt * TT, (tt + 1) * TT)
                    a = ld[(2 * k) % 4]
                    b = ld[(2 * k + 1) % 4]
                    S.dma("sp", a[:], projT[rows0:rows0 + 128, sl], writes=[a])
                    for blk in range(4):
                        pb = blk ^ 1
                        S.dma("sp", b[blk * 32:(blk + 1) * 32, :], projT[rows0 + pb * 32:rows0 + (pb + 1) * 32, sl], writes=[b])
                    S.op("dve", lambda e: e.tensor_tensor(a[:], a[:], ropeC[:, sl], ALU.mult), reads=[a, ropeC], writes=[a])
                    S.op("pool", lambda e: e.tensor_tensor(b[:], b[:], ropeS[:, sl], ALU.mult), reads=[b, ropeS], writes=[b])
                    S.op("dve", lambda e: e.tensor_tensor(dst_ap, a[:], b[:], ALU.add), reads=[a, b], writes=[dst_tile])

                kctr = 0
                for h in range(H):
                    for m in range(2):
                        for tt in range(NT):
                            rope_tile(c.DA_W + h * 256 + m * 128, tt, kr[m][:, tt * TT:(tt + 1) * TT], kr[m], kctr)
                            kctr += 1
                        S.dma("pool", kr[m][:, T:NK], kcT_d[l, h, m], writes=[kr[m]])
                    vk = 0
                    for e_ in range(2):
                        for tt in range(NT):
                            v = vt[vk % 2]
                            vk += 1
                            r0 = 2 * c.DA_W + h * 256 + e_ * 128
                            S.dma("sp", v[:], projT[r0:r0 + 128, tt * TT:(tt + 1) * TT], writes=[v])
                            for j in range(TT // 128):
                                kb = tt * (TT // 128) + j
                                S.op("pe", lambda e, v=v, j=j: e.transpose(pst[:], v[:, j * 128:(j + 1) * 128], ident[:]),
                                     reads=[v, ident], writes=[pst])
                                S.op("act", lambda e, kb=kb, e_=e_: e.activation(vtok[:, kb, e_ * 128:(e_ + 1) * 128], pst[:], AF.Copy),
                                     reads=[pst], writes=[vtok])
                    S.dma("pool", vtok[:, NB:NKB, :], vc_d[l, h].rearrange("(j p) e -> p j e", p=128), writes=[vtok])
                    for tt in range(NT):
                        sl = slice(tt * TT, (tt + 1) * TT)
                        for m in range(2):
                            q = qr[m][tt % 2]
                            rope_tile(h * 256 + m * 128, tt, q[:], q, kctr)
                            kctr += 1
                            for kb in range(NKB):
                                ps_ = pss[kb % 2]
                                ksl = slice(kb * 128, (kb + 1) * 128)
                                S.op("pe", lambda e, ps_=ps_, ksl=ksl, q=q, m=m: e.matmul(ps_[:], kr[m][:, ksl], q[:], start=True, stop=False),
                                     reads=[kr[m], q], writes=[ps_])
                                S.op("pe", lambda e, ps_=ps_, ksl=ksl, sl=sl: e.matmul(ps_[:], kaug[:, ksl], qaug[:, sl], start=False, stop=True),
                                     reads=[kaug, qaug], writes=[ps_])
                                p_ = pt[kb % 3]
                                S.op("act", lambda e, p_=p_, ps_=ps_: e.activation(p_[:], ps_[:], AF.Exp, bias=-scale * MASK_BIG, scale=scale),
                                     reads=[ps_], writes=[p_])
                                st, sp_ = (kb == 0), (kb == NKB - 1)
                                S.op("pe", lambda e, p_=p_, kb=kb, st=st, sp_=sp_: e.matmul(po[0][:], vtok[:, kb, 0:128], p_[:], start=st, stop=sp_),
                                     reads=[vtok, p_], writes=[po[0]])
                                S.op("pe", lambda e, p_=p_, kb=kb, st=st, sp_=sp_: e.matmul(po[1][:], vtok[:, kb, 128:256], p_[:], start=st, stop=sp_),
                                     reads=[vtok, p_], writes=[po[1]])
                                S.op("pe", lambda e, p_=p_, st=st, sp_=sp_: e.matmul(pden[:], ones_b[:], p_[:], start=st, stop=sp_),
                                     reads=[ones_b, p_], writes=[pden])
                            S.op("dve", lambda e: e.reciprocal(rden[:], pden[:]), reads=[pden], writes=[rden])
                            for e_ in range(2):
                                S.op("dve", lambda e, e_=e_, m=m: e.tensor_tensor(ym[m][:, e_, :], po[e_][:], rden[:], ALU.mult),
                                     reads=[po[e_], rden], writes=[ym[m]])
                        S.op("dve", lambda e: e.scalar_tensor_tensor(yd[:], ym[1][:], nlam[:, 0:1], ym[0][:], ALU.mult, ALU.add),
                             reads=[ym[0], ym[1], nlam], writes=[yd])
                        S.op("act", lambda e: e.activation(sq[:], yd[:], AF.Square), reads=[yd], writes=[sq])
                        for e_ in range(2):
                            S.op("pe", lambda e, e_=e_: e.matmul(pss2[:], ones_f[:], sq[:, e_, :], start=(e_ == 0), stop=(e_ == 1)),
                                 reads=[ones_f, sq], writes=[pss2])
                        S.op("act", lambda e: e.activation(rs[:], pss2[:], AF.Sqrt, bias=NORM_EPS, scale=1.0 / 256), reads=[pss2], writes=[rs])
                        S.op("dve", lambda e: e.reciprocal(rs[:], rs[:]), reads=[rs], writes=[rs])
                        for e_ in range(2):
                            o_ = ob[e_]
                            S.op("dve", lambda e, e_=e_: e.tensor_tensor(tq[:], yd[:, e_, :], rs[:], ALU.mult), reads=[yd, rs], writes=[tq])
                            S.op("act", lambda e, e_=e_, o_=o_: e.activation(o_[:], tq[:], AF.Identity, bias=0.0, scale=sg[:, e_:e_ + 1]),
                                 reads=[tq, sg], writes=[o_])
                            r0 = h * 256 + e_ * 128
                            S.dma("sp", mixT_d[r0:r0 + 128, sl], o_[:], reads=[o_])
                S.barrier()

        def phase_cm(l):
            NB = c.NB
            U0 = c.UV0
            V0 = c.UV0 + c.CM_W
            M0 = c.DA_W + c.RW_W
            CG = c.CM_W // 4
            cbw = min(128, CG)
            with ExitStack() as es:
                rstd = alloc(es, "crstd", [128, T])
                cmg = alloc(es, "cmg", [128, CMC])
                identb = alloc(es, "cidentb", [128, 128], BF16)
                wsb = alloc(es, "wsb", [128, 4, 128], BF16)
                bsr = alloc(es, "bsr", [128, 4, 128])
                xt = [alloc(es, f"cx{i}", [128, TT]) for i in range(3)]
                sq = [alloc(es, f"csq{i}", [128, TT]) for i in range(2)]
                vfull = [alloc(es, f"cv{i}", [128, T]) for i in range(2)]
                zb = [alloc(es, f"czb{i}", [128, T], BF16) for i in range(2)]
                ztok = alloc(es, "ztok", [128, NB, c.CM_W], BF16)
                ut = [alloc(es, f"cu{i}", [128, T]) for i in range(2)]
                ot = [alloc(es, f"co{i}", [128, T]) for i in range(2)]
                ps = [palloc(es, f"cps{i}", [128, TT]) for i in range(2)]
                pstb = [palloc(es, f"cpst{i}", [128, 128], BF16) for i in range(2)]
                pm = [palloc(es, f"cpm{i}", [128, 512]) for i in range(2)]
                S.dma("sp", cmg[:], cmg_d[l], writes=[cmg])
                S.dma("pool", wsb[:], wsT_d[l].rearrange("g q p -> q g p"), writes=[wsb])
                S.dma("sp", bsr[:], bsr_d[l].rearrange("g q p -> q g p"), writes=[bsr])
                S.op("dve", lambda e: e.tensor_copy(identb[:], ident[:]), reads=[ident], writes=[identb])
                k = 0
                for tt in range(NT):
                    p = ps[tt % 2]
                    for ch in range(CMC):
                        x = xt[k % 3]
                        q = sq[k % 2]
                        k += 1
                        S.dma("sp", x[:], projT[V0 + ch * 128:V0 + (ch + 1) * 128, tt * TT:(tt + 1) * TT], writes=[x])
                        S.op("act", lambda e, q=q, x=x: e.activation(q[:], x[:], AF.Square), reads=[x], writes=[q])
                        S.op("pe", lambda e, p=p, q=q, ch=ch: e.matmul(p[:], ones_f[:], q[:], start=(ch == 0), stop=(ch == CMC - 1)),
                             reads=[ones_f, q], writes=[p])
                    sl = slice(tt * TT, (tt + 1) * TT)
                    S.op("act", lambda e, p=p, sl=sl: e.activation(rstd[:, sl], p[:], AF.Sqrt, bias=NORM_EPS, scale=1.0 / c.CM_W),
                         reads=[p], writes=[rstd])
                    S.op("dve", lambda e, sl=sl: e.reciprocal(rstd[:, sl], rstd[:, sl]), reads=[rstd], writes=[rstd])
                k = 0
                for ch in range(CMC):
                    v = vfull[ch % 2]
                    z = zb[ch % 2]
                    S.dma("sp", v[:], projT[V0 + ch * 128:V0 + (ch + 1) * 128, :], writes=[v])
                    S.op("dve", lambda e, v=v: e.tensor_tensor(v[:], v[:], rstd[:], ALU.mult), reads=[v, rstd], writes=[v])
                    S.op("act", lambda e, v=v, z=z, ch=ch: e.activation(z[:], v[:], AF.Identity, bias=0.0, scale=cmg[:, ch:ch + 1]),
                         reads=[v, cmg], writes=[z])
                    for n in range(NB):
                        pb = pstb[k % 2]
                        k += 1
                        S.op("pe", lambda e, pb=pb, z=z, n=n: e.transpose(pb[:], z[:, n * 128:(n + 1) * 128], identb[:]),
                             reads=[z, identb], writes=[pb])
                        if k % 2 == 0:
                            S.op("dve", lambda e, pb=pb, n=n, ch=ch: e.tensor_copy(ztok[:, n, ch * 128:(ch + 1) * 128], pb[:]),
                                 reads=[pb], writes=[ztok])
                        else:
                            S.op("act", lambda e, pb=pb, n=n, ch=ch: e.activation(ztok[:, n, ch * 128:(ch + 1) * 128], pb[:], AF.Copy),
                                 reads=[pb], writes=[ztok])
                bi = 0
                k = 0
                for g in range(4):
                    for sub in range(CG // cbw):
                        cb0 = g * CG + sub * cbw
                        u = ut[bi % 2]
                        o = ot[bi % 2]
                        bi += 1
                        S.dma("sp", u[0:cbw, :], projT[U0 + cb0:U0 + cb0 + cbw, :], writes=[u])
                        for n0 in range(0, NB, 4):
                            p = pm[k % 2]
                            k += 1
                            nn = min(4, NB - n0)
                            for j in range(nn):
                                n = n0 + j
                                S.op("pe", lambda e, p=p, j=j, n=n, cb0=cb0, g=g: e.matmul(
                                    p[0:cbw, j * 128:(j + 1) * 128], ztok[:, n, cb0:cb0 + cbw], wsb[:, g, :], start=True, stop=True),
                                    reads=[ztok, wsb], writes=[p])
                            for j in range(nn):
                                n = n0 + j
                                S.op("dve", lambda e, p=p, j=j, n=n, o=o, g=g: e.tensor_tensor(
                                    o[0:cbw, n * 128:(n + 1) * 128], p[0:cbw, j * 128:(j + 1) * 128], bsr[0:cbw, g, :], ALU.add),
                                    reads=[p, bsr], writes=[o])
                        S.op("pool", lambda e, o=o, u=u: e.tensor_tensor(o[0:cbw, :], o[0:cbw, :], u[0:cbw, :], ALU.mult), reads=[o, u], writes=[o])
                        S.dma("sp", mixT_d[M0 + cb0:M0 + cb0 + cbw, :], o[0:cbw, :], reads=[o])
                S.barrier()


        def phase_rw_pre(l):
            R0 = c.RW0
            with ExitStack() as es:
                mprev = alloc(es, "rmprev", [128, T], BF16)
                mnext = alloc(es, "rmnext", [128, T], BF16)
                cmask = alloc(es, "rcmask", [128, T])
                cw = alloc(es, "rcw", [128, NRC, 3])
                w0 = alloc(es, "rw0", [128, 2, RG])
                a0 = alloc(es, "ra0", [128, 2, RG])
                prm = alloc(es, "rprm", [128, 5, RG])
                bones = alloc(es, "rbones", [128, 128])
                w2 = alloc(es, "rw2", [128, RW_W])
                a2 = alloc(es, "ra2", [128, RW_W])
                g2a = alloc(es, "rg2a", [128, RW_W])
                g2b = alloc(es, "rg2b", [32, RW_W])
                raw = [alloc(es, f"rraw{i}", [128, T]) for i in range(2)]
                xm = alloc(es, "rxm", [128, T])
                tdec = alloc(es, "rtdec", [128, T])
                aac = alloc(es, "raac", [128, T])
                sgl = alloc(es, "rsgl", [128, T])
                sgl2 = alloc(es, "rsgl2", [32, T])
                rt = alloc(es, "rrt", [128, T])
                kt = alloc(es, "rkt", [128, T])
                vt_ = alloc(es, "rvt", [128, T])
                kk = alloc(es, "rkk", [128, T])
                t1 = alloc(es, "rt1", [128, T])
                t2 = alloc(es, "rt2", [128, T])
                t3 = alloc(es, "rt3", [128, T])
                lw = alloc(es, "rlw", [128, T])
                av = alloc(es, "rav", [128, T])
                kdsum = alloc(es, "rkds", [128, T])
                pend = alloc(es, "rpend", [128, NCH])
                ltc = alloc(es, "rltc", [128, NCH, 1])
                ps = [palloc(es, f"rps{i}", [128, TT]) for i in range(4)]
                S.dma("pool", mprev[:], mprev_d, writes=[mprev])
                S.dma("pool", mnext[:], mnext_d, writes=[mnext])
                S.dma("sp", cmask[:], cmask_d, writes=[cmask])
                S.dma("sp", cw[:], rw_cw_d[l], writes=[cw])
                S.dma("sp", w0[:], w0T_d[l], writes=[w0])
                S.dma("sp", a0[:], a0T_d[l], writes=[a0])
                S.dma("sp", prm[:], rwp_d[l], writes=[prm])
                cka = alloc(es, "rcka", [128, RG])
                S.op("dve", lambda e: e.tensor_scalar(cka[:], prm[:, 1, :], -1.0, None, ALU.mult), reads=[prm], writes=[cka])
                S.op("dve", lambda e: e.tensor_scalar(cka[:], cka[:], 1.0, None, ALU.add), reads=[cka], writes=[cka])
                S.dma("sp", bones[:], bones_d, writes=[bones])
                S.dma("sp", w2[:], w2_d[l].rearrange("d r c -> (d r) c"), writes=[w2])
                S.dma("sp", a2[:], a2_d[l].rearrange("d r c -> (d r) c"), writes=[a2])
                S.dma("sp", g2a[:], g2_d[l, 0:128, :], writes=[g2a])
                S.dma("sp", g2b[:], g2_d[l, 128:160, :], writes=[g2b])
                pk = [0]
                rk = [0]

                def conv_rows(row0, nrows, cidx, dst):
                    u = raw[rk[0] % 2]
                    rk[0] += 1
                    n = nrows
                    S.dma("sp", u[0:n, :], projT[row0:row0 + n, :], writes=[u])
                    S.op("pool", lambda e: e.tensor_scalar(dst[0:n, :], u[0:n, :], cw[0:n, cidx, 1:2], None, ALU.mult), reads=[u, cw], writes=[dst])
                    S.op("dve", lambda e: e.tensor_tensor(xm[0:n, 0:T - 1], u[0:n, 0:T - 1], mprev[0:n, 1:T], ALU.mult), reads=[u, mprev], writes=[xm])
                    S.op("dve", lambda e: e.scalar_tensor_tensor(dst[0:n, 1:T], xm[0:n, 0:T - 1], cw[0:n, cidx, 0:1], dst[0:n, 1:T], ALU.mult, ALU.add),
                         reads=[xm, cw, dst], writes=[dst])
                    S.op("dve", lambda e: e.tensor_tensor(xm[0:n, 0:T - 1], u[0:n, 1:T], mnext[0:n, 0:T - 1], ALU.mult), reads=[u, mnext], writes=[xm])
                    S.op("dve", lambda e: e.scalar_tensor_tensor(dst[0:n, 0:T - 1], xm[0:n, 0:T - 1], cw[0:n, cidx, 2:3], dst[0:n, 0:T - 1], ALU.mult, ALU.add),
                         reads=[xm, cw, dst], writes=[dst])

                cdec = 3 * RG
                conv_rows(R0 + 3 * RW_W, 128, cdec, tdec)
                S.op("act", lambda e: e.activation(tdec[:], tdec[:], AF.Tanh), reads=[tdec], writes=[tdec])
                conv_rows(R0 + 3 * RW_W + 128, 128, cdec + 1, aac)
                conv_rows(R0 + 3 * RW_W + 256, 128, cdec + 2, sgl)
                S.op("act", lambda e: e.activation(sgl[:], sgl[:], AF.Sigmoid), reads=[sgl], writes=[sgl])
                conv_rows(R0 + 3 * RW_W + 384, 32, cdec + 3, sgl2)
                S.op("act", lambda e: e.activation(sgl2[:], sgl2[:], AF.Sigmoid), reads=[sgl2], writes=[sgl2])

                def headsum(dst, src):
                    for tt in range(NT):
                        p = ps[pk[0] % 4]
                        pk[0] += 1
                        sl = slice(tt * TT, (tt + 1) * TT)
                        S.op("pe", lambda e, p=p, sl=sl: e.matmul(p[:], bones[:], src[:, sl], start=True, stop=True), reads=[bones, src], writes=[p])
                        S.op("act", lambda e, p=p, sl=sl: e.activation(dst[:, sl], p[:], AF.Copy), reads=[p], writes=[dst])

                for g in range(RG):
                    gs = slice(g * 128, (g + 1) * 128)
                    conv_rows(R0 + g * 128, 128, g, rt)
                    conv_rows(R0 + RW_W + g * 128, 128, RG + g, kt)
                    conv_rows(R0 + 2 * RW_W + g * 128, 128, 2 * RG + g, vt_)
                    S.dma("sp", rwV[gs, :], vt_[:], reads=[vt_])
                    for tt in range(NT):
                        p = ps[pk[0] % 4]
                        pk[0] += 1
                        sl = slice(tt * TT, (tt + 1) * TT)
                        S.op("pe", lambda e, p=p, sl=sl: e.matmul(p[:], g2a[:, gs], sgl[:, sl], start=True, stop=False), reads=[g2a, sgl], writes=[p])
                        S.op("pe", lambda e, p=p, sl=sl: e.matmul(p[:], g2b[:, gs], sgl2[:, sl], start=False, stop=True), reads=[g2b, sgl2], writes=[p])
                        S.op("act", lambda e, p=p, sl=sl: e.activation(t1[:, sl], p[:], AF.Copy), reads=[p], writes=[t1])
                    S.dma("sp", rwGate[gs, :], t1[:], reads=[t1])
                    S.op("pool", lambda e: e.tensor_scalar(kk[:], kt[:], prm[:, 0, g:g + 1], None, ALU.mult), reads=[kt, prm], writes=[kk])
                    S.op("act", lambda e: e.activation(t2[:], kk[:], AF.Square), reads=[kk], writes=[t2])
                    headsum(t3, t2)
                    S.op("act", lambda e: e.activation(t3[:], t3[:], AF.Sqrt, bias=1e-12, scale=1.0), reads=[t3], writes=[t3])
                    S.op("dve", lambda e: e.reciprocal(t3[:], t3[:]), reads=[t3], writes=[t3])
                    S.op("dve", lambda e: e.tensor_tensor(kk[:], kk[:], t3[:], ALU.mult), reads=[kk, t3], writes=[kk])
                    for d in range(2):
                        ds_ = slice(d * 64, (d + 1) * 64)
                        for tt in range(NT):
                            sl = slice(tt * TT, (tt + 1) * TT)
                            p = ps[pk[0] % 4]
                            pk[0] += 1
                            S.op("pe", lambda e, p=p, sl=sl: e.matmul(p[:], w2[ds_, gs], tdec[ds_, sl], start=True, stop=True), reads=[w2, tdec], writes=[p])
                            S.op("act", lambda e, p=p, sl=sl: e.activation(lw[:, sl], p[:], AF.Sigmoid, bias=w0[:, d, g:g + 1], scale=1.0),
                                 reads=[p, w0], writes=[lw])
                            p = ps[pk[0] % 4]
                            pk[0] += 1
                            S.op("pe", lambda e, p=p, sl=sl: e.matmul(p[:], a2[ds_, gs], aac[ds_, sl], start=True, stop=True), reads=[a2, aac], writes=[p])
                            S.op("act", lambda e, p=p, sl=sl: e.activation(av[:, sl], p[:], AF.Sigmoid, bias=a0[:, d, g:g + 1], scale=1.0),
                                 reads=[p, a0], writes=[av])
                        S.op("pool", lambda e: e.tensor_scalar(lw[:], lw[:], -math.exp(-0.5), None, ALU.mult), reads=[lw], writes=[lw])
                        S.op("act", lambda e: e.activation(t1[:], av[:], AF.Identity, bias=cka[:, g:g + 1], scale=prm[:, 1, g:g + 1]),
                             reads=[av, prm, cka], writes=[t1])
                        S.op("dve", lambda e: e.tensor_tensor(t1[:], t1[:], kt[:], ALU.mult), reads=[t1, kt], writes=[t1])
                        if d == 0:
                            S.op("pool", lambda e: e.tensor_copy(kdsum[:], t1[:]), reads=[t1], writes=[kdsum])
                        else:
                            S.op("pool", lambda e: e.tensor_tensor(kdsum[:], kdsum[:], t1[:], ALU.add), reads=[t1, kdsum], writes=[kdsum])
                        S.op("dve", lambda e: e.tensor_tensor_scan(t2[:], cmask[:], lw[:], 0.0, ALU.mult, ALU.add), reads=[cmask, lw], writes=[t2])
                        S.op("dve", lambda e: e.tensor_copy(ltc[:], t2[:].rearrange("p (n c) -> p n c", c=CK)[:, :, CK - 1:CK]), reads=[t2], writes=[ltc])
                        ltot = ltc[:]
                        S.op("act", lambda e: e.activation(pend[:].rearrange("p (n o) -> p n o", o=1), ltot, AF.Exp), reads=[ltc], writes=[pend])
                        S.dma("sp", rwP[d][gs, :], pend[:], reads=[pend])
                        if d == 1:
                            S.op("dve", lambda e: e.tensor_tensor(t3[:], lw[:], t2[:], ALU.subtract), reads=[lw, t2], writes=[t3])
                            S.op("dve", lambda e: e.tensor_tensor(
                                t2[:].rearrange("p (n c) -> p n c", c=CK), t3[:].rearrange("p (n c) -> p n c", c=CK),
                                ltot.to_broadcast([128, NCH, CK]), ALU.add), reads=[t3, ltc], writes=[t2])
                        S.op("act", lambda e: e.activation(t3[:], t2[:], AF.Exp), reads=[t2], writes=[t3])
                        S.op("dve", lambda e: e.tensor_tensor(t3[:], t3[:], rt[:], ALU.mult), reads=[t3, rt], writes=[t3])
                        S.dma("sp", rwR[d][gs, :], t3[:], reads=[t3])
                        S.op("act", lambda e: e.activation(t3[:], t2[:], AF.Exp, scale=-1.0), reads=[t2], writes=[t3])
                        S.op("dve", lambda e: e.tensor_tensor(t1[:], t1[:], t3[:], ALU.mult), reads=[t3, t1], writes=[t1])
                        S.dma("sp", rwK[d][gs, :], t1[:], reads=[t1])
                        S.op("dve", lambda e: e.tensor_tensor(t3[:], t3[:], kk[:], ALU.mult), reads=[t3, kk], writes=[t3])
                        S.op("dve", lambda e: e.tensor_tensor(t3[:], t3[:], av[:], ALU.mult), reads=[t3, av], writes=[t3])
                        S.dma("sp", rwB[d][gs, :], t3[:], reads=[t3])
                        S.op("dve", lambda e: e.tensor_tensor(t2[:], t2[:], lw[:], ALU.subtract), reads=[t2, lw], writes=[t2])
                        S.op("act", lambda e: e.activation(t2[:], t2[:], AF.Exp), reads=[t2], writes=[t2])
                        S.op("dve", lambda e: e.tensor_tensor(t2[:], t2[:], kk[:], ALU.mult), reads=[t2, kk], writes=[t2])
                        S.dma("sp", rwA[d][gs, :], t2[:], reads=[t2])
                    S.op("dve", lambda e: e.scalar_tensor_tensor(kdsum[:], kdsum[:], prm[:, 2, g:g + 1], rt[:], ALU.mult, ALU.mult),
                         reads=[kdsum, prm, rt], writes=[kdsum])
                    headsum(t1, kdsum)
                    S.op("dve", lambda e: e.tensor_tensor(t1[:], t1[:], vt_[:], ALU.mult), reads=[t1, vt_], writes=[t1])
                    S.dma("sp", rwBon[gs, :], t1[:], reads=[t1])
                S.barrier()

        def phase_rw_scan(l):
            HB = min(8, RH)
            NU = RH // HB
            with ExitStack() as es:
                tri = alloc(es, "tri", [64, 4, 64])
                keep = alloc(es, "keepc", [128, 2, NCH])
                S.dma("sp", tri[:], tri_d, writes=[tri])
                S.dma("sp", keep[:], keep_d, writes=[keep])
                SU, U_, SL, LW = 0, 1, 2, 3
                mk = {0: dict(N=SU, NT=SL, G=SU, H=U_), 1: dict(N=SL, NT=SU, G=SL, H=LW)}
                D_ = {}
                for d in range(2):
                    t = {}
                    for nm in ("A", "B", "K", "R", "V", "Kt", "Bt", "Vt", "N", "NT", "G", "Hk", "Hb", "X", "Pa", "PaT", "Pb", "PbT",
                               "XT", "nS", "Y", "ST"):
                        t[nm] = alloc(es, f"s{nm}{d}", [64, RH, 64])
                    t["P"] = alloc(es, f"sP{d}", [64, RH, NCH])
                    D_[d] = t
                    S.dma("sp", t["ST"][:], s0T_d[l, d].rearrange("g (j k) v -> k (g j) v", j=2), writes=[t["ST"]])
                    S.dma("sp", t["P"][:], rwP[d].rearrange("(h k) n -> k h n", k=64), writes=[t["P"]])
                psT = palloc(es, "spsT", [64, 512])
                ps1 = [palloc(es, f"sps1{i}", [64, 512]) for i in range(2)]
                ps2 = [palloc(es, f"sps2{i}", [64, 512]) for i in range(2)]
                ps3 = [palloc(es, f"sps3{i}", [64, 512]) for i in range(3)]
                ctr = dict(p1=0, p2=0, p3=0, ev=0)
                I64 = ident[0:64, 0:64]

                def flat(ap):
                    return ap.rearrange("p h c -> p (h c)")

                def evac(dst_ap, dst_t, p, cols):
                    ctr["ev"] += 1
                    if ctr["ev"] % 2 == 0:
                        S.op("act", lambda e: e.activation(dst_ap, p[0:64, 0:cols], AF.Copy), reads=[p], writes=[dst_t])
                    else:
                        S.op("dve", lambda e: e.tensor_copy(dst_ap, p[0:64, 0:cols]), reads=[p], writes=[dst_t])

                def hview(tile_, u):
                    return tile_[:, u * HB:(u + 1) * HB, :]

                def pview(p):
                    return p[0:64, 0:HB * 64].rearrange("p (h c) -> p h c", c=64)

                def load_chunk(d, n):
                    t = D_[d]
                    cs = slice(n * CK, (n + 1) * CK)
                    for nm, src in (("A", rwA[d]), ("B", rwB[d]), ("K", rwK[d]), ("R", rwR[d]), ("V", rwV)):
                        S.dma("sp", t[nm][:], src[:, cs].rearrange("(h k) t -> k h t", k=64), writes=[t[nm]])

                order = {0: list(range(NCH)), 1: list(range(NCH - 1, -1, -1))}
                for step in range(NCH):
                    for d in range(2):
                        t = D_[d]
                        n = order[d][step]
                        load_chunk(d, n)
                        A, B, K_, R, V = (t[x] for x in ("A", "B", "K", "R", "V"))
                        m = mk[d]
                        S.op("dve", lambda e, t=t, n=n, d=d: e.tensor_scalar(flat(t["ST"][:]), flat(t["ST"][:]), keep[0:64, d, n:n + 1], None, ALU.mult),
                             reads=[t["ST"], keep], writes=[t["ST"]])
                        for u in range(NU):
                            hs = [u * HB + i for i in range(HB)]
                            for nm_src, nm_dst in ((K_, "Kt"), (B, "Bt"), (V, "Vt")):
                                for i, hd in enumerate(hs):
                                    S.op("pe", lambda e, i=i, hd=hd, nm_src=nm_src: e.transpose(
                                        psT[0:64, i * 64:(i + 1) * 64], nm_src[:, hd, :], I64), reads=[nm_src, ident], writes=[psT])
                                evac(flat(hview(t[nm_dst], u)), t[nm_dst], psT, HB * 64)
                            for (dst, lt, rt_, mask) in (("N", B, A, m["N"]), ("NT", A, B, m["NT"]), ("G", K_, A, m["G"]),
                                                          ("Hk", K_, R, m["H"]), ("Hb", B, R, m["H"])):
                                p = ps1[ctr["p1"] % 2]
                                ctr["p1"] += 1
                                for i, hd in enumerate(hs):
                                    S.op("pe", lambda e, p=p, i=i, hd=hd, lt=lt, rt_=rt_: e.matmul(
                                        p[0:64, i * 64:(i + 1) * 64], lt[:, hd, :], rt_[:, hd, :], start=True, stop=True),
                                        reads=[lt, rt_], writes=[p])
                                S.op("dve", lambda e, p=p, dst=dst, mask=mask, t=t, u=u: e.tensor_tensor(
                                    hview(t[dst], u), pview(p), tri[:, mask:mask + 1, :].to_broadcast([64, HB, 64]), ALU.mult),
                                    reads=[p, tri], writes=[t[dst]])
                            S.op("dve", lambda e, t=t, u=u: e.scalar_tensor_tensor(
                                hview(t["X"], u), hview(t["N"], u), -1.0, I64.unsqueeze(1).to_broadcast([64, HB, 64]), ALU.mult, ALU.add),
                                reads=[t["N"], ident], writes=[t["X"]])
                            cur, curT = "N", "NT"
                            nlev = 5
                            for lev in range(nlev):
                                nxt, nxtT = ("Pa", "PaT") if lev % 2 == 0 else ("Pb", "PbT")
                                p = ps2[ctr["p2"] % 2]
                                ctr["p2"] += 1
                                for i, hd in enumerate(hs):
                                    S.op("pe", lambda e, p=p, i=i, hd=hd, t=t, cur=cur, curT=curT: e.matmul(
                                        p[0:64, i * 64:(i + 1) * 64], t[cur][:, hd, :], t[curT][:, hd, :], start=True, stop=True),
                                        reads=[t[cur], t[curT]], writes=[p])
                                evac(flat(hview(t[nxtT], u)), t[nxtT], p, HB * 64)
                                if lev < nlev - 1:
                                    p = ps2[ctr["p2"] % 2]
                                    ctr["p2"] += 1
                                    for i, hd in enumerate(hs):
                                        S.op("pe", lambda e, p=p, i=i, hd=hd, t=t, cur=cur, curT=curT: e.matmul(
                                            p[0:64, i * 64:(i + 1) * 64], t[curT][:, hd, :], t[cur][:, hd, :], start=True, stop=True),
                                            reads=[t[cur], t[curT]], writes=[p])
                                    evac(flat(hview(t[nxt], u)), t[nxt], p, HB * 64)
                                p = ps2[ctr["p2"] % 2]
                                ctr["p2"] += 1
                                for i, hd in enumerate(hs):
                                    S.op("pe", lambda e, p=p, i=i, hd=hd, t=t, nxtT=nxtT: e.matmul(
                                        p[0:64, i * 64:(i + 1) * 64], t[nxtT][:, hd, :], t["X"][:, hd, :], start=True, stop=True),
                                        reads=[t[nxtT], t["X"]], writes=[p])
                                S.op("dve", lambda e, p=p, t=t, u=u: e.tensor_tensor(
                                    hview(t["X"], u), hview(t["X"], u), pview(p), ALU.add), reads=[p, t["X"]], writes=[t["X"]])
                                cur, curT = nxt, nxtT
                            p = ps3[ctr["p3"] % 3]
                            ctr["p3"] += 1
                            for i, hd in enumerate(hs):
                                S.op("pe", lambda e, p=p, i=i, hd=hd, t=t: e.matmul(
                                    p[0:64, i * 64:(i + 1) * 64], A[:, hd, :], t["ST"][:, hd, :], start=True, stop=False),
                                    reads=[A, t["ST"]], writes=[p])
                                S.op("pe", lambda e, p=p, i=i, hd=hd, t=t: e.matmul(
                                    p[0:64, i * 64:(i + 1) * 64], t["G"][:, hd, :], t["Vt"][:, hd, :], start=False, stop=True),
                                    reads=[t["G"], t["Vt"]], writes=[p])
                            evac(flat(hview(t["XT"], u)), t["XT"], p, HB * 64)
                            p = ps3[ctr["p3"] % 3]
                            ctr["p3"] += 1
                            for i, hd in enumerate(hs):
                                S.op("pe", lambda e, p=p, i=i, hd=hd, t=t: e.matmul(
                                    p[0:64, i * 64:(i + 1) * 64], t["X"][:, hd, :], t["XT"][:, hd, :], start=True, stop=True),
                                    reads=[t["X"], t["XT"]], writes=[p])
                            S.op("dve", lambda e, p=p, t=t, u=u: e.tensor_scalar(
                                flat(hview(t["nS"], u)), p[0:64, 0:HB * 64], -1.0, None, ALU.mult), reads=[p], writes=[t["nS"]])
                            p = ps3[ctr["p3"] % 3]
                            ctr["p3"] += 1
                            for i, hd in enumerate(hs):
                                S.op("pe", lambda e, p=p, i=i, hd=hd, t=t: e.matmul(
                                    p[0:64, i * 64:(i + 1) * 64], t["ST"][:, hd, :], R[:, hd, :], start=True, stop=False),
                                    reads=[t["ST"], R], writes=[p])
                                S.op("pe", lambda e, p=p, i=i, hd=hd, t=t: e.matmul(
                                    p[0:64, i * 64:(i + 1) * 64], t["Vt"][:, hd, :], t["Hk"][:, hd, :], start=False, stop=False),
                                    reads=[t["Vt"], t["Hk"]], writes=[p])
                                S.op("pe", lambda e, p=p, i=i, hd=hd, t=t: e.matmul(
                                    p[0:64, i * 64:(i + 1) * 64], t["nS"][:, hd, :], t["Hb"][:, hd, :], start=False, stop=True),
                                    reads=[t["nS"], t["Hb"]], writes=[p])
                            evac(flat(hview(t["Y"], u)), t["Y"], p, HB * 64)
                            p = ps3[ctr["p3"] % 3]
                            ctr["p3"] += 1
                            for i, hd in enumerate(hs):
                                S.op("pe", lambda e, p=p, i=i, hd=hd, t=t: e.matmul(
                                    p[0:64, i * 64:(i + 1) * 64], t["Kt"][:, hd, :], t["Vt"][:, hd, :], start=True, stop=False),
                                    reads=[t["Kt"], t["Vt"]], writes=[p])
                                S.op("pe", lambda e, p=p, i=i, hd=hd, t=t: e.matmul(
                                    p[0:64, i * 64:(i + 1) * 64], t["Bt"][:, hd, :], t["nS"][:, hd, :], start=False, stop=True),
                                    reads=[t["Bt"], t["nS"]], writes=[p])
                            S.op("dve", lambda e, p=p, t=t, u=u: e.tensor_tensor(hview(t["ST"], u), hview(t["ST"], u), pview(p), ALU.add),
                                 reads=[p, t["ST"]], writes=[t["ST"]])
                            S.op("dve", lambda e, t=t, u=u, n=n: e.tensor_tensor(
                                hview(t["ST"], u), hview(t["ST"], u), t["P"][:, u * HB:(u + 1) * HB, n:n + 1].to_broadcast([64, HB, 64]), ALU.mult),
                                reads=[t["P"], t["ST"]], writes=[t["ST"]])
                        cs = slice(n * CK, (n + 1) * CK)
                        S.dma("sp", rwY[d][:, cs].rearrange("(h v) t -> v h t", v=64), t["Y"][:], reads=[t["Y"]])
                        tpos = (n + 1) * CK if d == 0 else n * CK
                        if tpos % c.SEQ == 0:
                            seg = tpos // c.SEQ - 1 if d == 0 else tpos // c.SEQ
                            S.dma("sp", statesT[seg, l, d].rearrange("g (j k) v -> k (g j) v", j=2), t["ST"][:], reads=[t["ST"]])
                S.barrier()

        def phase_rw_post(l):
            with ExitStack() as es:
                prm = alloc(es, "qprm", [128, 5, RG])
                bones = alloc(es, "qbones", [128, 128])
                y0 = [alloc(es, f"qy0{i}", [128, T]) for i in range(2)]
                y1 = [alloc(es, f"qy1{i}", [128, T]) for i in range(2)]
                bon = [alloc(es, f"qbon{i}", [128, T]) for i in range(2)]
                gat = [alloc(es, f"qgat{i}", [128, T]) for i in range(2)]
                mu = alloc(es, "qmu", [128, T])
                sq = alloc(es, "qsq", [128, T])
                ps = [palloc(es, f"qps{i}", [128, TT]) for i in range(4)]
                S.dma("sp", prm[:], rwp_d[l], writes=[prm])
                S.dma("sp", bones[:], bones_d, writes=[bones])
                pk = 0
                for g in range(RG):
                    gs = slice(g * 128, (g + 1) * 128)
                    a, b, bo, ga = y0[g % 2], y1[g % 2], bon[g % 2], gat[g % 2]
                    S.dma("sp", a[:], rwY[0][gs, :], writes=[a])
                    S.dma("sp", b[:], rwY[1][gs, :], writes=[b])
                    S.dma("sp", bo[:], rwBon[gs, :], writes=[bo])
                    S.dma("sp", ga[:], rwGate[gs, :], writes=[ga])
                    S.op("dve", lambda e, a=a, b=b: e.tensor_tensor(a[:], a[:], b[:], ALU.add), reads=[a, b], writes=[a])
                    for tt in range(NT):
                        sl = slice(tt * TT, (tt + 1) * TT)
                        p = ps[pk % 4]
                        pk += 1
                        S.op("pe", lambda e, p=p, a=a, sl=sl: e.matmul(p[:], bones[:], a[:, sl], start=True, stop=True), reads=[bones, a], writes=[p])
                        S.op("act", lambda e, p=p, sl=sl: e.activation(mu[:, sl], p[:], AF.Identity, bias=0.0, scale=-1.0 / 64), reads=[p], writes=[mu])
                    S.op("dve", lambda e, a=a: e.tensor_tensor(a[:], a[:], mu[:], ALU.add), reads=[a, mu], writes=[a])
                    S.op("act", lambda e, a=a: e.activation(sq[:], a[:], AF.Square), reads=[a], writes=[sq])
                    for tt in range(NT):
                        sl = slice(tt * TT, (tt + 1) * TT)
                        p = ps[pk % 4]
                        pk += 1
                        S.op("pe", lambda e, p=p, sl=sl: e.matmul(p[:], bones[:], sq[:, sl], start=True, stop=True), reads=[bones, sq], writes=[p])
                        S.op("act", lambda e, p=p, sl=sl: e.activation(mu[:, sl], p[:], AF.Sqrt, bias=64e-5, scale=1.0 / 64), reads=[p], writes=[mu])
                    S.op("dve", lambda e: e.reciprocal(mu[:], mu[:]), reads=[mu], writes=[mu])
                    S.op("dve", lambda e, a=a: e.tensor_tensor(a[:], a[:], mu[:], ALU.mult), reads=[a, mu], writes=[a])
                    S.op("act", lambda e, a=a, g=g: e.activation(a[:], a[:], AF.Identity, bias=prm[:, 4, g:g + 1], scale=prm[:, 3, g:g + 1]),
                         reads=[a, prm], writes=[a])
                    S.op("dve", lambda e, a=a, bo=bo: e.tensor_tensor(a[:], a[:], bo[:], ALU.add), reads=[a, bo], writes=[a])
                    S.op("pool", lambda e, a=a, ga=ga: e.tensor_tensor(a[:], a[:], ga[:], ALU.mult), reads=[a, ga], writes=[a])
                    S.dma("sp", mixT_d[c.DA_W + g * 128:c.DA_W + (g + 1) * 128, :], a[:], reads=[a])
                S.barrier()

        def phase_rw_zero(l):
            with ExitStack() as es:
                z = alloc(es, "zz", [128, T])
                S.op("dve", lambda e: e.memset(z[:], 0.0), writes=[z])
                for g in range(c.RG):
                    S.dma("sp", mixT_d[c.DA_W + g * 128:c.DA_W + (g + 1) * 128, :], z[:], reads=[z])
                S.barrier()

        phase_mod()
        src = xT_in
        for l in range(L):
            if stop <= 1:
                break
            with ExitStack() as es:
                hT = alloc(es, "hT", [128, DC, T], BF16)
                phase_norm(es, src, a1[l], (l, SH1), hT)
                if stop <= 2:
                    break
                phase_proj(l, hT)
            if stop <= 3:
                break
            phase_attn(l)
            phase_cm(l)
            if rw:
                phase_rw_pre(l)
                if rwstop <= 1:
                    break
                phase_rw_scan(l)
                if rwstop <= 2:
                    break
                phase_rw_post(l)
            else:
                phase_rw_zero(l)
            with ExitStack() as es:
                mixT = alloc(es, "mixT", [128, DC, T], BF16)
                for ch in range(DC):
                    S.dma("pool", mixT[:, ch, :], mixT_d[ch * 128:(ch + 1) * 128, :], writes=[mixT])
                phase_wout(l, mixT, src, xs[0])
            if stop <= 4:
                break
            if debug and l == 0:
                with ExitStack() as es:
                    tmp = alloc(es, "dbgx", [128, T])
                    for ch in range(DC):
                        S.dma("sp", tmp[:], xs[0][ch * 128:(ch + 1) * 128, :], writes=[tmp])
                        S.dma("sp", dbg["x1"][ch * 128:(ch + 1) * 128, :], tmp[:], reads=[tmp])
                    S.barrier()
            with ExitStack() as es:
                hT = alloc(es, "hT2", [128, DC, T], BF16)
                phase_norm(es, xs[0], a2[l], (l, SH2), hT)
                if stop <= 5:
                    break
                phase_ffn_up(l, hT)
            if stop <= 6:
                break
            phase_ffn_down(l, xs[0], xs[1])
            src = xs[1]
        if stop > 7 and rwstop >= 9:
            phase_norm(None, src, fg, None, None, dst_dram=yT)
        S.barrier()
        es_glob.close()

    for si in range(NS):
        cur[0] = si
        emit_stream()
    return nc, S


def _pm(v, nchunk):
    return np.ascontiguousarray(np.asarray(v, np.float32).reshape(nchunk, 128).T)


def make_streams(cfg, BATCH, DEC_BATCH):
    st = [("lat", b) for b in range(DEC_BATCH)]
    per = cfg.T // cfg.SEQ
    for i in range(0, BATCH, per):
        st.append(("ctx", list(range(i, i + per))))
    return st


def prep_inputs(cfg, inp, core_streams):
    c = cfg
    L, DC, T = c.L, c.DC, c.T
    f32 = np.float32
    shared = {}
    shared["ident"] = np.eye(128, dtype=f32)
    shared["mod_w"] = np.ascontiguousarray(inp["mod_w"], f32)
    shared["mod_bT"] = np.stack([_pm(inp["mod_b"][l], 6 * DC) for l in range(L)])
    shared["n1gT"] = np.stack([_pm(inp["norm1_g"][l], DC) for l in range(L)])
    shared["n2gT"] = np.stack([_pm(inp["norm2_g"][l], DC) for l in range(L)])
    shared["fgT"] = _pm(inp["final_norm_g"], DC)
    shared["w_in"] = np.ascontiguousarray(inp["w_in"], f32)
    shared["w_out"] = np.ascontiguousarray(inp["w_out"], f32)
    shared["ffn_up"] = np.ascontiguousarray(inp["ffn_up"], f32)
    shared["ffn_down"] = np.ascontiguousarray(inp["ffn_down"], f32)
    cw = np.asarray(inp["ffn_conv_w"], f32)
    shared["ffn_cw"] = np.ascontiguousarray(cw.reshape(L, 3, 2 * c.KF, 128).transpose(0, 3, 2, 1))
    shared["ffn_cb"] = np.stack([_pm(inp["ffn_conv_b"][l], 2 * c.KF) for l in range(L)])
    shared["lam_rep"] = np.ascontiguousarray(np.broadcast_to(
        np.asarray(inp["da_lambda"], f32).reshape(L, 1, 512), (L, 128, 512)))
    shared["subg"] = np.stack([_pm(inp["da_subln_g"][l], 2) for l in range(L)])
    shared["cmg"] = np.stack([_pm(inp["cm_norm_g"][l], c.CM_W // 128) for l in range(L)])
    shared["wsT"] = np.ascontiguousarray(np.asarray(inp["cm_ws"], f32).transpose(0, 1, 3, 2))
    shared["bs_rep"] = np.ascontiguousarray(np.broadcast_to(
        np.asarray(inp["cm_bs"], f32)[:, :, None, :], (L, 4, 128, 128)))
    RG, RW_W, RW_IN = c.RG, c.RW_W, c.RW_IN
    NRC = (RW_IN + 127) // 128
    CK = 64
    NCH = T // CK
    rcw = np.zeros((L, 3, NRC * 128), f32)
    rcw[:, :, :RW_IN] = np.asarray(inp["rw_conv_w"], f32)
    shared["rw_cw"] = np.ascontiguousarray(rcw.reshape(L, 3, NRC, 128).transpose(0, 3, 2, 1))
    shared["w0T"] = np.ascontiguousarray(np.asarray(inp["rw_w0"], f32).reshape(L, 2, RG, 128).transpose(0, 3, 1, 2))
    shared["a0T"] = np.ascontiguousarray(np.asarray(inp["rw_a0"], f32).reshape(L, 2, RG, 128).transpose(0, 3, 1, 2))
    shared["rw_w2"] = np.ascontiguousarray(inp["rw_w2"], f32)
    shared["rw_a2"] = np.ascontiguousarray(inp["rw_a2"], f32)
    shared["rw_g2"] = np.ascontiguousarray(inp["rw_g2"], f32)
    prm = np.stack([np.asarray(inp[k], f32).reshape(L, RW_W) for k in ("rw_k_k", "rw_k_a", "rw_r_k", "rw_gn_g", "rw_gn_b")], axis=1)
    shared["rwp"] = np.ascontiguousarray(prm.reshape(L, 5, RG, 128).transpose(0, 3, 1, 2))
    bo = np.zeros((128, 128), f32)
    bo[:64, :64] = 1.0
    bo[64:, 64:] = 1.0
    shared["bones"] = bo
    tt_ = np.arange(T)
    shared["cmask"] = np.ascontiguousarray(np.broadcast_to((tt_ % CK != 0).astype(f32), (128, T)))
    ii = np.arange(64)
    tri = np.stack([(ii[:, None] < ii[None, :]), (ii[:, None] <= ii[None, :]),
                    (ii[:, None] > ii[None, :]), (ii[:, None] >= ii[None, :])], axis=1).astype(f32)
    shared["tri"] = np.ascontiguousarray(tri)
    maps = []
    for streams in core_streams:
        cm = dict(shared)
        for si, (kind, ident_) in enumerate(streams):
            m = _prep_stream(cfg, inp, kind, ident_)
            for k_, v_ in m.items():
                cm[f"{k_}_{si}"] = v_
        maps.append(cm)
    return maps


def _prep_stream(cfg, inp, kind, ident_):
    c = cfg
    L, DC, T = c.L, c.DC, c.T
    f32 = np.float32
    RG = c.RG
    CK = 64
    NCH = T // CK
    if True:
        m = {}
        if kind == "lat":
            b = ident_
            m["xT"] = np.ascontiguousarray(np.asarray(inp["x_sample"][b], f32).T)
            m["condT"] = _pm(inp["c"][b], DC)
            seqlen = T
        else:
            xs_ = np.concatenate([np.asarray(inp["x_prompt"][s], f32) for s in ident_], axis=0)
            m["xT"] = np.ascontiguousarray(xs_.T)
            m["condT"] = _pm(inp["c_ctx"], DC)
            seqlen = c.SEQ
        t = np.arange(T)
        mp = (t % seqlen != 0).astype(f32)
        mn = (t % seqlen != seqlen - 1).astype(f32)
        m["mprev"] = np.ascontiguousarray(np.broadcast_to(mp, (128, T)))
        m["mnext"] = np.ascontiguousarray(np.broadcast_to(mn, (128, T)))
        qa = np.zeros((8, T), f32)
        ka = np.zeros((8, c.NK), f32)
        sid = t // seqlen
        qa[sid, t] = MASK_BIG
        ka[sid, t] = 1.0
        rc = np.ones((128, T), np.float64)
        rsn = np.zeros((128, T), np.float64)
        if kind == "lat":
            ka[0, T:] = 1.0
            kc = np.asarray(inp["cache_da_k"][b], f32)
            m["kcT"] = np.ascontiguousarray(kc.transpose(0, 2, 3, 4, 1))
            vcc = np.asarray(inp["cache_da_v"][b], f32)
            m["vc"] = np.ascontiguousarray(vcc.transpose(0, 2, 1, 3))
            p = np.arange(128)
            i = p % 64
            j = i % 32
            inv = 10000.0 ** (-j.astype(np.float64) / 32)
            row = (t // c.GRID_W).astype(np.float64)
            col = (t % c.GRID_W).astype(np.float64)
            pos = np.where((p < 64)[:, None], row[None, :], col[None, :])
            ang = (pos.astype(np.float32) * inv.astype(np.float32)[:, None]).astype(np.float32)
            rc = np.cos(ang)
            sn = np.sin(ang)
            rsn = np.where((i < 32)[:, None], -sn, sn)
        else:
            m["kcT"] = np.zeros((L, c.H, 2, 128, c.PAST), f32)
            m["vc"] = np.zeros((L, c.H, c.PAST, 256), f32)
        m["qaug"] = qa
        m["kaug"] = ka
        kp = np.ones((2, NCH), f32)
        if kind == "lat":
            st = np.asarray(inp["state_rwkv"][b], f32)
            m["s0T"] = np.ascontiguousarray(st.transpose(0, 1, 2, 4, 3).reshape(L, 2, RG, 128, 64))
        else:
            m["s0T"] = np.zeros((L, 2, RG, 128, 64), f32)
            cpos = np.arange(NCH) * CK
            kp[0] = (cpos % c.SEQ != 0)
            kp[1] = ((cpos + CK) % c.SEQ != 0)
        m["keepc"] = np.ascontiguousarray(np.broadcast_to(kp, (128, 2, NCH)))
        m["ropeC"] = np.ascontiguousarray(rc, f32)
        m["ropeS"] = np.ascontiguousarray(rsn, f32)
    return m


def assemble(cfg, results, core_streams, BATCH, DEC_BATCH):
    c = cfg
    L, T, D = c.L, c.T, c.D
    f32 = np.float32
    y_prompt = np.zeros((BATCH, c.SEQ, D), f32)
    y_sample = np.zeros((DEC_BATCH, T, D), f32)
    new_k = np.zeros((BATCH, L, c.SEQ, c.H, 2, 128), f32)
    new_v = np.zeros((BATCH, L, c.SEQ, c.H, 256), f32)
    new_s = np.zeros((BATCH, L, 2, c.RH, 64, 64), f32)
    for ci, streams in enumerate(core_streams):
      for si, (kind, ident_) in enumerate(streams):
        r = {k_[:-len(f"_{si}")]: v_ for k_, v_ in results[ci].items() if k_.endswith(f"_{si}")}
        if kind == "lat":
            y_sample[ident_] = r["yT"].T
        else:
            y = r["yT"].T
            kv = r["kvT"]
            for j, s in enumerate(ident_):
                tsl = slice(j * c.SEQ, (j + 1) * c.SEQ)
                y_prompt[s] = y[tsl]
                for l in range(L):
                    new_k[s, l] = kv[l, 0:c.DA_W, tsl].T.reshape(c.SEQ, c.H, 2, 128)
                    new_v[s, l] = kv[l, c.DA_W:2 * c.DA_W, tsl].T.reshape(c.SEQ, c.H, 256)
                st = r["statesT"][j]
                new_s[s] = st.reshape(L, 2, c.RH, 64, 64).transpose(0, 1, 2, 4, 3)
    return y_prompt, y_sample, new_k, new_v, new_s


def plan_cores(streams, n_cores):
    ns = (len(streams) + n_cores - 1) // n_cores
    cs = []
    for ci in range(n_cores):
        cs.append([streams[min(ci * ns + j, len(streams) - 1)] for j in range(ns)])
    return cs, ns


N_CORES = 3


def kernel(**inputs):
    cfg = Cfg()
    inp = {k: np.asarray(v) for k, v in inputs.items()}
    BATCH = inp["x_prompt"].shape[0]
    DEC_BATCH = inp["x_sample"].shape[0]
    streams = make_streams(cfg, BATCH, DEC_BATCH)
    core_streams, ns = plan_cores(streams, N_CORES)
    maps = prep_inputs(cfg, inp, core_streams)
    nc, _ = build(cfg, NS=ns)
    res = run_bass_kernel_spmd(nc, maps, core_ids=list(range(N_CORES)))
    return assemble(cfg, res.results, core_streams, BATCH, DEC_BATCH)
```

```python
import math
from contextlib import ExitStack
import numpy as np
import concourse.bass as bass
import concourse.mybir as mybir
from concourse.bass_utils import run_bass_kernel_spmd

F32 = mybir.dt.float32
BF16 = mybir.dt.bfloat16
AF = mybir.ActivationFunctionType
ALU = mybir.AluOpType
AX = mybir.AxisListType

EPOCH = 30000
SAME_ENGINE_SYNC = True


class Buf:
    __slots__ = ("name", "w", "r")

    def __init__(self, name=""):
        self.name = name
        self.w = None
        self.r = {}


class Tile:
    def __init__(self, t, name=""):
        self.t = t
        self.b = Buf(name)

    def __getitem__(self, k):
        return self.t[k]


def _bufs(xs):
    out = []
    for x in xs:
        if x is None:
            continue
        out.append(x.b if isinstance(x, Tile) else x)
    return out


class Sched:
    def __init__(self, nc, ndma=8):
        self.nc = nc
        self.eng = {"pe": nc.tensor, "dve": nc.vector, "act": nc.scalar,
                    "pool": nc.gpsimd, "sp": nc.sync}
        self.csem = {}
        self.cnt = {}
        self.epoch = {}
        self.seen = {e: {} for e in self.eng}
        for e in self.eng:
            self.epoch[e] = 0
            self.cnt[e] = 0
            self.csem[(e, 0)] = nc.alloc_semaphore(f"c_{e}_0")
        self.ndma = ndma
        self.dsem = {}
        self.dval = {}
        self.dnext = {}
        self.n_instr = 0

    def _wait(self, e, tok):
        if tok is None:
            return
        if tok[0] == "c":
            _, fe, ep, c = tok
            if fe == e and (e == "pe" or not SAME_ENGINE_SYNC):
                return
            key = ("c", fe, ep)
            if self.seen[e].get(key, 0) >= c:
                return
            self.eng[e].wait_ge(self.csem[(fe, ep)], c)
            self.seen[e][key] = c
        else:
            _, q, i, v = tok
            key = ("d", q, i)
            if self.seen[e].get(key, 0) >= v:
                return
            self.eng[e].wait_ge(self.dsem[(q, i)], v)
            self.seen[e][key] = v

    def _deps(self, e, reads, writes):
        for b in reads:
            self._wait(e, b.w)
        for b in writes:
            self._wait(e, b.w)
            for t in b.r.values():
                self._wait(e, t)

    def _mark(self, key, tok, reads, writes):
        for b in reads:
            b.r[key] = tok
        for b in writes:
            b.w = tok
            b.r = {}

    def _next_tok(self, e):
        if self.cnt[e] >= EPOCH:
            self.epoch[e] += 1
            self.cnt[e] = 0
            self.csem[(e, self.epoch[e])] = self.nc.alloc_semaphore(f"c_{e}_{self.epoch[e]}")
        self.cnt[e] += 1
        return ("c", e, self.epoch[e], self.cnt[e])

    def op(self, e, fn, reads=(), writes=()):
        reads = _bufs(reads)
        writes = _bufs(writes)
        self._deps(e, reads, writes)
        tok = self._next_tok(e)
        ins = fn(self.eng[e])
        ins.then_inc(self.csem[(e, tok[2])], 1)
        self._mark(e, tok, reads, writes)
        self.n_instr += 1
        return tok

    def dma(self, q, out, in_, reads=(), writes=(), **kw):
        reads = _bufs(reads)
        writes = _bufs(writes)
        self._deps(q, reads, writes)
        i = self.dnext.get(q, 0)
        self.dnext[q] = (i + 1) % self.ndma
        if (q, i) not in self.dsem:
            self.dsem[(q, i)] = self.nc.alloc_semaphore(f"d_{q}_{i}")
            self.dval[(q, i)] = 0
        if self.dval[(q, i)] > 0:
            self._wait(q, ("d", q, i, self.dval[(q, i)]))
        self.dval[(q, i)] += 16
        tok = ("d", q, i, self.dval[(q, i)])
        self.eng[q].dma_start(out=out, in_=in_, **kw).then_inc(self.dsem[(q, i)], 16)
        self._mark(("d", q, i), tok, reads, writes)
        self.n_instr += 1
        return tok

    def barrier(self):
        toks = []
        for e in self.eng:
            if self.cnt[e] > 0:
                toks.append(("c", e, self.epoch[e], self.cnt[e]))
        for (q, i), v in self.dval.items():
            if v > 0:
                toks.append(("d", q, i, v))
        for e in self.eng:
            for t in toks:
                if t[0] == "c" and t[1] == e and (e == "pe" or not SAME_ENGINE_SYNC):
                    continue
                self._wait(e, t)


class Cfg:
    def __init__(self, D=4096, T=2048, DFF=11008, PAST=256, SEQ=256, GRID_W=64, L=2):
        self.D, self.T, self.DFF, self.PAST, self.SEQ, self.GRID_W, self.L = D, T, DFF, PAST, SEQ, GRID_W, L
        self.DC = D // 128
        self.DA_W = D // 2
        self.H = self.DA_W // 256
        self.RW_W = D // 4
        self.RH = self.RW_W // 64
        self.RG = self.RW_W // 128
        self.RW_IN = 3 * self.RW_W + 128 + 128 + 160
        self.CM_W = D // 4
        self.P_IN = 3 * self.DA_W + self.RW_IN + 2 * self.CM_W
        self.TT = min(512, T)
        self.NT = T // self.TT
        self.NB = T // 128
        self.KF = DFF // 128
        self.NSEG = T // SEQ
        self.RW0 = 3 * self.DA_W
        self.UV0 = self.RW0 + self.RW_IN
        self.NK = T + PAST
        self.NKB = self.NK // 128


NORM_EPS = 1e-6
PER_STREAM = {"xT", "condT", "mprev", "mnext", "kcT", "vc", "qaug", "kaug", "ropeC", "ropeS", "keepc", "s0T"}
MASK_BIG = 1024.0


def build(cfg, debug=False, stop=99, rw=True, rwstop=9, NS=1):
    c = cfg
    D, T, DC, L, TT, NT = c.D, c.T, c.DC, c.L, c.TT, c.NT
    nc = bass.Bass("TRN2", target_bir_lowering=False)
    S = Sched(nc)

    cache = {}
    cur = [0]

    def _mk(name, shape, dt, kind, per):
        key = f"{name}_{cur[0]}" if per else name
        if key not in cache:
            cache[key] = nc.dram_tensor(key, list(shape), dt, kind=kind).ap()
        return cache[key]

    def din(name, shape, dt=F32):
        return _mk(name, shape, dt, "ExternalInput", name in PER_STREAM)

    def dout(name, shape, dt=F32):
        return _mk(name, shape, dt, "ExternalOutput", True)

    def dscr(name, shape, dt=F32):
        return _mk(name, shape, dt, "Internal", False)

    uid = [0]

    def emit_stream():
        xT_in = din("xT", [D, T])
        condT = din("condT", [128, DC])
        ident_d = din("ident", [128, 128])
        mod_w = din("mod_w", [L, D, 6 * D])
        mod_bT = din("mod_bT", [L, 128, 6 * DC])
        n1gT = din("n1gT", [L, 128, DC])
        n2gT = din("n2gT", [L, 128, DC])
        fgT = din("fgT", [128, DC])
        w_in = din("w_in", [L, D, c.P_IN])
        w_out = din("w_out", [L, D, D])
        ffn_up = din("ffn_up", [L, D, 2 * c.DFF])
        ffn_down = din("ffn_down", [L, c.DFF, D])
        ffn_cw = din("ffn_cw", [L, 128, 2 * c.KF, 3])
        ffn_cb = din("ffn_cb", [L, 128, 2 * c.KF])
        mprev_d = din("mprev", [128, T])
        mnext_d = din("mnext", [128, T])
        kcT_d = din("kcT", [L, c.H, 2, 128, c.PAST])
        vc_d = din("vc", [L, c.H, c.PAST, 256])
        qaug_d = din("qaug", [8, T])
        kaug_d = din("kaug", [8, c.NK])
        ropeC_d = din("ropeC", [128, T])
        ropeS_d = din("ropeS", [128, T])
        lam_d = din("lam_rep", [L, 128, 512])
        subg_d = din("subg", [L, 128, 2])
        CMC = c.CM_W // 128
        cmg_d = din("cmg", [L, 128, CMC])
        wsT_d = din("wsT", [L, 4, 128, 128])
        bsr_d = din("bs_rep", [L, 4, 128, 128])
        RG, RH, RW_W = c.RG, c.RH, c.RW_W
        CK = 64
        NCH = T // CK
        NRC = (c.RW_IN + 127) // 128
        rw_cw_d = din("rw_cw", [L, 128, NRC, 3])
        w0T_d = din("w0T", [L, 128, 2, RG])
        a0T_d = din("a0T", [L, 128, 2, RG])
        w2_d = din("rw_w2", [L, 2, 64, RW_W])
        a2_d = din("rw_a2", [L, 2, 64, RW_W])
        g2_d = din("rw_g2", [L, 160, RW_W])
        rwp_d = din("rwp", [L, 128, 5, RG])
        bones_d = din("bones", [128, 128])
        cmask_d = din("cmask", [128, T])
        tri_d = din("tri", [64, 4, 64])
        keep_d = din("keepc", [128, 2, NCH])
        s0T_d = din("s0T", [L, 2, RG, 128, 64])
        statesT = dout("statesT", [c.NSEG, L, 2, RG, 128, 64])

        yT = dout("yT", [D, T])
        kvT = dout("kvT", [L, 2 * c.DA_W, T])
        dbg = {}
        if debug:
            dbg["projT"] = dout("dbg_projT", [c.P_IN, T])
            dbg["modT"] = dout("dbg_modT", [L, 128, 6 * DC])
            dbg["x1"] = dout("dbg_x1", [D, T])

        xs = [dscr("xs0", [D, T]), dscr("xs1", [D, T])]
        projT = dbg["projT"] if debug else dscr("projT", [c.P_IN, T])
        actT = dscr("actT", [c.DFF, T], BF16)
        mixT_d = dscr("mixT_d", [D, T])
        dd = dout if (debug and rwstop < 9) else dscr
        rwA = [dd(f"rwA{d}", [RW_W, T]) for d in range(2)]
        rwB = [dd(f"rwB{d}", [RW_W, T]) for d in range(2)]
        rwK = [dd(f"rwK{d}", [RW_W, T]) for d in range(2)]
        rwR = [dd(f"rwR{d}", [RW_W, T]) for d in range(2)]
        rwP = [dd(f"rwP{d}", [RW_W, NCH]) for d in range(2)]
        rwV = dd("rwV", [RW_W, T])
        rwBon = dd("rwBon", [RW_W, T])
        rwGate = dd("rwGate", [RW_W, T])
        rwY = [dd(f"rwY{d}", [RW_W, T]) for d in range(2)]

        es_glob = ExitStack()


        def alloc(es, name, shape, dt=F32):
            uid[0] += 1
            return Tile(es.enter_context(nc.sbuf_tensor(f"s{uid[0]}_{name}", list(shape), dt)), name)

        def palloc(es, name, shape, dt=F32):
            uid[0] += 1
            return Tile(es.enter_context(nc.psum_tensor(f"p{uid[0]}_{name}", list(shape), dt)), name)

        ident = alloc(es_glob, "ident", [128, 128])
        ones_f = alloc(es_glob, "ones_f", [128, 128])
        ones_b = alloc(es_glob, "ones_b", [128, 128], BF16)
        modT = [alloc(es_glob, f"modT{l}", [128, 6 * DC]) for l in range(L)]
        a1 = [alloc(es_glob, f"a1_{l}", [128, DC]) for l in range(L)]
        a2 = [alloc(es_glob, f"a2_{l}", [128, DC]) for l in range(L)]
        fg = alloc(es_glob, "fg", [128, DC])
        S.dma("sp", ident[:], ident_d, writes=[ident])
        S.dma("sp", fg[:], fgT, writes=[fg])
        S.op("dve", lambda e: e.memset(ones_f[:], 1.0), writes=[ones_f])
        S.op("dve", lambda e: e.memset(ones_b[:], 1.0), writes=[ones_b])

        SH1, SC1, G1, SH2, SC2, G2 = range(6)

        def modcol(l, which, ch):
            return modT[l][:, which * DC + ch: which * DC + ch + 1]

        def phase_mod():
            NBW = 2048
            nblk = 6 * D // NBW
            with ExitStack() as es:
                cnd = alloc(es, "cnd", [128, DC])
                sc = alloc(es, "silu_c", [128, DC])
                one1 = alloc(es, "one1", [1, 1])
                wbuf = [alloc(es, f"mw{i}", [128, NBW]) for i in range(3)]
                row = [alloc(es, f"mrow{i}", [1, NBW]) for i in range(2)]
                mb = alloc(es, "mb", [128, 6 * DC])
                gt = alloc(es, "gt", [128, DC])
                ps = [palloc(es, f"mps{i}", [128, NBW]) for i in range(1)]
                psT = palloc(es, "mpsT", [128, 6 * DC])
                S.dma("sp", cnd[:], condT, writes=[cnd])
                S.op("act", lambda e: e.activation(sc[:], cnd[:], AF.Silu), reads=[cnd], writes=[sc])
                S.op("dve", lambda e: e.memset(one1[:], 1.0), writes=[one1])
                k = 0
                for l in range(L):
                    for nb in range(nblk):
                        p = ps[0]
                        for ch in range(DC):
                            w = wbuf[k % 3]
                            k += 1
                            S.dma("sp", w[:], mod_w[l, ch * 128:(ch + 1) * 128, nb * NBW:(nb + 1) * NBW], writes=[w])
                            for j in range(NBW // 512):
                                S.op("pe", lambda e, j=j, w=w, ch=ch, p=p: e.matmul(
                                    p[0:1, j * 512:(j + 1) * 512], sc[:, ch:ch + 1], w[:, j * 512:(j + 1) * 512],
                                    start=(ch == 0), stop=(ch == DC - 1)), reads=[sc, w], writes=[p])
                        r = row[nb % 2]
                        S.op("act", lambda e, r=r, p=p: e.activation(r[:], p[0:1, :], AF.Copy), reads=[p], writes=[r])
                        for j in range(NBW // 128):
                            col = nb * (NBW // 128) + j
                            S.op("pe", lambda e, r=r, j=j, col=col: e.matmul(
                                psT[:, col:col + 1], r[0:1, j * 128:(j + 1) * 128], one1[0:1, 0:1],
                                start=True, stop=True), reads=[r, one1], writes=[psT])
                    S.dma("sp", mb[:], mod_bT[l], writes=[mb])
                    S.op("dve", lambda e, l=l: e.tensor_tensor(modT[l][:], psT[:], mb[:], ALU.add),
                         reads=[psT, mb], writes=[modT[l]])
                    for (dst, src, which) in ((a1[l], n1gT, SC1), (a2[l], n2gT, SC2)):
                        S.dma("sp", gt[:], src[l], writes=[gt])
                        S.op("dve", lambda e, dst=dst, which=which, l=l: e.scalar_tensor_tensor(
                            dst[:], modT[l][:, which * DC:(which + 1) * DC], 1.0, gt[:], ALU.add, ALU.mult),
                            reads=[modT[l], gt], writes=[dst])
                    if debug:
                        S.dma("sp", dbg["modT"][l], modT[l][:], reads=[modT[l]])
                S.barrier()

        def phase_norm(es_outer, src, a_t, sh_l_which, hT, dst_dram=None):
            with ExitStack() as es:
                rstd = alloc(es, "rstd", [128, T])
                xt = [alloc(es, f"nx{i}", [128, TT]) for i in range(3)]
                sq = [alloc(es, f"nsq{i}", [128, TT]) for i in range(2)]
                ps = [palloc(es, f"nps{i}", [128, TT]) for i in range(2)]
                k = 0
                for tt in range(NT):
                    p = ps[tt % 2]
                    for ch in range(DC):
                        x = xt[k % 3]
                        q = sq[k % 2]
                        k += 1
                        S.dma("sp", x[:], src[ch * 128:(ch + 1) * 128, tt * TT:(tt + 1) * TT], writes=[x])
                        S.op("act", lambda e, q=q, x=x: e.activation(q[:], x[:], AF.Square), reads=[x], writes=[q])
                        S.op("pe", lambda e, p=p, q=q, ch=ch: e.matmul(p[:], ones_f[:], q[:], start=(ch == 0), stop=(ch == DC - 1)),
                             reads=[ones_f, q], writes=[p])
                    sl = slice(tt * TT, (tt + 1) * TT)
                    S.op("act", lambda e, p=p, sl=sl: e.activation(rstd[:, sl], p[:], AF.Sqrt, bias=NORM_EPS, scale=1.0 / D),
                         reads=[p], writes=[rstd])
                    S.op("dve", lambda e, sl=sl: e.reciprocal(rstd[:, sl], rstd[:, sl]), reads=[rstd], writes=[rstd])
                with ExitStack() as es2:
                    x2 = [alloc(es2, f"nxx{i}", [128, T]) for i in range(2)]
                    tm = [alloc(es2, f"ntm{i}", [128, T]) for i in range(2)]
                    for ch in range(DC):
                        x = x2[ch % 2]
                        t_ = tm[ch % 2]
                        S.dma("sp", x[:], src[ch * 128:(ch + 1) * 128, :], writes=[x])
                        S.op("dve", lambda e, t_=t_, x=x: e.tensor_tensor(t_[:], x[:], rstd[:], ALU.mult), reads=[x, rstd], writes=[t_])
                        if dst_dram is None:
                            l, which = sh_l_which
                            S.op("act", lambda e, t_=t_, ch=ch, l=l, which=which: e.activation(
                                hT[:, ch, :], t_[:], AF.Identity, bias=modcol(l, which, ch), scale=a_t[:, ch:ch + 1]),
                                reads=[t_, a_t, modT[l]], writes=[hT])
                        else:
                            S.op("act", lambda e, x=x, t_=t_, ch=ch: e.activation(
                                x[:], t_[:], AF.Identity, bias=0.0, scale=a_t[:, ch:ch + 1]), reads=[t_, a_t], writes=[x])
                            S.dma("sp", dst_dram[ch * 128:(ch + 1) * 128, :], x[:], reads=[x])
                    S.barrier()

        def dense(es, hT, KC, W, cols, epilogue, tag, nw=3, nps=4, pre=None):
            wb = [alloc(es, f"{tag}w{i}", [128, KC, 128], BF16) for i in range(nw)]
            ps = [palloc(es, f"{tag}p{i}", [128, TT]) for i in range(nps)]
            k = 0
            for i, (c0, mw) in enumerate(cols):
                w = wb[i % nw]
                S.dma("pool", w[:, :, 0:mw], W[:, c0:c0 + mw].rearrange("(c p) m -> p c m", p=128), writes=[w])
                if pre is not None:
                    pre(i)
                for tt in range(NT):
                    p = ps[k % nps]
                    k += 1
                    for ch in range(KC):
                        S.op("pe", lambda e, p=p, w=w, ch=ch, mw=mw, tt=tt: e.matmul(
                            p[0:mw, :], w[:, ch, 0:mw], hT[:, ch, tt * TT:(tt + 1) * TT],
                            start=(ch == 0), stop=(ch == KC - 1)), reads=[w, hT], writes=[p])
                    epilogue(i, tt, p, mw)

        def chunks(n0, n):
            out = []
            x = n0
            while x < n0 + n:
                w = min(128, n0 + n - x)
                out.append((x, w))
                x += w
            return out

        def phase_proj(l, hT):
            with ExitStack() as es:
                stage = [alloc(es, f"pst{i}", [128, T]) for i in range(2)]
                cols = chunks(0, c.P_IN)

                def epi(i, tt, p, mw):
                    st = stage[i % 2]
                    eng = "act" if (i * NT + tt) % 2 == 0 else "dve"
                    sl = slice(tt * TT, (tt + 1) * TT)
                    if eng == "act":
                        S.op("act", lambda e: e.activation(st[0:mw, sl], p[0:mw, :], AF.Copy), reads=[p], writes=[st])
                    else:
                        S.op("dve", lambda e: e.tensor_copy(st[0:mw, sl], p[0:mw, :]), reads=[p], writes=[st])
                    if tt == NT - 1:
                        c0 = cols[i][0]
                        S.dma("sp", projT[c0:c0 + mw, :], st[0:mw, :], reads=[st])
                        if c.DA_W <= c0 < 3 * c.DA_W:
                            S.dma("sp", kvT[l, c0 - c.DA_W:c0 - c.DA_W + mw, :], st[0:mw, :], reads=[st])

                dense(es, hT, DC, w_in[l], cols, epi, "pj")
                S.barrier()

        def phase_wout(l, mixT, src, dst):
            with ExitStack() as es:
                xo = [alloc(es, f"wxo{i}", [128, T]) for i in range(2)]
                xn = [alloc(es, f"wxn{i}", [128, T]) for i in range(2)]
                cols = chunks(0, D)

                def pre(i):
                    S.dma("sp", xo[i % 2][:], src[i * 128:(i + 1) * 128, :], writes=[xo[i % 2]])

                def epi(i, tt, p, mw):
                    sl = slice(tt * TT, (tt + 1) * TT)
                    S.op("dve", lambda e: e.scalar_tensor_tensor(
                        xn[i % 2][:, sl], p[:], modcol(l, G1, i), xo[i % 2][:, sl], ALU.mult, ALU.add),
                        reads=[p, modT[l], xo[i % 2]], writes=[xn[i % 2]])
                    if tt == NT - 1:
                        S.dma("sp", dst[i * 128:(i + 1) * 128, :], xn[i % 2][:], reads=[xn[i % 2]])

                dense(es, mixT, DC, w_out[l], cols, epi, "wo", pre=pre)
                S.barrier()

        def phase_ffn_up(l, hT):
            KF = c.KF
            with ExitStack() as es:
                mprev = alloc(es, "mprev", [128, T], BF16)
                mnext = alloc(es, "mnext", [128, T], BF16)
                cw = alloc(es, "fcw", [128, 2 * KF, 3])
                cb = alloc(es, "fcb", [128, 2 * KF])
                up = [alloc(es, f"fup{i}", [128, T]) for i in range(2)]
                o = [alloc(es, f"fo{i}", [128, T]) for i in range(2)]
                xm1 = alloc(es, "fxm", [128, T])
                xm = [xm1, xm1]
                ab = [alloc(es, f"fab{i}", [128, T], BF16) for i in range(2)]
                S.dma("pool", mprev[:], mprev_d, writes=[mprev])
                S.dma("pool", mnext[:], mnext_d, writes=[mnext])
                S.dma("sp", cw[:], ffn_cw[l], writes=[cw])
                S.dma("sp", cb[:], ffn_cb[l], writes=[cb])
                cols = []
                for m in range(KF):
                    cols.append((m * 128, 128))
                    cols.append((c.DFF + m * 128, 128))

                def epi(i, tt, p, mw):
                    br = i % 2
                    m = i // 2
                    cidx = m if br == 0 else KF + m
                    u = up[br]
                    sl = slice(tt * TT, (tt + 1) * TT)
                    ob = o[br]
                    S.op("act", lambda e: e.activation(u[:, sl], p[:], AF.Copy), reads=[p], writes=[u])
                    S.op("act", lambda e: e.activation(ob[:, sl], p[:], AF.Identity, bias=0.0, scale=cw[:, cidx, 1:2]),
                         reads=[p, cw], writes=[ob])
                    if tt < NT - 1:
                        return
                    S.op("dve", lambda e: e.tensor_tensor(xm[0][:, 0:T - 1], u[:, 0:T - 1], mprev[:, 1:T], ALU.mult),
                         reads=[u, mprev], writes=[xm[0]])
                    S.op("dve", lambda e: e.scalar_tensor_tensor(ob[:, 1:T], xm[0][:, 0:T - 1], cw[:, cidx, 0:1], ob[:, 1:T], ALU.mult, ALU.add),
                         reads=[xm[0], cw, ob], writes=[ob])
                    S.op("pool", lambda e: e.tensor_tensor(xm[1][:, 0:T - 1], u[:, 1:T], mnext[:, 0:T - 1], ALU.mult),
                         reads=[u, mnext], writes=[xm[1]])
                    S.op("dve", lambda e: e.scalar_tensor_tensor(ob[:, 0:T - 1], xm[1][:, 0:T - 1], cw[:, cidx, 2:3], ob[:, 0:T - 1], ALU.mult, ALU.add),
                         reads=[xm[1], cw, ob], writes=[ob])
                    if br == 0:
                        S.op("act", lambda e: e.activation(ob[:], ob[:], AF.Silu, bias=cb[:, cidx:cidx + 1]), reads=[ob, cb], writes=[ob])
                    else:
                        a = ab[m % 2]
                        S.op("dve", lambda e: e.scalar_tensor_tensor(a[:], ob[:], cb[:, cidx:cidx + 1], o[0][:], ALU.add, ALU.mult),
                             reads=[ob, cb, o[0]], writes=[a])
                        S.dma("sp", actT[m * 128:(m + 1) * 128, :], a[:], reads=[a])

                dense(es, hT, DC, ffn_up[l], cols, epi, "fu", nw=2)
                S.barrier()

        def phase_ffn_down(l, src, dst):
            KF = c.KF
            with ExitStack() as es:
                at = alloc(es, "fdact", [128, KF, TT], BF16)
                wb = [alloc(es, f"fdw{i}", [128, KF, 128], BF16) for i in range(2)]
                xo = [alloc(es, f"fdxo{i}", [128, TT]) for i in range(3)]
                xn = [alloc(es, f"fdxn{i}", [128, TT]) for i in range(3)]
                ps = [palloc(es, f"fdp{i}", [128, TT]) for i in range(3)]
                k = 0
                for tt in range(NT):
                    sl = slice(tt * TT, (tt + 1) * TT)
                    S.dma("sp", at[:], actT[:, sl].rearrange("(c p) t -> p c t", p=128), writes=[at])
                    for m in range(DC):
                        w = wb[k % 2]
                        p = ps[k % 3]
                        x_o = xo[k % 3]
                        x_n = xn[k % 3]
                        k += 1
                        S.dma("pool", w[:], ffn_down[l][:, m * 128:(m + 1) * 128].rearrange("(c p) m -> p c m", p=128), writes=[w])
                        S.dma("sp", x_o[:], src[m * 128:(m + 1) * 128, sl], writes=[x_o])
                        for ch in range(KF):
                            S.op("pe", lambda e, p=p, w=w, ch=ch: e.matmul(p[:], w[:, ch, :], at[:, ch, :], start=(ch == 0), stop=(ch == KF - 1)),
                                 reads=[w, at], writes=[p])
                        S.op("dve", lambda e, p=p, x_o=x_o, x_n=x_n, m=m: e.scalar_tensor_tensor(
                            x_n[:], p[:], modcol(l, G2, m), x_o[:], ALU.mult, ALU.add), reads=[p, modT[l], x_o], writes=[x_n])
                        S.dma("sp", dst[m * 128:(m + 1) * 128, sl], x_n[:], reads=[x_n])
                S.barrier()


        def phase_attn(l):
            H, NB, NK, NKB = c.H, c.NB, c.NK, c.NKB
            lam_init = 0.8 - 0.6 * math.exp(-0.3 * l)
            scale = 128.0 ** -0.5
            with ExitStack() as es:
                ropeC = alloc(es, "ropeC", [128, T])
                ropeS = alloc(es, "ropeS", [128, T])
                qaug = alloc(es, "qaug", [8, T], BF16)
                kaug = alloc(es, "kaug", [8, NK], BF16)
                lamt = alloc(es, "lamt", [128, 512])
                lamp = alloc(es, "lamp", [128, 512])
                lr = alloc(es, "lr", [128, 4])
                nlam = alloc(es, "nlam", [128, 1])
                sg = alloc(es, "sg", [128, 2])
                identb = alloc(es, "identb", [128, 128], BF16)
                ld = [alloc(es, f"ald{i}", [128, TT]) for i in range(4)]
                qr = [[alloc(es, f"qr{m}{i}", [128, TT], BF16) for i in range(2)] for m in range(2)]
                kr = [alloc(es, f"kr{m}", [128, NK], BF16) for m in range(2)]
                vt = [alloc(es, f"vt{i}", [128, TT]) for i in range(2)]
                vtok = alloc(es, "vtok", [128, NKB, 256], BF16)
                pt = [alloc(es, f"pt{i}", [128, TT], BF16) for i in range(3)]
                ym = [alloc(es, f"ym{m}", [128, 2, TT]) for m in range(2)]
                yd = alloc(es, "yd", [128, 2, TT])
                sq = alloc(es, "asq", [128, 2, TT])
                rden = alloc(es, "rden", [128, TT])
                rs = alloc(es, "ars", [128, TT])
                tq = alloc(es, "atq", [128, TT])
                ob = [alloc(es, f"aob{i}", [128, TT]) for i in range(2)]
                po = [palloc(es, f"apo{i}", [128, TT]) for i in range(2)]
                pden = palloc(es, "apden", [128, TT])
                pss = [palloc(es, f"apss{i}", [128, TT]) for i in range(2)]
                pss2 = palloc(es, "apss2", [128, TT])
                pst = palloc(es, "apst", [128, 128])
                S.dma("sp", ropeC[:], ropeC_d, writes=[ropeC])
                S.dma("sp", ropeS[:], ropeS_d, writes=[ropeS])
                S.dma("pool", qaug[:], qaug_d, writes=[qaug])
                S.dma("pool", kaug[:], kaug_d, writes=[kaug])
                S.dma("sp", lamt[:], lam_d[l], writes=[lamt])
                S.dma("sp", sg[:], subg_d[l], writes=[sg])
                S.op("dve", lambda e: e.tensor_copy(identb[:], ident[:]), reads=[ident], writes=[identb])
                S.op("dve", lambda e: e.tensor_tensor(lamp[:, 0:128], lamt[:, 0:128], lamt[:, 128:256], ALU.mult), reads=[lamt], writes=[lamp])
                S.op("dve", lambda e: e.tensor_tensor(lamp[:, 128:256], lamt[:, 256:384], lamt[:, 384:512], ALU.mult), reads=[lamt], writes=[lamp])
                S.op("dve", lambda e: e.tensor_reduce(lr[:, 0:2], lamp[:, 0:256].rearrange("p (a b) -> p a b", a=2), AX.X, ALU.add), reads=[lamp], writes=[lr])
                S.op("act", lambda e: e.activation(lr[:, 2:4], lr[:, 0:2], AF.Exp), reads=[lr], writes=[lr])
                S.op("dve", lambda e: e.tensor_tensor(nlam[:], lr[:, 3:4], lr[:, 2:3], ALU.subtract), reads=[lr], writes=[nlam])
                S.op("dve", lambda e: e.tensor_scalar(nlam[:], nlam[:], -lam_init, None, ALU.add), reads=[nlam], writes=[nlam])
                S.op("dve", lambda e: e.tensor_scalar(sg[:], sg[:], 1.0 - lam_init, None, ALU.mult), reads=[sg], writes=[sg])

                def rope_tile(rows0, tt, dst_ap, dst_tile, k):
                    sl = slice(tt * TT, (tt + 1) * TT)
                    a = ld[(2 * k) % 4]
                    b = ld[(2 * k + 1) % 4]
                    S.dma("sp", a[:], projT[rows0:rows0 + 128, sl], writes=[a])
                    for blk in range(4):
                        pb = blk ^ 1
                        S.dma("sp", b[blk * 32:(blk + 1) * 32, :], projT[rows0 + pb * 32:rows0 + (pb + 1) * 32, sl], writes=[b])
                    S.op("dve", lambda e: e.tensor_tensor(a[:], a[:], ropeC[:, sl], ALU.mult), reads=[a, ropeC], writes=[a])
                    S.op("pool", lambda e: e.tensor_tensor(b[:], b[:], ropeS[:, sl], ALU.mult), reads=[b, ropeS], writes=[b])
                    S.op("dve", lambda e: e.tensor_tensor(dst_ap, a[:], b[:], ALU.add), reads=[a, b], writes=[dst_tile])

                kctr = 0
                for h in range(H):
                    for m in range(2):
                        for tt in range(NT):
                            rope_tile(c.DA_W + h * 256 + m * 128, tt, kr[m][:, tt * TT:(tt + 1) * TT], kr[m], kctr)
                            kctr += 1
                        S.dma("pool", kr[m][:, T:NK], kcT_d[l, h, m], writes=[kr[m]])
                    vk = 0
                    for e_ in range(2):
                        for tt in range(NT):
                            v = vt[vk % 2]
                            vk += 1
                            r0 = 2 * c.DA_W + h * 256 + e_ * 128
                            S.dma("sp", v[:], projT[r0:r0 + 128, tt * TT:(tt + 1) * TT], writes=[v])
                            for j in range(TT // 128):
                                kb = tt * (TT // 128) + j
                                S.op("pe", lambda e, v=v, j=j: e.transpose(pst[:], v[:, j * 128:(j + 1) * 128], ident[:]),
                                     reads=[v, ident], writes=[pst])
                                S.op("act", lambda e, kb=kb, e_=e_: e.activation(vtok[:, kb, e_ * 128:(e_ + 1) * 128], pst[:], AF.Copy),
                                     reads=[pst], writes=[vtok])
                    S.dma("pool", vtok[:, NB:NKB, :], vc_d[l, h].rearrange("(j p) e -> p j e", p=128), writes=[vtok])
                    for tt in range(NT):
                        sl = slice(tt * TT, (tt + 1) * TT)
                        for m in range(2):
                            q = qr[m][tt % 2]
                            rope_tile(h * 256 + m * 128, tt, q[:], q, kctr)
                            kctr += 1
                            for kb in range(NKB):
                                ps_ = pss[kb % 2]
                                ksl = slice(kb * 128, (kb + 1) * 128)
                                S.op("pe", lambda e, ps_=ps_, ksl=ksl, q=q, m=m: e.matmul(ps_[:], kr[m][:, ksl], q[:], start=True, stop=False),
                                     reads=[kr[m], q], writes=[ps_])
                                S.op("pe", lambda e, ps_=ps_, ksl=ksl, sl=sl: e.matmul(ps_[:], kaug[:, ksl], qaug[:, sl], start=False, stop=True),
                                     reads=[kaug, qaug], writes=[ps_])
                                p_ = pt[kb % 3]
                                S.op("act", lambda e, p_=p_, ps_=ps_: e.activation(p_[:], ps_[:], AF.Exp, bias=-scale * MASK_BIG, scale=scale),
                                     reads=[ps_], writes=[p_])
                                st, sp_ = (kb == 0), (kb == NKB - 1)
                                S.op("pe", lambda e, p_=p_, kb=kb, st=st, sp_=sp_: e.matmul(po[0][:], vtok[:, kb, 0:128], p_[:], start=st, stop=sp_),
                                     reads=[vtok, p_], writes=[po[0]])
                                S.op("pe", lambda e, p_=p_, kb=kb, st=st, sp_=sp_: e.matmul(po[1][:], vtok[:, kb, 128:256], p_[:], start=st, stop=sp_),
                                     reads=[vtok, p_], writes=[po[1]])
                                S.op("pe", lambda e, p_=p_, st=st, sp_=sp_: e.matmul(pden[:], ones_b[:], p_[:], start=st, stop=sp_),
                                     reads=[ones_b, p_], writes=[pden])
                            S.op("dve", lambda e: e.reciprocal(rden[:], pden[:]), reads=[pden], writes=[rden])
                            for e_ in range(2):
                                S.op("dve", lambda e, e_=e_, m=m: e.tensor_tensor(ym[m][:, e_, :], po[e_][:], rden[:], ALU.mult),
                                     reads=[po[e_], rden], writes=[ym[m]])
                        S.op("dve", lambda e: e.scalar_tensor_tensor(yd[:], ym[1][:], nlam[:, 0:1], ym[0][:], ALU.mult, ALU.add),
                             reads=[ym[0], ym[1], nlam], writes=[yd])
                        S.op("act", lambda e: e.activation(sq[:], yd[:], AF.Square), reads=[yd], writes=[sq])
                        for e_ in range(2):
                            S.op("pe", lambda e, e_=e_: e.matmul(pss2[:], ones_f[:], sq[:, e_, :], start=(e_ == 0), stop=(e_ == 1)),
                                 reads=[ones_f, sq], writes=[pss2])
                        S.op("act", lambda e: e.activation(rs[:], pss2[:], AF.Sqrt, bias=NORM_EPS, scale=1.0 / 256), reads=[pss2], writes=[rs])
                        S.op("dve", lambda e: e.reciprocal(rs[:], rs[:]), reads=[rs], writes=[rs])
                        for e_ in range(2):
                            o_ = ob[e_]
                            S.op("dve", lambda e, e_=e_: e.tensor_tensor(tq[:], yd[:, e_, :], rs[:], ALU.mult), reads=[yd, rs], writes=[tq])
                            S.op("act", lambda e, e_=e_, o_=o_: e.activation(o_[:], tq[:], AF.Identity, bias=0.0, scale=sg[:, e_:e_ + 1]),
                                 reads=[tq, sg], writes=[o_])
                            r0 = h * 256 + e_ * 128
                            S.dma("sp", mixT_d[r0:r0 + 128, sl], o_[:], reads=[o_])
                S.barrier()

        def phase_cm(l):
            NB = c.NB
            U0 = c.UV0
            V0 = c.UV0 + c.CM_W
            M0 = c.DA_W + c.RW_W
            CG = c.CM_W // 4
            cbw = min(128, CG)
            with ExitStack() as es:
                rstd = alloc(es, "crstd", [128, T])
                cmg = alloc(es, "cmg", [128, CMC])
                identb = alloc(es, "cidentb", [128, 128], BF16)
                wsb = alloc(es, "wsb", [128, 4, 128], BF16)
                bsr = alloc(es, "bsr", [128, 4, 128])
                xt = [alloc(es, f"cx{i}", [128, TT]) for i in range(3)]
                sq = [alloc(es, f"csq{i}", [128, TT]) for i in range(2)]
                vfull = [alloc(es, f"cv{i}", [128, T]) for i in range(2)]
                zb = [alloc(es, f"czb{i}", [128, T], BF16) for i in range(2)]
                ztok = alloc(es, "ztok", [128, NB, c.CM_W], BF16)
                ut = [alloc(es, f"cu{i}", [128, T]) for i in range(2)]
                ot = [alloc(es, f"co{i}", [128, T]) for i in range(2)]
                ps = [palloc(es, f"cps{i}", [128, TT]) for i in range(2)]
                pstb = [palloc(es, f"cpst{i}", [128, 128], BF16) for i in range(2)]
                pm = [palloc(es, f"cpm{i}", [128, 512]) for i in range(2)]
                S.dma("sp", cmg[:], cmg_d[l], writes=[cmg])
                S.dma("pool", wsb[:], wsT_d[l].rearrange("g q p -> q g p"), writes=[wsb])
                S.dma("sp", bsr[:], bsr_d[l].rearrange("g q p -> q g p"), writes=[bsr])
                S.op("dve", lambda e: e.tensor_copy(identb[:], ident[:]), reads=[ident], writes=[identb])
                k = 0
                for tt in range(NT):
                    p = ps[tt % 2]
                    for ch in range(CMC):
                        x = xt[k % 3]
                        q = sq[k % 2]
                        k += 1
                        S.dma("sp", x[:], projT[V0 + ch * 128:V0 + (ch + 1) * 128, tt * TT:(tt + 1) * TT], writes=[x])
                        S.op("act", lambda e, q=q, x=x: e.activation(q[:], x[:], AF.Square), reads=[x], writes=[q])
                        S.op("pe", lambda e, p=p, q=q, ch=ch: e.matmul(p[:], ones_f[:], q[:], start=(ch == 0), stop=(ch == CMC - 1)),
                             reads=[ones_f, q], writes=[p])
                    sl = slice(tt * TT, (tt + 1) * TT)
                    S.op("act", lambda e, p=p, sl=sl: e.activation(rstd[:, sl], p[:], AF.Sqrt, bias=NORM_EPS, scale=1.0 / c.CM_W),
                         reads=[p], writes=[rstd])
                    S.op("dve", lambda e, sl=sl: e.reciprocal(rstd[:, sl], rstd[:, sl]), reads=[rstd], writes=[rstd])
                k = 0
                for ch in range(CMC):
                    v = vfull[ch % 2]
                    z = zb[ch % 2]
                    S.dma("sp", v[:], projT[V0 + ch * 128:V0 + (ch + 1) * 128, :], writes=[v])
                    S.op("dve", lambda e, v=v: e.tensor_tensor(v[:], v[:], rstd[:], ALU.mult), reads=[v, rstd], writes=[v])
                    S.op("act", lambda e, v=v, z=z, ch=ch: e.activation(z[:], v[:], AF.Identity, bias=0.0, scale=cmg[:, ch:ch + 1]),
                         reads=[v, cmg], writes=[z])
                    for n in range(NB):
                        pb = pstb[k % 2]
                        k += 1
                        S.op("pe", lambda e, pb=pb, z=z, n=n: e.transpose(pb[:], z[:, n * 128:(n + 1) * 128], identb[:]),
                             reads=[z, identb], writes=[pb])
                        if k % 2 == 0:
                            S.op("dve", lambda e, pb=pb, n=n, ch=ch: e.tensor_copy(ztok[:, n, ch * 128:(ch + 1) * 128], pb[:]),
                                 reads=[pb], writes=[ztok])
                        else:
                            S.op("act", lambda e, pb=pb, n=n, ch=ch: e.activation(ztok[:, n, ch * 128:(ch + 1) * 128], pb[:], AF.Copy),
                                 reads=[pb], writes=[ztok])
                bi = 0
                k = 0
                for g in range(4):
                    for sub in range(CG // cbw):
                        cb0 = g * CG + sub * cbw
                        u = ut[bi % 2]
                        o = ot[bi % 2]
                        bi += 1
                        S.dma("sp", u[0:cbw, :], projT[U0 + cb0:U0 + cb0 + cbw, :], writes=[u])
                        for n0 in range(0, NB, 4):
                            p = pm[k % 2]
                            k += 1
                            nn = min(4, NB - n0)
                            for j in range(nn):
                                n = n0 + j
                                S.op("pe", lambda e, p=p, j=j, n=n, cb0=cb0, g=g: e.matmul(
                                    p[0:cbw, j * 128:(j + 1) * 128], ztok[:, n, cb0:cb0 + cbw], wsb[:, g, :], start=True, stop=True),
                                    reads=[ztok, wsb], writes=[p])
                            for j in range(nn):
                                n = n0 + j
                                S.op("dve", lambda e, p=p, j=j, n=n, o=o, g=g: e.tensor_tensor(
                                    o[0:cbw, n * 128:(n + 1) * 128], p[0:cbw, j * 128:(j + 1) * 128], bsr[0:cbw, g, :], ALU.add),
                                    reads=[p, bsr], writes=[o])
                        S.op("pool", lambda e, o=o, u=u: e.tensor_tensor(o[0:cbw, :], o[0:cbw, :], u[0:cbw, :], ALU.mult), reads=[o, u], writes=[o])
                        S.dma("sp", mixT_d[M0 + cb0:M0 + cb0 + cbw, :], o[0:cbw, :], reads=[o])
                S.barrier()


        def phase_rw_pre(l):
            R0 = c.RW0
            with ExitStack() as es:
                mprev = alloc(es, "rmprev", [128, T], BF16)
                mnext = alloc(es, "rmnext", [128, T], BF16)
                cmask = alloc(es, "rcmask", [128, T])
                cw = alloc(es, "rcw", [128, NRC, 3])
                w0 = alloc(es, "rw0", [128, 2, RG])
                a0 = alloc(es, "ra0", [128, 2, RG])
                prm = alloc(es, "rprm", [128, 5, RG])
                bones = alloc(es, "rbones", [128, 128])
                w2 = alloc(es, "rw2", [128, RW_W])
                a2 = alloc(es, "ra2", [128, RW_W])
                g2a = alloc(es, "rg2a", [128, RW_W])
                g2b = alloc(es, "rg2b", [32, RW_W])
                raw = [alloc(es, f"rraw{i}", [128, T]) for i in range(2)]
                xm = alloc(es, "rxm", [128, T])
                tdec = alloc(es, "rtdec", [128, T])
                aac = alloc(es, "raac", [128, T])
                sgl = alloc(es, "rsgl", [128, T])
                sgl2 = alloc(es, "rsgl2", [32, T])
                rt = alloc(es, "rrt", [128, T])
                kt = alloc(es, "rkt", [128, T])
                vt_ = alloc(es, "rvt", [128, T])
                kk = alloc(es, "rkk", [128, T])
                t1 = alloc(es, "rt1", [128, T])
                t2 = alloc(es, "rt2", [128, T])
                t3 = alloc(es, "rt3", [128, T])
                lw = alloc(es, "rlw", [128, T])
                av = alloc(es, "rav", [128, T])
                kdsum = alloc(es, "rkds", [128, T])
                pend = alloc(es, "rpend", [128, NCH])
                ltc = alloc(es, "rltc", [128, NCH, 1])
                ps = [palloc(es, f"rps{i}", [128, TT]) for i in range(4)]
                S.dma("pool", mprev[:], mprev_d, writes=[mprev])
                S.dma("pool", mnext[:], mnext_d, writes=[mnext])
                S.dma("sp", cmask[:], cmask_d, writes=[cmask])
                S.dma("sp", cw[:], rw_cw_d[l], writes=[cw])
                S.dma("sp", w0[:], w0T_d[l], writes=[w0])
                S.dma("sp", a0[:], a0T_d[l], writes=[a0])
                S.dma("sp", prm[:], rwp_d[l], writes=[prm])
                cka = alloc(es, "rcka", [128, RG])
                S.op("dve", lambda e: e.tensor_scalar(cka[:], prm[:, 1, :], -1.0, None, ALU.mult), reads=[prm], writes=[cka])
                S.op("dve", lambda e: e.tensor_scalar(cka[:], cka[:], 1.0, None, ALU.add), reads=[cka], writes=[cka])
                S.dma("sp", bones[:], bones_d, writes=[bones])
                S.dma("sp", w2[:], w2_d[l].rearrange("d r c -> (d r) c"), writes=[w2])
                S.dma("sp", a2[:], a2_d[l].rearrange("d r c -> (d r) c"), writes=[a2])
                S.dma("sp", g2a[:], g2_d[l, 0:128, :], writes=[g2a])
                S.dma("sp", g2b[:], g2_d[l, 128:160, :], writes=[g2b])
                pk = [0]
                rk = [0]

                def conv_rows(row0, nrows, cidx, dst):
                    u = raw[rk[0] % 2]
                    rk[0] += 1
                    n = nrows
                    S.dma("sp", u[0:n, :], projT[row0:row0 + n, :], writes=[u])
                    S.op("act", lambda e: e.activation(dst[0:n, :], u[0:n, :], AF.Identity, bias=0.0, scale=cw[0:n, cidx, 1:2]), reads=[u, cw], writes=[dst])
                    S.op("dve", lambda e: e.tensor_tensor(xm[0:n, 0:T - 1], u[0:n, 0:T - 1], mprev[0:n, 1:T], ALU.mult), reads=[u, mprev], writes=[xm])
                    S.op("dve", lambda e: e.scalar_tensor_tensor(dst[0:n, 1:T], xm[0:n, 0:T - 1], cw[0:n, cidx, 0:1], dst[0:n, 1:T], ALU.mult, ALU.add),
                         reads=[xm, cw, dst], writes=[dst])
                    S.op("dve", lambda e: e.tensor_tensor(xm[0:n, 0:T - 1], u[0:n, 1:T], mnext[0:n, 0:T - 1], ALU.mult), reads=[u, mnext], writes=[xm])
                    S.op("dve", lambda e: e.scalar_tensor_tensor(dst[0:n, 0:T - 1], xm[0:n, 0:T - 1], cw[0:n, cidx, 2:3], dst[0:n, 0:T - 1], ALU.mult, ALU.add),
                         reads=[xm, cw, dst], writes=[dst])

                cdec = 3 * RG
                conv_rows(R0 + 3 * RW_W, 128, cdec, tdec)
                S.op("act", lambda e: e.activation(tdec[:], tdec[:], AF.Tanh), reads=[tdec], writes=[tdec])
                conv_rows(R0 + 3 * RW_W + 128, 128, cdec + 1, aac)
                conv_rows(R0 + 3 * RW_W + 256, 128, cdec + 2, sgl)
                S.op("act", lambda e: e.activation(sgl[:], sgl[:], AF.Sigmoid), reads=[sgl], writes=[sgl])
                conv_rows(R0 + 3 * RW_W + 384, 32, cdec + 3, sgl2)
                S.op("act", lambda e: e.activation(sgl2[:], sgl2[:], AF.Sigmoid), reads=[sgl2], writes=[sgl2])

                def headsum(dst, src):
                    for tt in range(NT):
                        p = ps[pk[0] % 4]
                        pk[0] += 1
                        sl = slice(tt * TT, (tt + 1) * TT)
                        S.op("pe", lambda e, p=p, sl=sl: e.matmul(p[:], bones[:], src[:, sl], start=True, stop=True), reads=[bones, src], writes=[p])
                        S.op("act", lambda e, p=p, sl=sl: e.activation(dst[:, sl], p[:], AF.Copy), reads=[p], writes=[dst])

                for g in range(RG):
                    gs = slice(g * 128, (g + 1) * 128)
                    conv_rows(R0 + g * 128, 128, g, rt)
                    conv_rows(R0 + RW_W + g * 128, 128, RG + g, kt)
                    conv_rows(R0 + 2 * RW_W + g * 128, 128, 2 * RG + g, vt_)
                    S.dma("sp", rwV[gs, :], vt_[:], reads=[vt_])
                    for tt in range(NT):
                        p = ps[pk[0] % 4]
                        pk[0] += 1
                        sl = slice(tt * TT, (tt + 1) * TT)
                        S.op("pe", lambda e, p=p, sl=sl: e.matmul(p[:], g2a[:, gs], sgl[:, sl], start=True, stop=False), reads=[g2a, sgl], writes=[p])
                        S.op("pe", lambda e, p=p, sl=sl: e.matmul(p[:], g2b[:, gs], sgl2[:, sl], start=False, stop=True), reads=[g2b, sgl2], writes=[p])
                        S.op("act", lambda e, p=p, sl=sl: e.activation(t1[:, sl], p[:], AF.Copy), reads=[p], writes=[t1])
                    S.dma("sp", rwGate[gs, :], t1[:], reads=[t1])
                    S.op("pool", lambda e: e.tensor_scalar(kk[:], kt[:], prm[:, 0, g:g + 1], None, ALU.mult), reads=[kt, prm], writes=[kk])
                    S.op("act", lambda e: e.activation(t2[:], kk[:], AF.Square), reads=[kk], writes=[t2])
                    headsum(t3, t2)
                    S.op("act", lambda e: e.activation(t3[:], t3[:], AF.Sqrt, bias=1e-12, scale=1.0), reads=[t3], writes=[t3])
                    S.op("dve", lambda e: e.reciprocal(t3[:], t3[:]), reads=[t3], writes=[t3])
                    S.op("dve", lambda e: e.tensor_tensor(kk[:], kk[:], t3[:], ALU.mult), reads=[kk, t3], writes=[kk])
                    for d in range(2):
                        ds_ = slice(d * 64, (d + 1) * 64)
                        for tt in range(NT):
                            sl = slice(tt * TT, (tt + 1) * TT)
                            p = ps[pk[0] % 4]
                            pk[0] += 1
                            S.op("pe", lambda e, p=p, sl=sl: e.matmul(p[:], w2[ds_, gs], tdec[ds_, sl], start=True, stop=True), reads=[w2, tdec], writes=[p])
                            S.op("act", lambda e, p=p, sl=sl: e.activation(lw[:, sl], p[:], AF.Sigmoid, bias=w0[:, d, g:g + 1], scale=1.0),
                                 reads=[p, w0], writes=[lw])
                            p = ps[pk[0] % 4]
                            pk[0] += 1
                            S.op("pe", lambda e, p=p, sl=sl: e.matmul(p[:], a2[ds_, gs], aac[ds_, sl], start=True, stop=True), reads=[a2, aac], writes=[p])
                            S.op("act", lambda e, p=p, sl=sl: e.activation(av[:, sl], p[:], AF.Sigmoid, bias=a0[:, d, g:g + 1], scale=1.0),
                                 reads=[p, a0], writes=[av])
                        S.op("pool", lambda e: e.tensor_scalar(lw[:], lw[:], -math.exp(-0.5), None, ALU.mult), reads=[lw], writes=[lw])
                        S.op("act", lambda e: e.activation(t1[:], av[:], AF.Identity, bias=cka[:, g:g + 1], scale=prm[:, 1, g:g + 1]),
                             reads=[av, prm, cka], writes=[t1])
                        S.op("dve", lambda e: e.tensor_tensor(t1[:], t1[:], kt[:], ALU.mult), reads=[t1, kt], writes=[t1])
                        if d == 0:
                            S.op("pool", lambda e: e.tensor_copy(kdsum[:], t1[:]), reads=[t1], writes=[kdsum])
                        else:
                            S.op("pool", lambda e: e.tensor_tensor(kdsum[:], kdsum[:], t1[:], ALU.add), reads=[t1, kdsum], writes=[kdsum])
                        S.op("dve", lambda e: e.tensor_tensor_scan(t2[:], cmask[:], lw[:], 0.0, ALU.mult, ALU.add), reads=[cmask, lw], writes=[t2])
                        S.op("dve", lambda e: e.tensor_copy(ltc[:], t2[:].rearrange("p (n c) -> p n c", c=CK)[:, :, CK - 1:CK]), reads=[t2], writes=[ltc])
                        ltot = ltc[:]
                        S.op("act", lambda e: e.activation(pend[:].rearrange("p (n o) -> p n o", o=1), ltot, AF.Exp), reads=[ltc], writes=[pend])
                        S.dma("sp", rwP[d][gs, :], pend[:], reads=[pend])
                        if d == 1:
                            S.op("dve", lambda e: e.tensor_tensor(t3[:], lw[:], t2[:], ALU.subtract), reads=[lw, t2], writes=[t3])
                            S.op("dve", lambda e: e.tensor_tensor(
                                t2[:].rearrange("p (n c) -> p n c", c=CK), t3[:].rearrange("p (n c) -> p n c", c=CK),
                                ltot.to_broadcast([128, NCH, CK]), ALU.add), reads=[t3, ltc], writes=[t2])
                        S.op("act", lambda e: e.activation(t3[:], t2[:], AF.Exp), reads=[t2], writes=[t3])
                        S.op("dve", lambda e: e.tensor_tensor(t3[:], t3[:], rt[:], ALU.mult), reads=[t3, rt], writes=[t3])
                        S.dma("sp", rwR[d][gs, :], t3[:], reads=[t3])
                        S.op("act", lambda e: e.activation(t3[:], t2[:], AF.Exp, scale=-1.0), reads=[t2], writes=[t3])
                        S.op("dve", lambda e: e.tensor_tensor(t1[:], t1[:], t3[:], ALU.mult), reads=[t3, t1], writes=[t1])
                        S.dma("sp", rwK[d][gs, :], t1[:], reads=[t1])
                        S.op("dve", lambda e: e.tensor_tensor(t3[:], t3[:], kk[:], ALU.mult), reads=[t3, kk], writes=[t3])
                        S.op("dve", lambda e: e.tensor_tensor(t3[:], t3[:], av[:], ALU.mult), reads=[t3, av], writes=[t3])
                        S.dma("sp", rwB[d][gs, :], t3[:], reads=[t3])
                        S.op("dve", lambda e: e.tensor_tensor(t2[:], t2[:], lw[:], ALU.subtract), reads=[t2, lw], writes=[t2])
                        S.op("act", lambda e: e.activation(t2[:], t2[:], AF.Exp), reads=[t2], writes=[t2])
                        S.op("dve", lambda e: e.tensor_tensor(t2[:], t2[:], kk[:], ALU.mult), reads=[t2, kk], writes=[t2])
                        S.dma("sp", rwA[d][gs, :], t2[:], reads=[t2])
                    S.op("dve", lambda e: e.scalar_tensor_tensor(kdsum[:], kdsum[:], prm[:, 2, g:g + 1], rt[:], ALU.mult, ALU.mult),
                         reads=[kdsum, prm, rt], writes=[kdsum])
                    headsum(t1, kdsum)
                    S.op("dve", lambda e: e.tensor_tensor(t1[:], t1[:], vt_[:], ALU.mult), reads=[t1, vt_], writes=[t1])
                    S.dma("sp", rwBon[gs, :], t1[:], reads=[t1])
                S.barrier()

        def phase_rw_scan(l):
            HB = min(8, RH)
            NU = RH // HB
            with ExitStack() as es:
                tri = alloc(es, "tri", [64, 4, 64])
                keep = alloc(es, "keepc", [128, 2, NCH])
                S.dma("sp", tri[:], tri_d, writes=[tri])
                S.dma("sp", keep[:], keep_d, writes=[keep])
                SU, U_, SL, LW = 0, 1, 2, 3
                mk = {0: dict(N=SU, NT=SL, G=SU, H=U_), 1: dict(N=SL, NT=SU, G=SL, H=LW)}
                D_ = {}
                for d in range(2):
                    t = {}
                    for nm in ("A", "B", "K", "R", "V", "Kt", "Bt", "Vt", "N", "NT", "G", "Hk", "Hb", "X", "Pa", "PaT", "Pb", "PbT",
                               "XT", "nS", "Y", "ST"):
                        t[nm] = alloc(es, f"s{nm}{d}", [64, RH, 64])
                    t["P"] = alloc(es, f"sP{d}", [64, RH, NCH])
                    D_[d] = t
                    S.dma("sp", t["ST"][:], s0T_d[l, d].rearrange("g (j k) v -> k (g j) v", j=2), writes=[t["ST"]])
                    S.dma("sp", t["P"][:], rwP[d].rearrange("(h k) n -> k h n", k=64), writes=[t["P"]])
                psT = palloc(es, "spsT", [64, 512])
                ps1 = [palloc(es, f"sps1{i}", [64, 512]) for i in range(2)]
                ps2 = [palloc(es, f"sps2{i}", [64, 512]) for i in range(2)]
                ps3 = [palloc(es, f"sps3{i}", [64, 512]) for i in range(3)]
                ctr = dict(p1=0, p2=0, p3=0, ev=0)
                I64 = ident[0:64, 0:64]

                def flat(ap):
                    return ap.rearrange("p h c -> p (h c)")

                def evac(dst_ap, dst_t, p, cols):
                    ctr["ev"] += 1
                    if ctr["ev"] % 2 == 0:
                        S.op("act", lambda e: e.activation(dst_ap, p[0:64, 0:cols], AF.Copy), reads=[p], writes=[dst_t])
                    else:
                        S.op("dve", lambda e: e.tensor_copy(dst_ap, p[0:64, 0:cols]), reads=[p], writes=[dst_t])

                def hview(tile_, u):
                    return tile_[:, u * HB:(u + 1) * HB, :]

                def pview(p):
                    return p[0:64, 0:HB * 64].rearrange("p (h c) -> p h c", c=64)

                def load_chunk(d, n):
                    t = D_[d]
                    cs = slice(n * CK, (n + 1) * CK)
                    for nm, src in (("A", rwA[d]), ("B", rwB[d]), ("K", rwK[d]), ("R", rwR[d]), ("V", rwV)):
                        S.dma("sp", t[nm][:], src[:, cs].rearrange("(h k) t -> k h t", k=64), writes=[t[nm]])

                order = {0: list(range(NCH)), 1: list(range(NCH - 1, -1, -1))}
                for step in range(NCH):
                    for d in range(2):
                        t = D_[d]
                        n = order[d][step]
                        load_chunk(d, n)
                        A, B, K_, R, V = (t[x] for x in ("A", "B", "K", "R", "V"))
                        m = mk[d]
                        S.op("dve", lambda e, t=t, n=n, d=d: e.tensor_scalar(flat(t["ST"][:]), flat(t["ST"][:]), keep[0:64, d, n:n + 1], None, ALU.mult),
                             reads=[t["ST"], keep], writes=[t["ST"]])
                        for u in range(NU):
                            hs = [u * HB + i for i in range(HB)]
                            for nm_src, nm_dst in ((K_, "Kt"), (B, "Bt"), (V, "Vt")):
                                for i, hd in enumerate(hs):
                                    S.op("pe", lambda e, i=i, hd=hd, nm_src=nm_src: e.transpose(
                                        psT[0:64, i * 64:(i + 1) * 64], nm_src[:, hd, :], I64), reads=[nm_src, ident], writes=[psT])
                                evac(flat(hview(t[nm_dst], u)), t[nm_dst], psT, HB * 64)
                            for (dst, lt, rt_, mask) in (("N", B, A, m["N"]), ("NT", A, B, m["NT"]), ("G", K_, A, m["G"]),
                                                          ("Hk", K_, R, m["H"]), ("Hb", B, R, m["H"])):
                                p = ps1[ctr["p1"] % 2]
                                ctr["p1"] += 1
                                for i, hd in enumerate(hs):
                                    S.op("pe", lambda e, p=p, i=i, hd=hd, lt=lt, rt_=rt_: e.matmul(
                                        p[0:64, i * 64:(i + 1) * 64], lt[:, hd, :], rt_[:, hd, :], start=True, stop=True),
                                        reads=[lt, rt_], writes=[p])
                                S.op("dve", lambda e, p=p, dst=dst, mask=mask, t=t, u=u: e.tensor_tensor(
                                    hview(t[dst], u), pview(p), tri[:, mask:mask + 1, :].to_broadcast([64, HB, 64]), ALU.mult),
                                    reads=[p, tri], writes=[t[dst]])
                            S.op("dve", lambda e, t=t, u=u: e.scalar_tensor_tensor(
                                hview(t["X"], u), hview(t["N"], u), -1.0, I64.unsqueeze(1).to_broadcast([64, HB, 64]), ALU.mult, ALU.add),
                                reads=[t["N"], ident], writes=[t["X"]])
                            cur, curT = "N", "NT"
                            nlev = 5
                            for lev in range(nlev):
                                nxt, nxtT = ("Pa", "PaT") if lev % 2 == 0 else ("Pb", "PbT")
                                p = ps2[ctr["p2"] % 2]
                                ctr["p2"] += 1
                                for i, hd in enumerate(hs):
                                    S.op("pe", lambda e, p=p, i=i, hd=hd, t=t, cur=cur, curT=curT: e.matmul(
                                        p[0:64, i * 64:(i + 1) * 64], t[cur][:, hd, :], t[curT][:, hd, :], start=True, stop=True),
                                        reads=[t[cur], t[curT]], writes=[p])
                                evac(flat(hview(t[nxtT], u)), t[nxtT], p, HB * 64)
                                if lev < nlev - 1:
                                    p = ps2[ctr["p2"] % 2]
                                    ctr["p2"] += 1
                                    for i, hd in enumerate(hs):
                                        S.op("pe", lambda e, p=p, i=i, hd=hd, t=t, cur=cur, curT=curT: e.matmul(
                                            p[0:64, i * 64:(i + 1) * 64], t[curT][:, hd, :], t[cur][:, hd, :], start=True, stop=True),
                                            reads=[t[cur], t[curT]], writes=[p])
                                    evac(flat(hview(t[nxt], u)), t[nxt], p, HB * 64)
                                p = ps2[ctr["p2"] % 2]
                                ctr["p2"] += 1
                                for i, hd in enumerate(hs):
                                    S.op("pe", lambda e, p=p, i=i, hd=hd, t=t, nxtT=nxtT: e.matmul(
                                        p[0:64, i * 64:(i + 1) * 64], t[nxtT][:, hd, :], t["X"][:, hd, :], start=True, stop=True),
                                        reads=[t[nxtT], t["X"]], writes=[p])
                                S.op("dve", lambda e, p=p, t=t, u=u: e.tensor_tensor(
                                    hview(t["X"], u), hview(t["X"], u), pview(p), ALU.add), reads=[p, t["X"]], writes=[t["X"]])
                                cur, curT = nxt, nxtT
                            p = ps3[ctr["p3"] % 3]
                            ctr["p3"] += 1
                            for i, hd in enumerate(hs):
                                S.op("pe", lambda e, p=p, i=i, hd=hd, t=t: e.matmul(
                                    p[0:64, i * 64:(i + 1) * 64], A[:, hd, :], t["ST"][:, hd, :], start=True, stop=False),
                                    reads=[A, t["ST"]], writes=[p])
                                S.op("pe", lambda e, p=p, i=i, hd=hd, t=t: e.matmul(
                                    p[0:64, i * 64:(i + 1) * 64], t["G"][:, hd, :], t["Vt"][:, hd, :], start=False, stop=True),
                                    reads=[t["G"], t["Vt"]], writes=[p])
                            evac(flat(hview(t["XT"], u)), t["XT"], p, HB * 64)
                            p = ps3[ctr["p3"] % 3]
                            ctr["p3"] += 1
                            for i, hd in enumerate(hs):
                                S.op("pe", lambda e, p=p, i=i, hd=hd, t=t: e.matmul(
                                    p[0:64, i * 64:(i + 1) * 64], t["X"][:, hd, :], t["XT"][:, hd, :], start=True, stop=True),
                                    reads=[t["X"], t["XT"]], writes=[p])
                            S.op("dve", lambda e, p=p, t=t, u=u: e.tensor_scalar(
                                flat(hview(t["nS"], u)), p[0:64, 0:HB * 64], -1.0, None, ALU.mult), reads=[p], writes=[t["nS"]])
                            p = ps3[ctr["p3"] % 3]
                            ctr["p3"] += 1
                            for i, hd in enumerate(hs):
                                S.op("pe", lambda e, p=p, i=i, hd=hd, t=t: e.matmul(
                                    p[0:64, i * 64:(i + 1) * 64], t["ST"][:, hd, :], R[:, hd, :], start=True, stop=False),
                                    reads=[t["ST"], R], writes=[p])
                                S.op("pe", lambda e, p=p, i=i, hd=hd, t=t: e.matmul(
                                    p[0:64, i * 64:(i + 1) * 64], t["Vt"][:, hd, :], t["Hk"][:, hd, :], start=False, stop=False),
                                    reads=[t["Vt"], t["Hk"]], writes=[p])
                                S.op("pe", lambda e, p=p, i=i, hd=hd, t=t: e.matmul(
                                    p[0:64, i * 64:(i + 1) * 64], t["nS"][:, hd, :], t["Hb"][:, hd, :], start=False, stop=True),
                                    reads=[t["nS"], t["Hb"]], writes=[p])
                            evac(flat(hview(t["Y"], u)), t["Y"], p, HB * 64)
                            p = ps3[ctr["p3"] % 3]
                            ctr["p3"] += 1
                            for i, hd in enumerate(hs):
                                S.op("pe", lambda e, p=p, i=i, hd=hd, t=t: e.matmul(
                                    p[0:64, i * 64:(i + 1) * 64], t["Kt"][:, hd, :], t["Vt"][:, hd, :], start=True, stop=False),
                                    reads=[t["Kt"], t["Vt"]], writes=[p])
                                S.op("pe", lambda e, p=p, i=i, hd=hd, t=t: e.matmul(
                                    p[0:64, i * 64:(i + 1) * 64], t["Bt"][:, hd, :], t["nS"][:, hd, :], start=False, stop=True),
                                    reads=[t["Bt"], t["nS"]], writes=[p])
                            S.op("dve", lambda e, p=p, t=t, u=u: e.tensor_tensor(hview(t["ST"], u), hview(t["ST"], u), pview(p), ALU.add),
                                 reads=[p, t["ST"]], writes=[t["ST"]])
                            S.op("dve", lambda e, t=t, u=u, n=n: e.tensor_tensor(
                                hview(t["ST"], u), hview(t["ST"], u), t["P"][:, u * HB:(u + 1) * HB, n:n + 1].to_broadcast([64, HB, 64]), ALU.mult),
                                reads=[t["P"], t["ST"]], writes=[t["ST"]])
                        cs = slice(n * CK, (n + 1) * CK)
                        S.dma("sp", rwY[d][:, cs].rearrange("(h v) t -> v h t", v=64), t["Y"][:], reads=[t["Y"]])
                        tpos = (n + 1) * CK if d == 0 else n * CK
                        if tpos % c.SEQ == 0:
                            seg = tpos // c.SEQ - 1 if d == 0 else tpos // c.SEQ
                            S.dma("sp", statesT[seg, l, d].rearrange("g (j k) v -> k (g j) v", j=2), t["ST"][:], reads=[t["ST"]])
                S.barrier()

        def phase_rw_post(l):
            with ExitStack() as es:
                prm = alloc(es, "qprm", [128, 5, RG])
                bones = alloc(es, "qbones", [128, 128])
                y0 = [alloc(es, f"qy0{i}", [128, T]) for i in range(2)]
                y1 = [alloc(es, f"qy1{i}", [128, T]) for i in range(2)]
                bon = [alloc(es, f"qbon{i}", [128, T]) for i in range(2)]
                gat = [alloc(es, f"qgat{i}", [128, T]) for i in range(2)]
                mu = alloc(es, "qmu", [128, T])
                sq = alloc(es, "qsq", [128, T])
                ps = [palloc(es, f"qps{i}", [128, TT]) for i in range(4)]
                S.dma("sp", prm[:], rwp_d[l], writes=[prm])
                S.dma("sp", bones[:], bones_d, writes=[bones])
                pk = 0
                for g in range(RG):
                    gs = slice(g * 128, (g + 1) * 128)
                    a, b, bo, ga = y0[g % 2], y1[g % 2], bon[g % 2], gat[g % 2]
                    S.dma("sp", a[:], rwY[0][gs, :], writes=[a])
                    S.dma("sp", b[:], rwY[1][gs, :], writes=[b])
                    S.dma("sp", bo[:], rwBon[gs, :], writes=[bo])
                    S.dma("sp", ga[:], rwGate[gs, :], writes=[ga])
                    S.op("dve", lambda e, a=a, b=b: e.tensor_tensor(a[:], a[:], b[:], ALU.add), reads=[a, b], writes=[a])
                    for tt in range(NT):
                        sl = slice(tt * TT, (tt + 1) * TT)
                        p = ps[pk % 4]
                        pk += 1
                        S.op("pe", lambda e, p=p, a=a, sl=sl: e.matmul(p[:], bones[:], a[:, sl], start=True, stop=True), reads=[bones, a], writes=[p])
                        S.op("act", lambda e, p=p, sl=sl: e.activation(mu[:, sl], p[:], AF.Identity, bias=0.0, scale=-1.0 / 64), reads=[p], writes=[mu])
                    S.op("dve", lambda e, a=a: e.tensor_tensor(a[:], a[:], mu[:], ALU.add), reads=[a, mu], writes=[a])
                    S.op("act", lambda e, a=a: e.activation(sq[:], a[:], AF.Square), reads=[a], writes=[sq])
                    for tt in range(NT):
                        sl = slice(tt * TT, (tt + 1) * TT)
                        p = ps[pk % 4]
                        pk += 1
                        S.op("pe", lambda e, p=p, sl=sl: e.matmul(p[:], bones[:], sq[:, sl], start=True, stop=True), reads=[bones, sq], writes=[p])
                        S.op("act", lambda e, p=p, sl=sl: e.activation(mu[:, sl], p[:], AF.Sqrt, bias=64e-5, scale=1.0 / 64), reads=[p], writes=[mu])
                    S.op("dve", lambda e: e.reciprocal(mu[:], mu[:]), reads=[mu], writes=[mu])
                    S.op("dve", lambda e, a=a: e.tensor_tensor(a[:], a[:], mu[:], ALU.mult), reads=[a, mu], writes=[a])
                    S.op("act", lambda e, a=a, g=g: e.activation(a[:], a[:], AF.Identity, bias=prm[:, 4, g:g + 1], scale=prm[:, 3, g:g + 1]),
                         reads=[a, prm], writes=[a])
                    S.op("dve", lambda e, a=a, bo=bo: e.tensor_tensor(a[:], a[:], bo[:], ALU.add), reads=[a, bo], writes=[a])
                    S.op("pool", lambda e, a=a, ga=ga: e.tensor_tensor(a[:], a[:], ga[:], ALU.mult), reads=[a, ga], writes=[a])
                    S.dma("sp", mixT_d[c.DA_W + g * 128:c.DA_W + (g + 1) * 128, :], a[:], reads=[a])
                S.barrier()

        def phase_rw_zero(l):
            with ExitStack() as es:
                z = alloc(es, "zz", [128, T])
                S.op("dve", lambda e: e.memset(z[:], 0.0), writes=[z])
                for g in range(c.RG):
                    S.dma("sp", mixT_d[c.DA_W + g * 128:c.DA_W + (g + 1) * 128, :], z[:], reads=[z])
                S.barrier()

        phase_mod()
        src = xT_in
        for l in range(L):
            if stop <= 1:
                break
            with ExitStack() as es:
                hT = alloc(es, "hT", [128, DC, T], BF16)
                phase_norm(es, src, a1[l], (l, SH1), hT)
                if stop <= 2:
                    break
                phase_proj(l, hT)
            if stop <= 3:
                break
            phase_attn(l)
            phase_cm(l)
            if rw:
                phase_rw_pre(l)
                if rwstop <= 1:
                    break
                phase_rw_scan(l)
                if rwstop <= 2:
                    break
                phase_rw_post(l)
            else:
                phase_rw_zero(l)
            with ExitStack() as es:
                mixT = alloc(es, "mixT", [128, DC, T], BF16)
                for ch in range(DC):
                    S.dma("pool", mixT[:, ch, :], mixT_d[ch * 128:(ch + 1) * 128, :], writes=[mixT])
                phase_wout(l, mixT, src, xs[0])
            if stop <= 4:
                break
            if debug and l == 0:
                with ExitStack() as es:
                    tmp = alloc(es, "dbgx", [128, T])
                    for ch in range(DC):
                        S.dma("sp", tmp[:], xs[0][ch * 128:(ch + 1) * 128, :], writes=[tmp])
                        S.dma("sp", dbg["x1"][ch * 128:(ch + 1) * 128, :], tmp[:], reads=[tmp])
                    S.barrier()
            with ExitStack() as es:
                hT = alloc(es, "hT2", [128, DC, T], BF16)
                phase_norm(es, xs[0], a2[l], (l, SH2), hT)
                if stop <= 5:
                    break
                phase_ffn_up(l, hT)
            if stop <= 6:
                break
            phase_ffn_down(l, xs[0], xs[1])
            src = xs[1]
        if stop > 7 and rwstop >= 9:
            phase_norm(None, src, fg, None, None, dst_dram=yT)
        S.barrier()
        es_glob.close()

    for si in range(NS):
        cur[0] = si
        emit_stream()
    return nc, S


def _pm(v, nchunk):
    return np.ascontiguousarray(np.asarray(v, np.float32).reshape(nchunk, 128).T)


def make_streams(cfg, BATCH, DEC_BATCH):
    st = [("lat", b) for b in range(DEC_BATCH)]
    per = cfg.T // cfg.SEQ
    for i in range(0, BATCH, per):
        st.append(("ctx", list(range(i, i + per))))
    return st


def prep_inputs(cfg, inp, core_streams):
    c = cfg
    L, DC, T = c.L, c.DC, c.T
    f32 = np.float32
    shared = {}
    shared["ident"] = np.eye(128, dtype=f32)
    shared["mod_w"] = np.ascontiguousarray(inp["mod_w"], f32)
    shared["mod_bT"] = np.stack([_pm(inp["mod_b"][l], 6 * DC) for l in range(L)])
    shared["n1gT"] = np.stack([_pm(inp["norm1_g"][l], DC) for l in range(L)])
    shared["n2gT"] = np.stack([_pm(inp["norm2_g"][l], DC) for l in range(L)])
    shared["fgT"] = _pm(inp["final_norm_g"], DC)
    shared["w_in"] = np.ascontiguousarray(inp["w_in"], f32)
    shared["w_out"] = np.ascontiguousarray(inp["w_out"], f32)
    shared["ffn_up"] = np.ascontiguousarray(inp["ffn_up"], f32)
    shared["ffn_down"] = np.ascontiguousarray(inp["ffn_down"], f32)
    cw = np.asarray(inp["ffn_conv_w"], f32)
    shared["ffn_cw"] = np.ascontiguousarray(cw.reshape(L, 3, 2 * c.KF, 128).transpose(0, 3, 2, 1))
    shared["ffn_cb"] = np.stack([_pm(inp["ffn_conv_b"][l], 2 * c.KF) for l in range(L)])
    shared["lam_rep"] = np.ascontiguousarray(np.broadcast_to(
        np.asarray(inp["da_lambda"], f32).reshape(L, 1, 512), (L, 128, 512)))
    shared["subg"] = np.stack([_pm(inp["da_subln_g"][l], 2) for l in range(L)])
    shared["cmg"] = np.stack([_pm(inp["cm_norm_g"][l], c.CM_W // 128) for l in range(L)])
    shared["wsT"] = np.ascontiguousarray(np.asarray(inp["cm_ws"], f32).transpose(0, 1, 3, 2))
    shared["bs_rep"] = np.ascontiguousarray(np.broadcast_to(
        np.asarray(inp["cm_bs"], f32)[:, :, None, :], (L, 4, 128, 128)))
    RG, RW_W, RW_IN = c.RG, c.RW_W, c.RW_IN
    NRC = (RW_IN + 127) // 128
    CK = 64
    NCH = T // CK
    rcw = np.zeros((L, 3, NRC * 128), f32)
    rcw[:, :, :RW_IN] = np.asarray(inp["rw_conv_w"], f32)
    shared["rw_cw"] = np.ascontiguousarray(rcw.reshape(L, 3, NRC, 128).transpose(0, 3, 2, 1))
    shared["w0T"] = np.ascontiguousarray(np.asarray(inp["rw_w0"], f32).reshape(L, 2, RG, 128).transpose(0, 3, 1, 2))
    shared["a0T"] = np.ascontiguousarray(np.asarray(inp["rw_a0"], f32).reshape(L, 2, RG, 128).transpose(0, 3, 1, 2))
    shared["rw_w2"] = np.ascontiguousarray(inp["rw_w2"], f32)
    shared["rw_a2"] = np.ascontiguousarray(inp["rw_a2"], f32)
    shared["rw_g2"] = np.ascontiguousarray(inp["rw_g2"], f32)
    prm = np.stack([np.asarray(inp[k], f32).reshape(L, RW_W) for k in ("rw_k_k", "rw_k_a", "rw_r_k", "rw_gn_g", "rw_gn_b")], axis=1)
    shared["rwp"] = np.ascontiguousarray(prm.reshape(L, 5, RG, 128).transpose(0, 3, 1, 2))
    bo = np.zeros((128, 128), f32)
    bo[:64, :64] = 1.0
    bo[64:, 64:] = 1.0
    shared["bones"] = bo
    tt_ = np.arange(T)
    shared["cmask"] = np.ascontiguousarray(np.broadcast_to((tt_ % CK != 0).astype(f32), (128, T)))
    ii = np.arange(64)
    tri = np.stack([(ii[:, None] < ii[None, :]), (ii[:, None] <= ii[None, :]),
                    (ii[:, None] > ii[None, :]), (ii[:, None] >= ii[None, :])], axis=1).astype(f32)
    shared["tri"] = np.ascontiguousarray(tri)
    maps = []
    for streams in core_streams:
        cm = dict(shared)
        for si, (kind, ident_) in enumerate(streams):
            m = _prep_stream(cfg, inp, kind, ident_)
            for k_, v_ in m.items():
                cm[f"{k_}_{si}"] = v_
        maps.append(cm)
    return maps


def _prep_stream(cfg, inp, kind, ident_):
    c = cfg
    L, DC, T = c.L, c.DC, c.T
    f32 = np.float32
    RG = c.RG
    CK = 64
    NCH = T // CK
    if True:
        m = {}
        if kind == "lat":
            b = ident_
            m["xT"] = np.ascontiguousarray(np.asarray(inp["x_sample"][b], f32).T)
            m["condT"] = _pm(inp["c"][b], DC)
            seqlen = T
        else:
            xs_ = np.concatenate([np.asarray(inp["x_prompt"][s], f32) for s in ident_], axis=0)
            m["xT"] = np.ascontiguousarray(xs_.T)
            m["condT"] = _pm(inp["c_ctx"], DC)
            seqlen = c.SEQ
        t = np.arange(T)
        mp = (t % seqlen != 0).astype(f32)
        mn = (t % seqlen != seqlen - 1).astype(f32)
        m["mprev"] = np.ascontiguousarray(np.broadcast_to(mp, (128, T)))
        m["mnext"] = np.ascontiguousarray(np.broadcast_to(mn, (128, T)))
        qa = np.zeros((8, T), f32)
        ka = np.zeros((8, c.NK), f32)
        sid = t // seqlen
        qa[sid, t] = MASK_BIG
        ka[sid, t] = 1.0
        rc = np.ones((128, T), np.float64)
        rsn = np.zeros((128, T), np.float64)
        if kind == "lat":
            ka[0, T:] = 1.0
            kc = np.asarray(inp["cache_da_k"][b], f32)
            m["kcT"] = np.ascontiguousarray(kc.transpose(0, 2, 3, 4, 1))
            vcc = np.asarray(inp["cache_da_v"][b], f32)
            m["vc"] = np.ascontiguousarray(vcc.transpose(0, 2, 1, 3))
            p = np.arange(128)
            i = p % 64
            j = i % 32
            inv = 10000.0 ** (-j.astype(np.float64) / 32)
            row = (t // c.GRID_W).astype(np.float64)
            col = (t % c.GRID_W).astype(np.float64)
            pos = np.where((p < 64)[:, None], row[None, :], col[None, :])
            ang = (pos.astype(np.float32) * inv.astype(np.float32)[:, None]).astype(np.float32)
            rc = np.cos(ang)
            sn = np.sin(ang)
            rsn = np.where((i < 32)[:, None], -sn, sn)
        else:
            m["kcT"] = np.zeros((L, c.H, 2, 128, c.PAST), f32)
            m["vc"] = np.zeros((L, c.H, c.PAST, 256), f32)
        m["qaug"] = qa
        m["kaug"] = ka
        kp = np.ones((2, NCH), f32)
        if kind == "lat":
            st = np.asarray(inp["state_rwkv"][b], f32)
            m["s0T"] = np.ascontiguousarray(st.transpose(0, 1, 2, 4, 3).reshape(L, 2, RG, 128, 64))
        else:
            m["s0T"] = np.zeros((L, 2, RG, 128, 64), f32)
            cpos = np.arange(NCH) * CK
            kp[0] = (cpos % c.SEQ != 0)
            kp[1] = ((cpos + CK) % c.SEQ != 0)
        m["keepc"] = np.ascontiguousarray(np.broadcast_to(kp, (128, 2, NCH)))
        m["ropeC"] = np.ascontiguousarray(rc, f32)
        m["ropeS"] = np.ascontiguousarray(rsn, f32)
    return m


def assemble(cfg, results, core_streams, BATCH, DEC_BATCH):
    c = cfg
    L, T, D = c.L, c.T, c.D
    f32 = np.float32
    y_prompt = np.zeros((BATCH, c.SEQ, D), f32)
    y_sample = np.zeros((DEC_BATCH, T, D), f32)
    new_k = np.zeros((BATCH, L, c.SEQ, c.H, 2, 128), f32)
    new_v = np.zeros((BATCH, L, c.SEQ, c.H, 256), f32)
    new_s = np.zeros((BATCH, L, 2, c.RH, 64, 64), f32)
    for ci, streams in enumerate(core_streams):
      for si, (kind, ident_) in enumerate(streams):
        r = {k_[:-len(f"_{si}")]: v_ for k_, v_ in results[ci].items() if k_.endswith(f"_{si}")}
        if kind == "lat":
            y_sample[ident_] = r["yT"].T
        else:
            y = r["yT"].T
            kv = r["kvT"]
            for j, s in enumerate(ident_):
                tsl = slice(j * c.SEQ, (j + 1) * c.SEQ)
                y_prompt[s] = y[tsl]
                for l in range(L):
                    new_k[s, l] = kv[l, 0:c.DA_W, tsl].T.reshape(c.SEQ, c.H, 2, 128)
                    new_v[s, l] = kv[l, c.DA_W:2 * c.DA_W, tsl].T.reshape(c.SEQ, c.H, 256)
                st = r["statesT"][j]
                new_s[s] = st.reshape(L, 2, c.RH, 64, 64).transpose(0, 1, 2, 4, 3)
    return y_prompt, y_sample, new_k, new_v, new_s


def plan_cores(streams, n_cores):
    ns = (len(streams) + n_cores - 1) // n_cores
    cs = []
    for ci in range(n_cores):
        cs.append([streams[min(ci * ns + j, len(streams) - 1)] for j in range(ns)])
    return cs, ns


N_CORES = 6


def kernel(**inputs):
    cfg = Cfg()
    inp = {k: np.asarray(v) for k, v in inputs.items()}
    BATCH = inp["x_prompt"].shape[0]
    DEC_BATCH = inp["x_sample"].shape[0]
    streams = make_streams(cfg, BATCH, DEC_BATCH)
    core_streams, ns = plan_cores(streams, N_CORES)
    maps = prep_inputs(cfg, inp, core_streams)
    nc, _ = build(cfg, NS=ns)
    res = run_bass_kernel_spmd(nc, maps, core_ids=list(range(N_CORES)))
    return assemble(cfg, res.results, core_streams, BATCH, DEC_BATCH)
```

```python
import math
from contextlib import ExitStack
import numpy as np
import concourse.bass as bass
import concourse.mybir as mybir
from concourse.bass_utils import run_bass_kernel_spmd

F32 = mybir.dt.float32
BF16 = mybir.dt.bfloat16
AF = mybir.ActivationFunctionType
ALU = mybir.AluOpType
AX = mybir.AxisListType

EPOCH = 30000
SAME_ENGINE_SYNC = True


class Buf:
    __slots__ = ("name", "w", "r")

    def __init__(self, name=""):
        self.name = name
        self.w = None
        self.r = {}


class Tile:
    def __init__(self, t, name=""):
        self.t = t
        self.b = Buf(name)

    def __getitem__(self, k):
        return self.t[k]


def _bufs(xs):
    out = []
    for x in xs:
        if x is None:
            continue
        out.append(x.b if isinstance(x, Tile) else x)
    return out


class Sched:
    def __init__(self, nc, ndma=8):
        self.nc = nc
        self.eng = {"pe": nc.tensor, "dve": nc.vector, "act": nc.scalar,
                    "pool": nc.gpsimd, "sp": nc.sync}
        self.csem = {}
        self.cnt = {}
        self.epoch = {}
        self.seen = {e: {} for e in self.eng}
        for e in self.eng:
            self.epoch[e] = 0
            self.cnt[e] = 0
            self.csem[(e, 0)] = nc.alloc_semaphore(f"c_{e}_0")
        self.ndma = ndma
        self.dsem = {}
        self.dval = {}
        self.dnext = {}
        self.n_instr = 0
        self.pending = {}

    def _wait(self, e, tok):
        if tok is None:
            return
        if tok[0] == "c":
            _, fe, ep, c = tok
            if fe == e and (e == "pe" or not SAME_ENGINE_SYNC):
                return
            key = ("c", fe, ep)
            if self.seen[e].get(key, 0) >= c:
                return
            self.eng[e].wait_ge(self.csem[(fe, ep)], c)
            self.seen[e][key] = c
        else:
            _, q, i, v = tok
            key = ("d", q, i)
            if self.seen[e].get(key, 0) >= v:
                return
            self.eng[e].wait_ge(self.dsem[(q, i)], v)
            self.seen[e][key] = v

    def _deps(self, e, reads, writes):
        for b in reads:
            self._wait(e, b.w)
        for b in writes:
            self._wait(e, b.w)
            for t in b.r.values():
                self._wait(e, t)

    def _mark(self, key, tok, reads, writes):
        for b in reads:
            b.r[key] = tok
        for b in writes:
            b.w = tok
            b.r = {}

    def _next_tok(self, e):
        if self.cnt[e] >= EPOCH and not self.pending.get(e):
            self.epoch[e] += 1
            self.cnt[e] = 0
            self.csem[(e, self.epoch[e])] = self.nc.alloc_semaphore(f"c_{e}_{self.epoch[e]}")
        self.cnt[e] += 1
        return ("c", e, self.epoch[e], self.cnt[e])

    def op(self, e, fn, reads=(), writes=(), inc=True):
        reads = _bufs(reads)
        writes = _bufs(writes)
        self._deps(e, reads, writes)
        if not inc:
            self.pending[e] = True
            tok = ("c", e, self.epoch[e], self.cnt[e] + 1)
            fn(self.eng[e])
        else:
            self.pending[e] = False
            tok = self._next_tok(e)
            ins = fn(self.eng[e])
            ins.then_inc(self.csem[(e, tok[2])], 1)
        self._mark(e, tok, reads, writes)
        self.n_instr += 1
        return tok

    def dma(self, q, out, in_, reads=(), writes=(), **kw):
        reads = _bufs(reads)
        writes = _bufs(writes)
        self._deps(q, reads, writes)
        i = self.dnext.get(q, 0)
        self.dnext[q] = (i + 1) % self.ndma
        if (q, i) not in self.dsem:
            self.dsem[(q, i)] = self.nc.alloc_semaphore(f"d_{q}_{i}")
            self.dval[(q, i)] = 0
        if self.dval[(q, i)] > 0:
            self._wait(q, ("d", q, i, self.dval[(q, i)]))
        self.dval[(q, i)] += 16
        tok = ("d", q, i, self.dval[(q, i)])
        self.eng[q].dma_start(out=out, in_=in_, **kw).then_inc(self.dsem[(q, i)], 16)
        self._mark(("d", q, i), tok, reads, writes)
        self.n_instr += 1
        return tok

    def barrier(self):
        toks = []
        for e in self.eng:
            if self.cnt[e] > 0:
                toks.append(("c", e, self.epoch[e], self.cnt[e]))
        for (q, i), v in self.dval.items():
            if v > 0:
                toks.append(("d", q, i, v))
        for e in self.eng:
            for t in toks:
                if t[0] == "c" and t[1] == e and (e == "pe" or not SAME_ENGINE_SYNC):
                    continue
                self._wait(e, t)


class Cfg:
    def __init__(self, D=4096, T=2048, DFF=11008, PAST=256, SEQ=256, GRID_W=64, L=2):
        self.D, self.T, self.DFF, self.PAST, self.SEQ, self.GRID_W, self.L = D, T, DFF, PAST, SEQ, GRID_W, L
        self.DC = D // 128
        self.DA_W = D // 2
        self.H = self.DA_W // 256
        self.RW_W = D // 4
        self.RH = self.RW_W // 64
        self.RG = self.RW_W // 128
        self.RW_IN = 3 * self.RW_W + 128 + 128 + 160
        self.CM_W = D // 4
        self.P_IN = 3 * self.DA_W + self.RW_IN + 2 * self.CM_W
        self.TT = min(512, T)
        self.NT = T // self.TT
        self.NB = T // 128
        self.KF = DFF // 128
        self.NSEG = T // SEQ
        self.RW0 = 3 * self.DA_W
        self.UV0 = self.RW0 + self.RW_IN
        self.NK = T + PAST
        self.NKB = self.NK // 128


NORM_EPS = 1e-6
PER_STREAM = {"xT", "condT", "mprev", "mnext", "kcT", "vc", "qaug", "kaug", "ropeC", "ropeS", "keepc", "s0T"}
MASK_BIG = 1024.0


def build(cfg, debug=False, stop=99, rw=True, rwstop=9, NS=1):
    c = cfg
    D, T, DC, L, TT, NT = c.D, c.T, c.DC, c.L, c.TT, c.NT
    nc = bass.Bass("TRN2", target_bir_lowering=False)
    S = Sched(nc)

    cache = {}
    cur = [0]

    def _mk(name, shape, dt, kind, per):
        key = f"{name}_{cur[0]}" if per else name
        if key not in cache:
            cache[key] = nc.dram_tensor(key, list(shape), dt, kind=kind).ap()
        return cache[key]

    def din(name, shape, dt=F32):
        return _mk(name, shape, dt, "ExternalInput", name in PER_STREAM)

    def dout(name, shape, dt=F32):
        return _mk(name, shape, dt, "ExternalOutput", True)

    def dscr(name, shape, dt=F32):
        return _mk(name, shape, dt, "Internal", False)

    uid = [0]

    def emit_stream():
        xT_in = din("xT", [D, T])
        condT = din("condT", [128, DC])
        ident_d = din("ident", [128, 128])
        mod_w = din("mod_w", [L, D, 6 * D])
        mod_bT = din("mod_bT", [L, 128, 6 * DC])
        n1gT = din("n1gT", [L, 128, DC])
        n2gT = din("n2gT", [L, 128, DC])
        fgT = din("fgT", [128, DC])
        w_in = din("w_in", [L, D, c.P_IN])
        w_out = din("w_out", [L, D, D])
        ffn_up = din("ffn_up", [L, D, 2 * c.DFF])
        ffn_down = din("ffn_down", [L, c.DFF, D])
        ffn_cw = din("ffn_cw", [L, 128, 2 * c.KF, 3])
        ffn_cb = din("ffn_cb", [L, 128, 2 * c.KF])
        mprev_d = din("mprev", [128, T])
        mnext_d = din("mnext", [128, T])
        kcT_d = din("kcT", [L, c.H, 2, 128, c.PAST])
        vc_d = din("vc", [L, c.H, c.PAST, 256])
        qaug_d = din("qaug", [8, T])
        kaug_d = din("kaug", [8, c.NK])
        ropeC_d = din("ropeC", [128, T])
        ropeS_d = din("ropeS", [128, T])
        lam_d = din("lam_rep", [L, 128, 512])
        subg_d = din("subg", [L, 128, 2])
        CMC = c.CM_W // 128
        cmg_d = din("cmg", [L, 128, CMC])
        wsT_d = din("wsT", [L, 4, 128, 128])
        bsr_d = din("bs_rep", [L, 4, 128, 128])
        RG, RH, RW_W = c.RG, c.RH, c.RW_W
        CK = 64
        NCH = T // CK
        NRC = (c.RW_IN + 127) // 128
        rw_cw_d = din("rw_cw", [L, 128, NRC, 3])
        w0T_d = din("w0T", [L, 128, 2, RG])
        a0T_d = din("a0T", [L, 128, 2, RG])
        w2_d = din("rw_w2", [L, 2, 64, RW_W])
        a2_d = din("rw_a2", [L, 2, 64, RW_W])
        g2_d = din("rw_g2", [L, 160, RW_W])
        rwp_d = din("rwp", [L, 128, 5, RG])
        bones_d = din("bones", [128, 128])
        cmask_d = din("cmask", [128, T])
        tri_d = din("tri", [64, 4, 64])
        keep_d = din("keepc", [128, 2, NCH])
        s0T_d = din("s0T", [L, 2, RG, 128, 64])
        statesT = dout("statesT", [c.NSEG, L, 2, RG, 128, 64])

        yT = dout("yT", [D, T])
        kvT = dout("kvT", [L, 2 * c.DA_W, T])
        dbg = {}
        if debug:
            dbg["projT"] = dout("dbg_projT", [c.P_IN, T])
            dbg["modT"] = dout("dbg_modT", [L, 128, 6 * DC])
            dbg["x1"] = dout("dbg_x1", [D, T])

        xs = [dscr("xs0", [D, T]), dscr("xs1", [D, T])]
        projT = dbg["projT"] if debug else dscr("projT", [c.P_IN, T])
        actT = dscr("actT", [c.DFF, T], BF16)
        mixT_d = dscr("mixT_d", [D, T])
        dd = dout if (debug and rwstop < 9) else dscr
        rwA = [dd(f"rwA{d}", [RW_W, T]) for d in range(2)]
        rwB = [dd(f"rwB{d}", [RW_W, T]) for d in range(2)]
        rwK = [dd(f"rwK{d}", [RW_W, T]) for d in range(2)]
        rwR = [dd(f"rwR{d}", [RW_W, T]) for d in range(2)]
        rwP = [dd(f"rwP{d}", [RW_W, NCH]) for d in range(2)]
        rwV = dd("rwV", [RW_W, T])
        rwBon = dd("rwBon", [RW_W, T])
        rwGate = dd("rwGate", [RW_W, T])
        rwY = [dd(f"rwY{d}", [RW_W, T]) for d in range(2)]

        es_glob = ExitStack()


        def alloc(es, name, shape, dt=F32):
            uid[0] += 1
            return Tile(es.enter_context(nc.sbuf_tensor(f"s{uid[0]}_{name}", list(shape), dt)), name)

        def palloc(es, name, shape, dt=F32):
            uid[0] += 1
            return Tile(es.enter_context(nc.psum_tensor(f"p{uid[0]}_{name}", list(shape), dt)), name)

        ident = alloc(es_glob, "ident", [128, 128])
        ones_f = alloc(es_glob, "ones_f", [128, 128])
        ones_b = alloc(es_glob, "ones_b", [128, 128], BF16)
        modT = [alloc(es_glob, f"modT{l}", [128, 6 * DC]) for l in range(L)]
        a1 = [alloc(es_glob, f"a1_{l}", [128, DC]) for l in range(L)]
        a2 = [alloc(es_glob, f"a2_{l}", [128, DC]) for l in range(L)]
        fg = alloc(es_glob, "fg", [128, DC])
        S.dma("sp", ident[:], ident_d, writes=[ident])
        S.dma("sp", fg[:], fgT, writes=[fg])
        S.op("dve", lambda e: e.memset(ones_f[:], 1.0), writes=[ones_f])
        S.op("dve", lambda e: e.memset(ones_b[:], 1.0), writes=[ones_b])

        SH1, SC1, G1, SH2, SC2, G2 = range(6)

        def modcol(l, which, ch):
            return modT[l][:, which * DC + ch: which * DC + ch + 1]

        def phase_mod():
            NBW = 2048
            nblk = 6 * D // NBW
            with ExitStack() as es:
                cnd = alloc(es, "cnd", [128, DC])
                sc = alloc(es, "silu_c", [128, DC])
                one1 = alloc(es, "one1", [1, 1])
                wbuf = [alloc(es, f"mw{i}", [128, NBW]) for i in range(3)]
                row = [alloc(es, f"mrow{i}", [1, NBW]) for i in range(2)]
                mb = alloc(es, "mb", [128, 6 * DC])
                gt = alloc(es, "gt", [128, DC])
                ps = [palloc(es, f"mps{i}", [128, NBW]) for i in range(1)]
                psT = palloc(es, "mpsT", [128, 6 * DC])
                S.dma("sp", cnd[:], condT, writes=[cnd])
                S.op("act", lambda e: e.activation(sc[:], cnd[:], AF.Silu), reads=[cnd], writes=[sc])
                S.op("dve", lambda e: e.memset(one1[:], 1.0), writes=[one1])
                k = 0
                for l in range(L):
                    for nb in range(nblk):
                        p = ps[0]
                        for ch in range(DC):
                            w = wbuf[k % 3]
                            k += 1
                            S.dma("sp", w[:], mod_w[l, ch * 128:(ch + 1) * 128, nb * NBW:(nb + 1) * NBW], writes=[w])
                            for j in range(NBW // 512):
                                S.op("pe", lambda e, j=j, w=w, ch=ch, p=p: e.matmul(
                                    p[0:1, j * 512:(j + 1) * 512], sc[:, ch:ch + 1], w[:, j * 512:(j + 1) * 512],
                                    start=(ch == 0), stop=(ch == DC - 1)), reads=[sc, w], writes=[p], inc=(j == NBW // 512 - 1))
                        r = row[nb % 2]
                        S.op("act", lambda e, r=r, p=p: e.activation(r[:], p[0:1, :], AF.Copy), reads=[p], writes=[r])
                        for j in range(NBW // 128):
                            col = nb * (NBW // 128) + j
                            S.op("pe", lambda e, r=r, j=j, col=col: e.matmul(
                                psT[:, col:col + 1], r[0:1, j * 128:(j + 1) * 128], one1[0:1, 0:1],
                                start=True, stop=True), reads=[r, one1], writes=[psT], inc=(j == NBW // 128 - 1))
                    S.dma("sp", mb[:], mod_bT[l], writes=[mb])
                    S.op("dve", lambda e, l=l: e.tensor_tensor(modT[l][:], psT[:], mb[:], ALU.add),
                         reads=[psT, mb], writes=[modT[l]])
                    for (dst, src, which) in ((a1[l], n1gT, SC1), (a2[l], n2gT, SC2)):
                        S.dma("sp", gt[:], src[l], writes=[gt])
                        S.op("dve", lambda e, dst=dst, which=which, l=l: e.scalar_tensor_tensor(
                            dst[:], modT[l][:, which * DC:(which + 1) * DC], 1.0, gt[:], ALU.add, ALU.mult),
                            reads=[modT[l], gt], writes=[dst])
                    if debug:
                        S.dma("sp", dbg["modT"][l], modT[l][:], reads=[modT[l]])
                S.barrier()

        def phase_norm(es_outer, src, a_t, sh_l_which, hT, dst_dram=None):
            with ExitStack() as es:
                rstd = alloc(es, "rstd", [128, T])
                xt = [alloc(es, f"nx{i}", [128, TT]) for i in range(3)]
                sq = [alloc(es, f"nsq{i}", [128, TT]) for i in range(2)]
                ps = [palloc(es, f"nps{i}", [128, TT]) for i in range(2)]
                k = 0
                for tt in range(NT):
                    p = ps[tt % 2]
                    for ch in range(DC):
                        x = xt[k % 3]
                        q = sq[k % 2]
                        k += 1
                        S.dma("sp", x[:], src[ch * 128:(ch + 1) * 128, tt * TT:(tt + 1) * TT], writes=[x])
                        S.op("act", lambda e, q=q, x=x: e.activation(q[:], x[:], AF.Square), reads=[x], writes=[q])
                        S.op("pe", lambda e, p=p, q=q, ch=ch: e.matmul(p[:], ones_f[:], q[:], start=(ch == 0), stop=(ch == DC - 1)),
                             reads=[ones_f, q], writes=[p])
                    sl = slice(tt * TT, (tt + 1) * TT)
                    S.op("act", lambda e, p=p, sl=sl: e.activation(rstd[:, sl], p[:], AF.Sqrt, bias=NORM_EPS, scale=1.0 / D),
                         reads=[p], writes=[rstd])
                    S.op("dve", lambda e, sl=sl: e.reciprocal(rstd[:, sl], rstd[:, sl]), reads=[rstd], writes=[rstd])
                with ExitStack() as es2:
                    x2 = [alloc(es2, f"nxx{i}", [128, T]) for i in range(2)]
                    tm = [alloc(es2, f"ntm{i}", [128, T]) for i in range(2)]
                    for ch in range(DC):
                        x = x2[ch % 2]
                        t_ = tm[ch % 2]
                        S.dma("sp", x[:], src[ch * 128:(ch + 1) * 128, :], writes=[x])
                        S.op("dve", lambda e, t_=t_, x=x: e.tensor_tensor(t_[:], x[:], rstd[:], ALU.mult), reads=[x, rstd], writes=[t_])
                        if dst_dram is None:
                            l, which = sh_l_which
                            S.op("act", lambda e, t_=t_, ch=ch, l=l, which=which: e.activation(
                                hT[:, ch, :], t_[:], AF.Identity, bias=modcol(l, which, ch), scale=a_t[:, ch:ch + 1]),
                                reads=[t_, a_t, modT[l]], writes=[hT])
                        else:
                            S.op("act", lambda e, x=x, t_=t_, ch=ch: e.activation(
                                x[:], t_[:], AF.Identity, bias=0.0, scale=a_t[:, ch:ch + 1]), reads=[t_, a_t], writes=[x])
                            S.dma("sp", dst_dram[ch * 128:(ch + 1) * 128, :], x[:], reads=[x])
                    S.barrier()

        def dense(es, hT, KC, W, cols, epilogue, tag, nw=3, nps=4, pre=None):
            wb = [alloc(es, f"{tag}w{i}", [128, KC, 128], BF16) for i in range(nw)]
            ps = [palloc(es, f"{tag}p{i}", [128, TT]) for i in range(nps)]
            k = 0
            for i, (c0, mw) in enumerate(cols):
                w = wb[i % nw]
                S.dma("pool", w[:, :, 0:mw], W[:, c0:c0 + mw].rearrange("(c p) m -> p c m", p=128), writes=[w])
                if pre is not None:
                    pre(i)
                for tt in range(NT):
                    p = ps[k % nps]
                    k += 1
                    for ch in range(KC):
                        S.op("pe", lambda e, p=p, w=w, ch=ch, mw=mw, tt=tt: e.matmul(
                            p[0:mw, :], w[:, ch, 0:mw], hT[:, ch, tt * TT:(tt + 1) * TT],
                            start=(ch == 0), stop=(ch == KC - 1)), reads=[w, hT], writes=[p], inc=(ch == KC - 1))
                    epilogue(i, tt, p, mw)

        def chunks(n0, n):
            out = []
            x = n0
            while x < n0 + n:
                w = min(128, n0 + n - x)
                out.append((x, w))
                x += w
            return out

        def phase_proj(l, hT):
            with ExitStack() as es:
                stage = [alloc(es, f"pst{i}", [128, T]) for i in range(2)]
                cols = chunks(0, c.P_IN)

                def epi(i, tt, p, mw):
                    st = stage[i % 2]
                    eng = "act" if (i * NT + tt) % 2 == 0 else "dve"
                    sl = slice(tt * TT, (tt + 1) * TT)
                    if eng == "act":
                        S.op("act", lambda e: e.activation(st[0:mw, sl], p[0:mw, :], AF.Copy), reads=[p], writes=[st])
                    else:
                        S.op("dve", lambda e: e.tensor_copy(st[0:mw, sl], p[0:mw, :]), reads=[p], writes=[st])
                    if tt == NT - 1:
                        c0 = cols[i][0]
                        S.dma("sp", projT[c0:c0 + mw, :], st[0:mw, :], reads=[st])
                        if c.DA_W <= c0 < 3 * c.DA_W:
                            S.dma("sp", kvT[l, c0 - c.DA_W:c0 - c.DA_W + mw, :], st[0:mw, :], reads=[st])

                dense(es, hT, DC, w_in[l], cols, epi, "pj")
                S.barrier()

        def phase_wout(l, mixT, src, dst):
            with ExitStack() as es:
                xo = [alloc(es, f"wxo{i}", [128, T]) for i in range(2)]
                xn = [alloc(es, f"wxn{i}", [128, T]) for i in range(2)]
                cols = chunks(0, D)

                def pre(i):
                    S.dma("sp", xo[i % 2][:], src[i * 128:(i + 1) * 128, :], writes=[xo[i % 2]])

                def epi(i, tt, p, mw):
                    sl = slice(tt * TT, (tt + 1) * TT)
                    S.op("dve", lambda e: e.scalar_tensor_tensor(
                        xn[i % 2][:, sl], p[:], modcol(l, G1, i), xo[i % 2][:, sl], ALU.mult, ALU.add),
                        reads=[p, modT[l], xo[i % 2]], writes=[xn[i % 2]])
                    if tt == NT - 1:
                        S.dma("sp", dst[i * 128:(i + 1) * 128, :], xn[i % 2][:], reads=[xn[i % 2]])

                dense(es, mixT, DC, w_out[l], cols, epi, "wo", pre=pre)
                S.barrier()

        def phase_ffn_up(l, hT):
            KF = c.KF
            with ExitStack() as es:
                mprev = alloc(es, "mprev", [128, T], BF16)
                mnext = alloc(es, "mnext", [128, T], BF16)
                cw = alloc(es, "fcw", [128, 2 * KF, 3])
                cb = alloc(es, "fcb", [128, 2 * KF])
                up = [alloc(es, f"fup{i}", [128, T]) for i in range(2)]
                o = [alloc(es, f"fo{i}", [128, T]) for i in range(2)]
                xm1 = alloc(es, "fxm", [128, T])
                xm = [xm1, xm1]
                ab = [alloc(es, f"fab{i}", [128, T], BF16) for i in range(2)]
                S.dma("pool", mprev[:], mprev_d, writes=[mprev])
                S.dma("pool", mnext[:], mnext_d, writes=[mnext])
                S.dma("sp", cw[:], ffn_cw[l], writes=[cw])
                S.dma("sp", cb[:], ffn_cb[l], writes=[cb])
                cols = []
                for m in range(KF):
                    cols.append((m * 128, 128))
                    cols.append((c.DFF + m * 128, 128))

                def epi(i, tt, p, mw):
                    br = i % 2
                    m = i // 2
                    cidx = m if br == 0 else KF + m
                    u = up[br]
                    sl = slice(tt * TT, (tt + 1) * TT)
                    ob = o[br]
                    S.op("act", lambda e: e.activation(u[:, sl], p[:], AF.Copy), reads=[p], writes=[u])
                    S.op("act", lambda e: e.activation(ob[:, sl], p[:], AF.Identity, bias=0.0, scale=cw[:, cidx, 1:2]),
                         reads=[p, cw], writes=[ob])
                    if tt < NT - 1:
                        return
                    S.op("dve", lambda e: e.tensor_tensor(xm[0][:, 0:T - 1], u[:, 0:T - 1], mprev[:, 1:T], ALU.mult),
                         reads=[u, mprev], writes=[xm[0]])
                    S.op("dve", lambda e: e.scalar_tensor_tensor(ob[:, 1:T], xm[0][:, 0:T - 1], cw[:, cidx, 0:1], ob[:, 1:T], ALU.mult, ALU.add),
                         reads=[xm[0], cw, ob], writes=[ob])
                    S.op("pool", lambda e: e.tensor_tensor(xm[1][:, 0:T - 1], u[:, 1:T], mnext[:, 0:T - 1], ALU.mult),
                         reads=[u, mnext], writes=[xm[1]])
                    S.op("dve", lambda e: e.scalar_tensor_tensor(ob[:, 0:T - 1], xm[1][:, 0:T - 1], cw[:, cidx, 2:3], ob[:, 0:T - 1], ALU.mult, ALU.add),
                         reads=[xm[1], cw, ob], writes=[ob])
                    if br == 0:
                        S.op("act", lambda e: e.activation(ob[:], ob[:], AF.Silu, bias=cb[:, cidx:cidx + 1]), reads=[ob, cb], writes=[ob])
                    else:
                        a = ab[m % 2]
                        S.op("dve", lambda e: e.scalar_tensor_tensor(a[:], ob[:], cb[:, cidx:cidx + 1], o[0][:], ALU.add, ALU.mult),
                             reads=[ob, cb, o[0]], writes=[a])
                        S.dma("sp", actT[m * 128:(m + 1) * 128, :], a[:], reads=[a])

                dense(es, hT, DC, ffn_up[l], cols, epi, "fu", nw=2)
                S.barrier()

        def phase_ffn_down(l, src, dst):
            KF = c.KF
            with ExitStack() as es:
                at = alloc(es, "fdact", [128, KF, TT], BF16)
                wb = [alloc(es, f"fdw{i}", [128, KF, 128], BF16) for i in range(2)]
                xo = [alloc(es, f"fdxo{i}", [128, TT]) for i in range(3)]
                xn = [alloc(es, f"fdxn{i}", [128, TT]) for i in range(3)]
                ps = [palloc(es, f"fdp{i}", [128, TT]) for i in range(3)]
                k = 0
                for tt in range(NT):
                    sl = slice(tt * TT, (tt + 1) * TT)
                    S.dma("sp", at[:], actT[:, sl].rearrange("(c p) t -> p c t", p=128), writes=[at])
                    for m in range(DC):
                        w = wb[k % 2]
                        p = ps[k % 3]
                        x_o = xo[k % 3]
                        x_n = xn[k % 3]
                        k += 1
                        S.dma("pool", w[:], ffn_down[l][:, m * 128:(m + 1) * 128].rearrange("(c p) m -> p c m", p=128), writes=[w])
                        S.dma("sp", x_o[:], src[m * 128:(m + 1) * 128, sl], writes=[x_o])
                        for ch in range(KF):
                            S.op("pe", lambda e, p=p, w=w, ch=ch: e.matmul(p[:], w[:, ch, :], at[:, ch, :], start=(ch == 0), stop=(ch == KF - 1)),
                                 reads=[w, at], writes=[p], inc=(ch == KF - 1))
                        S.op("dve", lambda e, p=p, x_o=x_o, x_n=x_n, m=m: e.scalar_tensor_tensor(
                            x_n[:], p[:], modcol(l, G2, m), x_o[:], ALU.mult, ALU.add), reads=[p, modT[l], x_o], writes=[x_n])
                        S.dma("sp", dst[m * 128:(m + 1) * 128, sl], x_n[:], reads=[x_n])
                S.barrier()


        def phase_attn(l):
            H, NB, NK, NKB = c.H, c.NB, c.NK, c.NKB
            lam_init = 0.8 - 0.6 * math.exp(-0.3 * l)
            scale = 128.0 ** -0.5
            with ExitStack() as es:
                ropeC = alloc(es, "ropeC", [128, T])
                ropeS = alloc(es, "ropeS", [128, T])
                qaug = alloc(es, "qaug", [8, T], BF16)
                kaug = alloc(es, "kaug", [8, NK], BF16)
                lamt = alloc(es, "lamt", [128, 512])
                lamp = alloc(es, "lamp", [128, 512])
                lr = alloc(es, "lr", [128, 4])
                nlam = alloc(es, "nlam", [128, 1])
                sg = alloc(es, "sg", [128, 2])
                identb = alloc(es, "identb", [128, 128], BF16)
                ld = [alloc(es, f"ald{i}", [128, TT]) for i in range(4)]
                qr = [[alloc(es, f"qr{m}{i}", [128, TT], BF16) for i in range(2)] for m in range(2)]
                kr = [alloc(es, f"kr{m}", [128, NK], BF16) for m in range(2)]
                vt = [alloc(es, f"vt{i}", [128, TT]) for i in range(2)]
                vtok = alloc(es, "vtok", [128, NKB, 256], BF16)
                pt = [alloc(es, f"pt{i}", [128, TT], BF16) for i in range(3)]
                ym = [alloc(es, f"ym{m}", [128, 2, TT]) for m in range(2)]
                yd = alloc(es, "yd", [128, 2, TT])
                sq = alloc(es, "asq", [128, 2, TT])
                rden = alloc(es, "rden", [128, TT])
                rs = alloc(es, "ars", [128, TT])
                tq = alloc(es, "atq", [128, TT])
                ob = [alloc(es, f"aob{i}", [128, TT]) for i in range(2)]
                po = [palloc(es, f"apo{i}", [128, TT]) for i in range(2)]
                pden = palloc(es, "apden", [128, TT])
                pss = [palloc(es, f"apss{i}", [128, TT]) for i in range(2)]
                pss2 = palloc(es, "apss2", [128, TT])
                pst = palloc(es, "apst", [128, 128])
                S.dma("sp", ropeC[:], ropeC_d, writes=[ropeC])
                S.dma("sp", ropeS[:], ropeS_d, writes=[ropeS])
                S.dma("pool", qaug[:], qaug_d, writes=[qaug])
                S.dma("pool", kaug[:], kaug_d, writes=[kaug])
                S.dma("sp", lamt[:], lam_d[l], writes=[lamt])
                S.dma("sp", sg[:], subg_d[l], writes=[sg])
                S.op("dve", lambda e: e.tensor_copy(identb[:], ident[:]), reads=[ident], writes=[identb])
                S.op("dve", lambda e: e.tensor_tensor(lamp[:, 0:128], lamt[:, 0:128], lamt[:, 128:256], ALU.mult), reads=[lamt], writes=[lamp])
                S.op("dve", lambda e: e.tensor_tensor(lamp[:, 128:256], lamt[:, 256:384], lamt[:, 384:512], ALU.mult), reads=[lamt], writes=[lamp])
                S.op("dve", lambda e: e.tensor_reduce(lr[:, 0:2], lamp[:, 0:256].rearrange("p (a b) -> p a b", a=2), AX.X, ALU.add), reads=[lamp], writes=[lr])
                S.op("act", lambda e: e.activation(lr[:, 2:4], lr[:, 0:2], AF.Exp), reads=[lr], writes=[lr])
                S.op("dve", lambda e: e.tensor_tensor(nlam[:], lr[:, 3:4], lr[:, 2:3], ALU.subtract), reads=[lr], writes=[nlam])
                S.op("dve", lambda e: e.tensor_scalar(nlam[:], nlam[:], -lam_init, None, ALU.add), reads=[nlam], writes=[nlam])
                S.op("dve", lambda e: e.tensor_scalar(sg[:], sg[:], 1.0 - lam_init, None, ALU.mult), reads=[sg], writes=[sg])

                def rope_tile(rows0, tt, dst_ap, dst_tile, k):
                    sl = slice(tt * TT, (tt + 1) * TT)
                    a = ld[(2 * k) % 4]
                    b = ld[(2 * k + 1) % 4]
                    S.dma("sp", a[:], projT[rows0:rows0 + 128, sl], writes=[a])
                    for blk in range(4):
                        pb = blk ^ 1
                        S.dma("sp", b[blk * 32:(blk + 1) * 32, :], projT[rows0 + pb * 32:rows0 + (pb + 1) * 32, sl], writes=[b])
                    S.op("dve", lambda e: e.tensor_tensor(a[:], a[:], ropeC[:, sl], ALU.mult), reads=[a, ropeC], writes=[a])
                    S.op("pool", lambda e: e.tensor_tensor(b[:], b[:], ropeS[:, sl], ALU.mult), reads=[b, ropeS], writes=[b])
                    S.op("dve", lambda e: e.tensor_tensor(dst_ap, a[:], b[:], ALU.add), reads=[a, b], writes=[dst_tile])

                kctr = 0
                for h in range(H):
                    for m in range(2):
                        for tt in range(NT):
                            rope_tile(c.DA_W + h * 256 + m * 128, tt, kr[m][:, tt * TT:(tt + 1) * TT], kr[m], kctr)
                            kctr += 1
                        S.dma("pool", kr[m][:, T:NK], kcT_d[l, h, m], writes=[kr[m]])
                    vk = 0
                    for e_ in range(2):
                        for tt in range(NT):
                            v = vt[vk % 2]
                            vk += 1
                            r0 = 2 * c.DA_W + h * 256 + e_ * 128
                            S.dma("sp", v[:], projT[r0:r0 + 128, tt * TT:(tt + 1) * TT], writes=[v])
                            for j in range(TT // 128):
                                kb = tt * (TT // 128) + j
                                S.op("pe", lambda e, v=v, j=j: e.transpose(pst[:], v[:, j * 128:(j + 1) * 128], ident[:]),
                                     reads=[v, ident], writes=[pst])
                                S.op("act", lambda e, kb=kb, e_=e_: e.activation(vtok[:, kb, e_ * 128:(e_ + 1) * 128], pst[:], AF.Copy),
                                     reads=[pst], writes=[vtok])
                    S.dma("pool", vtok[:, NB:NKB, :], vc_d[l, h].rearrange("(j p) e -> p j e", p=128), writes=[vtok])
                    for tt in range(NT):
                        sl = slice(tt * TT, (tt + 1) * TT)
                        for m in range(2):
                            q = qr[m][tt % 2]
                            rope_tile(h * 256 + m * 128, tt, q[:], q, kctr)
                            kctr += 1
                            for kb in range(NKB):
                                ps_ = pss[kb % 2]
                                ksl = slice(kb * 128, (kb + 1) * 128)
                                S.op("pe", lambda e, ps_=ps_, ksl=ksl, q=q, m=m: e.matmul(ps_[:], kr[m][:, ksl], q[:], start=True, stop=False),
                                     reads=[kr[m], q], writes=[ps_], inc=False)
                                S.op("pe", lambda e, ps_=ps_, ksl=ksl, sl=sl: e.matmul(ps_[:], kaug[:, ksl], qaug[:, sl], start=False, stop=True),
                                     reads=[kaug, qaug], writes=[ps_])
                                p_ = pt[kb % 3]
                                S.op("act", lambda e, p_=p_, ps_=ps_: e.activation(p_[:], ps_[:], AF.Exp, bias=-scale * MASK_BIG, scale=scale),
                                     reads=[ps_], writes=[p_])
                                st, sp_ = (kb == 0), (kb == NKB - 1)
                                S.op("pe", lambda e, p_=p_, kb=kb, st=st, sp_=sp_: e.matmul(po[0][:], vtok[:, kb, 0:128], p_[:], start=st, stop=sp_),
                                     reads=[vtok, p_], writes=[po[0]], inc=False)
                                S.op("pe", lambda e, p_=p_, kb=kb, st=st, sp_=sp_: e.matmul(po[1][:], vtok[:, kb, 128:256], p_[:], start=st, stop=sp_),
                                     reads=[vtok, p_], writes=[po[1]], inc=False)
                                S.op("pe", lambda e, p_=p_, st=st, sp_=sp_: e.matmul(pden[:], ones_b[:], p_[:], start=st, stop=sp_),
                                     reads=[ones_b, p_], writes=[pden])
                            S.op("dve", lambda e: e.reciprocal(rden[:], pden[:]), reads=[pden], writes=[rden])
                            for e_ in range(2):
                                S.op("dve", lambda e, e_=e_, m=m: e.tensor_tensor(ym[m][:, e_, :], po[e_][:], rden[:], ALU.mult),
                                     reads=[po[e_], rden], writes=[ym[m]])
                        S.op("dve", lambda e: e.scalar_tensor_tensor(yd[:], ym[1][:], nlam[:, 0:1], ym[0][:], ALU.mult, ALU.add),
                             reads=[ym[0], ym[1], nlam], writes=[yd])
                        S.op("act", lambda e: e.activation(sq[:], yd[:], AF.Square), reads=[yd], writes=[sq])
                        for e_ in range(2):
                            S.op("pe", lambda e, e_=e_: e.matmul(pss2[:], ones_f[:], sq[:, e_, :], start=(e_ == 0), stop=(e_ == 1)),
                                 reads=[ones_f, sq], writes=[pss2])
                        S.op("act", lambda e: e.activation(rs[:], pss2[:], AF.Sqrt, bias=NORM_EPS, scale=1.0 / 256), reads=[pss2], writes=[rs])
                        S.op("dve", lambda e: e.reciprocal(rs[:], rs[:]), reads=[rs], writes=[rs])
                        for e_ in range(2):
                            o_ = ob[e_]
                            S.op("dve", lambda e, e_=e_: e.tensor_tensor(tq[:], yd[:, e_, :], rs[:], ALU.mult), reads=[yd, rs], writes=[tq])
                            S.op("act", lambda e, e_=e_, o_=o_: e.activation(o_[:], tq[:], AF.Identity, bias=0.0, scale=sg[:, e_:e_ + 1]),
                                 reads=[tq, sg], writes=[o_])
                            r0 = h * 256 + e_ * 128
                            S.dma("sp", mixT_d[r0:r0 + 128, sl], o_[:], reads=[o_])
                S.barrier()

        def phase_cm(l):
            NB = c.NB
            U0 = c.UV0
            V0 = c.UV0 + c.CM_W
            M0 = c.DA_W + c.RW_W
            CG = c.CM_W // 4
            cbw = min(128, CG)
            with ExitStack() as es:
                rstd = alloc(es, "crstd", [128, T])
                cmg = alloc(es, "cmg", [128, CMC])
                identb = alloc(es, "cidentb", [128, 128], BF16)
                wsb = alloc(es, "wsb", [128, 4, 128], BF16)
                bsr = alloc(es, "bsr", [128, 4, 128])
                xt = [alloc(es, f"cx{i}", [128, TT]) for i in range(3)]
                sq = [alloc(es, f"csq{i}", [128, TT]) for i in range(2)]
                vfull = [alloc(es, f"cv{i}", [128, T]) for i in range(2)]
                zb = [alloc(es, f"czb{i}", [128, T], BF16) for i in range(2)]
                ztok = alloc(es, "ztok", [128, NB, c.CM_W], BF16)
                ut = [alloc(es, f"cu{i}", [128, T]) for i in range(2)]
                ot = [alloc(es, f"co{i}", [128, T]) for i in range(2)]
                ps = [palloc(es, f"cps{i}", [128, TT]) for i in range(2)]
                pstb = [palloc(es, f"cpst{i}", [128, 128], BF16) for i in range(2)]
                pm = [palloc(es, f"cpm{i}", [128, 512]) for i in range(2)]
                S.dma("sp", cmg[:], cmg_d[l], writes=[cmg])
                S.dma("pool", wsb[:], wsT_d[l].rearrange("g q p -> q g p"), writes=[wsb])
                S.dma("sp", bsr[:], bsr_d[l].rearrange("g q p -> q g p"), writes=[bsr])
                S.op("dve", lambda e: e.tensor_copy(identb[:], ident[:]), reads=[ident], writes=[identb])
                k = 0
                for tt in range(NT):
                    p = ps[tt % 2]
                    for ch in range(CMC):
                        x = xt[k % 3]
                        q = sq[k % 2]
                        k += 1
                        S.dma("sp", x[:], projT[V0 + ch * 128:V0 + (ch + 1) * 128, tt * TT:(tt + 1) * TT], writes=[x])
                        S.op("act", lambda e, q=q, x=x: e.activation(q[:], x[:], AF.Square), reads=[x], writes=[q])
                        S.op("pe", lambda e, p=p, q=q, ch=ch: e.matmul(p[:], ones_f[:], q[:], start=(ch == 0), stop=(ch == CMC - 1)),
                             reads=[ones_f, q], writes=[p])
                    sl = slice(tt * TT, (tt + 1) * TT)
                    S.op("act", lambda e, p=p, sl=sl: e.activation(rstd[:, sl], p[:], AF.Sqrt, bias=NORM_EPS, scale=1.0 / c.CM_W),
                         reads=[p], writes=[rstd])
                    S.op("dve", lambda e, sl=sl: e.reciprocal(rstd[:, sl], rstd[:, sl]), reads=[rstd], writes=[rstd])
                k = 0
                for ch in range(CMC):
                    v = vfull[ch % 2]
                    z = zb[ch % 2]
                    S.dma("sp", v[:], projT[V0 + ch * 128:V0 + (ch + 1) * 128, :], writes=[v])
                    S.op("dve", lambda e, v=v: e.tensor_tensor(v[:], v[:], rstd[:], ALU.mult), reads=[v, rstd], writes=[v])
                    S.op("act", lambda e, v=v, z=z, ch=ch: e.activation(z[:], v[:], AF.Identity, bias=0.0, scale=cmg[:, ch:ch + 1]),
                         reads=[v, cmg], writes=[z])
                    for n in range(NB):
                        pb = pstb[k % 2]
                        k += 1
                        S.op("pe", lambda e, pb=pb, z=z, n=n: e.transpose(pb[:], z[:, n * 128:(n + 1) * 128], identb[:]),
                             reads=[z, identb], writes=[pb])
                        if k % 2 == 0:
                            S.op("dve", lambda e, pb=pb, n=n, ch=ch: e.tensor_copy(ztok[:, n, ch * 128:(ch + 1) * 128], pb[:]),
                                 reads=[pb], writes=[ztok])
                        else:
                            S.op("act", lambda e, pb=pb, n=n, ch=ch: e.activation(ztok[:, n, ch * 128:(ch + 1) * 128], pb[:], AF.Copy),
                                 reads=[pb], writes=[ztok])
                bi = 0
                k = 0
                for g in range(4):
                    for sub in range(CG // cbw):
                        cb0 = g * CG + sub * cbw
                        u = ut[bi % 2]
                        o = ot[bi % 2]
                        bi += 1
                        S.dma("sp", u[0:cbw, :], projT[U0 + cb0:U0 + cb0 + cbw, :], writes=[u])
                        for n0 in range(0, NB, 4):
                            p = pm[k % 2]
                            k += 1
                            nn = min(4, NB - n0)
                            for j in range(nn):
                                n = n0 + j
                                S.op("pe", lambda e, p=p, j=j, n=n, cb0=cb0, g=g: e.matmul(
                                    p[0:cbw, j * 128:(j + 1) * 128], ztok[:, n, cb0:cb0 + cbw], wsb[:, g, :], start=True, stop=True),
                                    reads=[ztok, wsb], writes=[p])
                            for j in range(nn):
                                n = n0 + j
                                S.op("dve", lambda e, p=p, j=j, n=n, o=o, g=g: e.tensor_tensor(
                                    o[0:cbw, n * 128:(n + 1) * 128], p[0:cbw, j * 128:(j + 1) * 128], bsr[0:cbw, g, :], ALU.add),
                                    reads=[p, bsr], writes=[o])
                        S.op("pool", lambda e, o=o, u=u: e.tensor_tensor(o[0:cbw, :], o[0:cbw, :], u[0:cbw, :], ALU.mult), reads=[o, u], writes=[o])
                        S.dma("sp", mixT_d[M0 + cb0:M0 + cb0 + cbw, :], o[0:cbw, :], reads=[o])
                S.barrier()


        def phase_rw_pre(l):
            R0 = c.RW0
            with ExitStack() as es:
                mprev = alloc(es, "rmprev", [128, T], BF16)
                mnext = alloc(es, "rmnext", [128, T], BF16)
                cmask = alloc(es, "rcmask", [128, T])
                cw = alloc(es, "rcw", [128, NRC, 3])
                w0 = alloc(es, "rw0", [128, 2, RG])
                a0 = alloc(es, "ra0", [128, 2, RG])
                prm = alloc(es, "rprm", [128, 5, RG])
                bones = alloc(es, "rbones", [128, 128])
                w2 = alloc(es, "rw2", [128, RW_W])
                a2 = alloc(es, "ra2", [128, RW_W])
                g2a = alloc(es, "rg2a", [128, RW_W])
                g2b = alloc(es, "rg2b", [32, RW_W])
                raw = [alloc(es, f"rraw{i}", [128, T]) for i in range(2)]
                xm = alloc(es, "rxm", [128, T])
                tdec = alloc(es, "rtdec", [128, T])
                aac = alloc(es, "raac", [128, T])
                sgl = alloc(es, "rsgl", [128, T])
                sgl2 = alloc(es, "rsgl2", [32, T])
                rt = alloc(es, "rrt", [128, T])
                kt = alloc(es, "rkt", [128, T])
                vt_ = alloc(es, "rvt", [128, T])
                kk = alloc(es, "rkk", [128, T])
                t1 = alloc(es, "rt1", [128, T])
                t2 = alloc(es, "rt2", [128, T])
                t3 = alloc(es, "rt3", [128, T])
                lw = alloc(es, "rlw", [128, T])
                av = alloc(es, "rav", [128, T])
                kdsum = alloc(es, "rkds", [128, T])
                pend = alloc(es, "rpend", [128, NCH])
                ltc = alloc(es, "rltc", [128, NCH, 1])
                ps = [palloc(es, f"rps{i}", [128, TT]) for i in range(4)]
                S.dma("pool", mprev[:], mprev_d, writes=[mprev])
                S.dma("pool", mnext[:], mnext_d, writes=[mnext])
                S.dma("sp", cmask[:], cmask_d, writes=[cmask])
                S.dma("sp", cw[:], rw_cw_d[l], writes=[cw])
                S.dma("sp", w0[:], w0T_d[l], writes=[w0])
                S.dma("sp", a0[:], a0T_d[l], writes=[a0])
                S.dma("sp", prm[:], rwp_d[l], writes=[prm])
                cka = alloc(es, "rcka", [128, RG])
                S.op("dve", lambda e: e.tensor_scalar(cka[:], prm[:, 1, :], -1.0, None, ALU.mult), reads=[prm], writes=[cka])
                S.op("dve", lambda e: e.tensor_scalar(cka[:], cka[:], 1.0, None, ALU.add), reads=[cka], writes=[cka])
                S.dma("sp", bones[:], bones_d, writes=[bones])
                S.dma("sp", w2[:], w2_d[l].rearrange("d r c -> (d r) c"), writes=[w2])
                S.dma("sp", a2[:], a2_d[l].rearrange("d r c -> (d r) c"), writes=[a2])
                S.dma("sp", g2a[:], g2_d[l, 0:128, :], writes=[g2a])
                S.dma("sp", g2b[:], g2_d[l, 128:160, :], writes=[g2b])
                pk = [0]
                rk = [0]

                def conv_rows(row0, nrows, cidx, dst):
                    u = raw[rk[0] % 2]
                    rk[0] += 1
                    n = nrows
                    S.dma("sp", u[0:n, :], projT[row0:row0 + n, :], writes=[u])
                    S.op("act", lambda e: e.activation(dst[0:n, :], u[0:n, :], AF.Identity, bias=0.0, scale=cw[0:n, cidx, 1:2]), reads=[u, cw], writes=[dst])
                    S.op("dve", lambda e: e.tensor_tensor(xm[0:n, 0:T - 1], u[0:n, 0:T - 1], mprev[0:n, 1:T], ALU.mult), reads=[u, mprev], writes=[xm])
                    S.op("dve", lambda e: e.scalar_tensor_tensor(dst[0:n, 1:T], xm[0:n, 0:T - 1], cw[0:n, cidx, 0:1], dst[0:n, 1:T], ALU.mult, ALU.add),
                         reads=[xm, cw, dst], writes=[dst])
                    S.op("dve", lambda e: e.tensor_tensor(xm[0:n, 0:T - 1], u[0:n, 1:T], mnext[0:n, 0:T - 1], ALU.mult), reads=[u, mnext], writes=[xm])
                    S.op("dve", lambda e: e.scalar_tensor_tensor(dst[0:n, 0:T - 1], xm[0:n, 0:T - 1], cw[0:n, cidx, 2:3], dst[0:n, 0:T - 1], ALU.mult, ALU.add),
                         reads=[xm, cw, dst], writes=[dst])

                cdec = 3 * RG
                conv_rows(R0 + 3 * RW_W, 128, cdec, tdec)
                S.op("act", lambda e: e.activation(tdec[:], tdec[:], AF.Tanh), reads=[tdec], writes=[tdec])
                conv_rows(R0 + 3 * RW_W + 128, 128, cdec + 1, aac)
                conv_rows(R0 + 3 * RW_W + 256, 128, cdec + 2, sgl)
                S.op("act", lambda e: e.activation(sgl[:], sgl[:], AF.Sigmoid), reads=[sgl], writes=[sgl])
                conv_rows(R0 + 3 * RW_W + 384, 32, cdec + 3, sgl2)
                S.op("act", lambda e: e.activation(sgl2[:], sgl2[:], AF.Sigmoid), reads=[sgl2], writes=[sgl2])

                def headsum(dst, src):
                    for tt in range(NT):
                        p = ps[pk[0] % 4]
                        pk[0] += 1
                        sl = slice(tt * TT, (tt + 1) * TT)
                        S.op("pe", lambda e, p=p, sl=sl: e.matmul(p[:], bones[:], src[:, sl], start=True, stop=True), reads=[bones, src], writes=[p])
                        S.op("act", lambda e, p=p, sl=sl: e.activation(dst[:, sl], p[:], AF.Copy), reads=[p], writes=[dst])

                for g in range(RG):
                    gs = slice(g * 128, (g + 1) * 128)
                    conv_rows(R0 + g * 128, 128, g, rt)
                    conv_rows(R0 + RW_W + g * 128, 128, RG + g, kt)
                    conv_rows(R0 + 2 * RW_W + g * 128, 128, 2 * RG + g, vt_)
                    S.dma("sp", rwV[gs, :], vt_[:], reads=[vt_])
                    for tt in range(NT):
                        p = ps[pk[0] % 4]
                        pk[0] += 1
                        sl = slice(tt * TT, (tt + 1) * TT)
                        S.op("pe", lambda e, p=p, sl=sl: e.matmul(p[:], g2a[:, gs], sgl[:, sl], start=True, stop=False), reads=[g2a, sgl], writes=[p])
                        S.op("pe", lambda e, p=p, sl=sl: e.matmul(p[:], g2b[:, gs], sgl2[:, sl], start=False, stop=True), reads=[g2b, sgl2], writes=[p])
                        S.op("act", lambda e, p=p, sl=sl: e.activation(t1[:, sl], p[:], AF.Copy), reads=[p], writes=[t1])
                    S.dma("sp", rwGate[gs, :], t1[:], reads=[t1])
                    S.op("pool", lambda e: e.tensor_scalar(kk[:], kt[:], prm[:, 0, g:g + 1], None, ALU.mult), reads=[kt, prm], writes=[kk])
                    S.op("act", lambda e: e.activation(t2[:], kk[:], AF.Square), reads=[kk], writes=[t2])
                    headsum(t3, t2)
                    S.op("act", lambda e: e.activation(t3[:], t3[:], AF.Sqrt, bias=1e-12, scale=1.0), reads=[t3], writes=[t3])
                    S.op("dve", lambda e: e.reciprocal(t3[:], t3[:]), reads=[t3], writes=[t3])
                    S.op("dve", lambda e: e.tensor_tensor(kk[:], kk[:], t3[:], ALU.mult), reads=[kk, t3], writes=[kk])
                    for d in range(2):
                        ds_ = slice(d * 64, (d + 1) * 64)
                        for tt in range(NT):
                            sl = slice(tt * TT, (tt + 1) * TT)
                            p = ps[pk[0] % 4]
                            pk[0] += 1
                            S.op("pe", lambda e, p=p, sl=sl: e.matmul(p[:], w2[ds_, gs], tdec[ds_, sl], start=True, stop=True), reads=[w2, tdec], writes=[p])
                            S.op("act", lambda e, p=p, sl=sl: e.activation(lw[:, sl], p[:], AF.Sigmoid, bias=w0[:, d, g:g + 1], scale=1.0),
                                 reads=[p, w0], writes=[lw])
                            p = ps[pk[0] % 4]
                            pk[0] += 1
                            S.op("pe", lambda e, p=p, sl=sl: e.matmul(p[:], a2[ds_, gs], aac[ds_, sl], start=True, stop=True), reads=[a2, aac], writes=[p])
                            S.op("act", lambda e, p=p, sl=sl: e.activation(av[:, sl], p[:], AF.Sigmoid, bias=a0[:, d, g:g + 1], scale=1.0),
                                 reads=[p, a0], writes=[av])
                        S.op("pool", lambda e: e.tensor_scalar(lw[:], lw[:], -math.exp(-0.5), None, ALU.mult), reads=[lw], writes=[lw])
                        S.op("act", lambda e: e.activation(t1[:], av[:], AF.Identity, bias=cka[:, g:g + 1], scale=prm[:, 1, g:g + 1]),
                             reads=[av, prm, cka], writes=[t1])
                        S.op("dve", lambda e: e.tensor_tensor(t1[:], t1[:], kt[:], ALU.mult), reads=[t1, kt], writes=[t1])
                        if d == 0:
                            S.op("pool", lambda e: e.tensor_copy(kdsum[:], t1[:]), reads=[t1], writes=[kdsum])
                        else:
                            S.op("pool", lambda e: e.tensor_tensor(kdsum[:], kdsum[:], t1[:], ALU.add), reads=[t1, kdsum], writes=[kdsum])
                        S.op("dve", lambda e: e.tensor_tensor_scan(t2[:], cmask[:], lw[:], 0.0, ALU.mult, ALU.add), reads=[cmask, lw], writes=[t2])
                        S.op("dve", lambda e: e.tensor_copy(ltc[:], t2[:].rearrange("p (n c) -> p n c", c=CK)[:, :, CK - 1:CK]), reads=[t2], writes=[ltc])
                        ltot = ltc[:]
                        S.op("act", lambda e: e.activation(pend[:].rearrange("p (n o) -> p n o", o=1), ltot, AF.Exp), reads=[ltc], writes=[pend])
                        S.dma("sp", rwP[d][gs, :], pend[:], reads=[pend])
                        if d == 1:
                            S.op("dve", lambda e: e.tensor_tensor(t3[:], lw[:], t2[:], ALU.subtract), reads=[lw, t2], writes=[t3])
                            S.op("dve", lambda e: e.tensor_tensor(
                                t2[:].rearrange("p (n c) -> p n c", c=CK), t3[:].rearrange("p (n c) -> p n c", c=CK),
                                ltot.to_broadcast([128, NCH, CK]), ALU.add), reads=[t3, ltc], writes=[t2])
                        S.op("act", lambda e: e.activation(t3[:], t2[:], AF.Exp), reads=[t2], writes=[t3])
                        S.op("dve", lambda e: e.tensor_tensor(t3[:], t3[:], rt[:], ALU.mult), reads=[t3, rt], writes=[t3])
                        S.dma("sp", rwR[d][gs, :], t3[:], reads=[t3])
                        S.op("act", lambda e: e.activation(t3[:], t2[:], AF.Exp, scale=-1.0), reads=[t2], writes=[t3])
                        S.op("dve", lambda e: e.tensor_tensor(t1[:], t1[:], t3[:], ALU.mult), reads=[t3, t1], writes=[t1])
                        S.dma("sp", rwK[d][gs, :], t1[:], reads=[t1])
                        S.op("dve", lambda e: e.tensor_tensor(t3[:], t3[:], kk[:], ALU.mult), reads=[t3, kk], writes=[t3])
                        S.op("dve", lambda e: e.tensor_tensor(t3[:], t3[:], av[:], ALU.mult), reads=[t3, av], writes=[t3])
                        S.dma("sp", rwB[d][gs, :], t3[:], reads=[t3])
                        S.op("dve", lambda e: e.tensor_tensor(t2[:], t2[:], lw[:], ALU.subtract), reads=[t2, lw], writes=[t2])
                        S.op("act", lambda e: e.activation(t2[:], t2[:], AF.Exp), reads=[t2], writes=[t2])
                        S.op("dve", lambda e: e.tensor_tensor(t2[:], t2[:], kk[:], ALU.mult), reads=[t2, kk], writes=[t2])
                        S.dma("sp", rwA[d][gs, :], t2[:], reads=[t2])
                    S.op("dve", lambda e: e.scalar_tensor_tensor(kdsum[:], kdsum[:], prm[:, 2, g:g + 1], rt[:], ALU.mult, ALU.mult),
                         reads=[kdsum, prm, rt], writes=[kdsum])
                    headsum(t1, kdsum)
                    S.op("dve", lambda e: e.tensor_tensor(t1[:], t1[:], vt_[:], ALU.mult), reads=[t1, vt_], writes=[t1])
                    S.dma("sp", rwBon[gs, :], t1[:], reads=[t1])
                S.barrier()

        def phase_rw_scan(l):
            HB = min(8, RH)
            NU = RH // HB
            with ExitStack() as es:
                tri = alloc(es, "tri", [64, 4, 64])
                keep = alloc(es, "keepc", [128, 2, NCH])
                S.dma("sp", tri[:], tri_d, writes=[tri])
                S.dma("sp", keep[:], keep_d, writes=[keep])
                SU, U_, SL, LW = 0, 1, 2, 3
                mk = {0: dict(N=SU, NT=SL, G=SU, H=U_), 1: dict(N=SL, NT=SU, G=SL, H=LW)}
                D_ = {}
                for d in range(2):
                    t = {}
                    for nm in ("A", "B", "K", "R", "V", "Kt", "Bt", "Vt", "N", "NT", "G", "Hk", "Hb", "X", "Pa", "PaT", "Pb", "PbT",
                               "XT", "nS", "Y", "ST"):
                        t[nm] = alloc(es, f"s{nm}{d}", [64, RH, 64])
                    t["P"] = alloc(es, f"sP{d}", [64, RH, NCH])
                    D_[d] = t
                    S.dma("sp", t["ST"][:], s0T_d[l, d].rearrange("g (j k) v -> k (g j) v", j=2), writes=[t["ST"]])
                    S.dma("sp", t["P"][:], rwP[d].rearrange("(h k) n -> k h n", k=64), writes=[t["P"]])
                psT = palloc(es, "spsT", [64, 512])
                ps1 = [palloc(es, f"sps1{i}", [64, 512]) for i in range(2)]
                ps2 = [palloc(es, f"sps2{i}", [64, 512]) for i in range(2)]
                ps3 = [palloc(es, f"sps3{i}", [64, 512]) for i in range(3)]
                ctr = dict(p1=0, p2=0, p3=0, ev=0)
                I64 = ident[0:64, 0:64]

                def flat(ap):
                    return ap.rearrange("p h c -> p (h c)")

                def evac(dst_ap, dst_t, p, cols):
                    ctr["ev"] += 1
                    if ctr["ev"] % 2 == 0:
                        S.op("act", lambda e: e.activation(dst_ap, p[0:64, 0:cols], AF.Copy), reads=[p], writes=[dst_t])
                    else:
                        S.op("dve", lambda e: e.tensor_copy(dst_ap, p[0:64, 0:cols]), reads=[p], writes=[dst_t])

                def hview(tile_, u):
                    return tile_[:, u * HB:(u + 1) * HB, :]

                def pview(p):
                    return p[0:64, 0:HB * 64].rearrange("p (h c) -> p h c", c=64)

                def load_chunk(d, n):
                    t = D_[d]
                    cs = slice(n * CK, (n + 1) * CK)
                    for nm, src in (("A", rwA[d]), ("B", rwB[d]), ("K", rwK[d]), ("R", rwR[d]), ("V", rwV)):
                        S.dma("sp", t[nm][:], src[:, cs].rearrange("(h k) t -> k h t", k=64), writes=[t[nm]])

                order = {0: list(range(NCH)), 1: list(range(NCH - 1, -1, -1))}
                for step in range(NCH):
                    for d in range(2):
                        t = D_[d]
                        n = order[d][step]
                        load_chunk(d, n)
                        A, B, K_, R, V = (t[x] for x in ("A", "B", "K", "R", "V"))
                        m = mk[d]
                        S.op("dve", lambda e, t=t, n=n, d=d: e.tensor_scalar(flat(t["ST"][:]), flat(t["ST"][:]), keep[0:64, d, n:n + 1], None, ALU.mult),
                             reads=[t["ST"], keep], writes=[t["ST"]])
                        for u in range(NU):
                            hs = [u * HB + i for i in range(HB)]
                            for nm_src, nm_dst in ((K_, "Kt"), (B, "Bt"), (V, "Vt")):
                                for i, hd in enumerate(hs):
                                    S.op("pe", lambda e, i=i, hd=hd, nm_src=nm_src: e.transpose(
                                        psT[0:64, i * 64:(i + 1) * 64], nm_src[:, hd, :], I64), reads=[nm_src, ident], writes=[psT], inc=(i == HB - 1))
                                evac(flat(hview(t[nm_dst], u)), t[nm_dst], psT, HB * 64)
                            for (dst, lt, rt_, mask) in (("N", B, A, m["N"]), ("NT", A, B, m["NT"]), ("G", K_, A, m["G"]),
                                                          ("Hk", K_, R, m["H"]), ("Hb", B, R, m["H"])):
                                p = ps1[ctr["p1"] % 2]
                                ctr["p1"] += 1
                                for i, hd in enumerate(hs):
                                    S.op("pe", lambda e, p=p, i=i, hd=hd, lt=lt, rt_=rt_: e.matmul(
                                        p[0:64, i * 64:(i + 1) * 64], lt[:, hd, :], rt_[:, hd, :], start=True, stop=True),
                                        reads=[lt, rt_], writes=[p], inc=(i == HB - 1))
                                S.op("dve", lambda e, p=p, dst=dst, mask=mask, t=t, u=u: e.tensor_tensor(
                                    hview(t[dst], u), pview(p), tri[:, mask:mask + 1, :].to_broadcast([64, HB, 64]), ALU.mult),
                                    reads=[p, tri], writes=[t[dst]])
                            S.op("dve", lambda e, t=t, u=u: e.scalar_tensor_tensor(
                                hview(t["X"], u), hview(t["N"], u), -1.0, I64.unsqueeze(1).to_broadcast([64, HB, 64]), ALU.mult, ALU.add),
                                reads=[t["N"], ident], writes=[t["X"]])
                            cur, curT = "N", "NT"
                            nlev = 5
                            for lev in range(nlev):
                                nxt, nxtT = ("Pa", "PaT") if lev % 2 == 0 else ("Pb", "PbT")
                                p = ps2[ctr["p2"] % 2]
                                ctr["p2"] += 1
                                for i, hd in enumerate(hs):
                                    S.op("pe", lambda e, p=p, i=i, hd=hd, t=t, cur=cur, curT=curT: e.matmul(
                                        p[0:64, i * 64:(i + 1) * 64], t[cur][:, hd, :], t[curT][:, hd, :], start=True, stop=True),
                                        reads=[t[cur], t[curT]], writes=[p], inc=(i == HB - 1))
                                evac(flat(hview(t[nxtT], u)), t[nxtT], p, HB * 64)
                                if lev < nlev - 1:
                                    p = ps2[ctr["p2"] % 2]
                                    ctr["p2"] += 1
                                    for i, hd in enumerate(hs):
                                        S.op("pe", lambda e, p=p, i=i, hd=hd, t=t, cur=cur, curT=curT: e.matmul(
                                            p[0:64, i * 64:(i + 1) * 64], t[curT][:, hd, :], t[cur][:, hd, :], start=True, stop=True),
                                            reads=[t[cur], t[curT]], writes=[p], inc=(i == HB - 1))
                                    evac(flat(hview(t[nxt], u)), t[nxt], p, HB * 64)
                                p = ps2[ctr["p2"] % 2]
                                ctr["p2"] += 1
                                for i, hd in enumerate(hs):
                                    S.op("pe", lambda e, p=p, i=i, hd=hd, t=t, nxtT=nxtT: e.matmul(
                                        p[0:64, i * 64:(i + 1) * 64], t[nxtT][:, hd, :], t["X"][:, hd, :], start=True, stop=True),
                                        reads=[t[nxtT], t["X"]], writes=[p], inc=(i == HB - 1))
                                S.op("dve", lambda e, p=p, t=t, u=u: e.tensor_tensor(
                                    hview(t["X"], u), hview(t["X"], u), pview(p), ALU.add), reads=[p, t["X"]], writes=[t["X"]])
                                cur, curT = nxt, nxtT
                            p = ps3[ctr["p3"] % 3]
                            ctr["p3"] += 1
                            for i, hd in enumerate(hs):
                                S.op("pe", lambda e, p=p, i=i, hd=hd, t=t: e.matmul(
                                    p[0:64, i * 64:(i + 1) * 64], A[:, hd, :], t["ST"][:, hd, :], start=True, stop=False),
                                    reads=[A, t["ST"]], writes=[p], inc=False)
                                S.op("pe", lambda e, p=p, i=i, hd=hd, t=t: e.matmul(
                                    p[0:64, i * 64:(i + 1) * 64], t["G"][:, hd, :], t["Vt"][:, hd, :], start=False, stop=True),
                                    reads=[t["G"], t["Vt"]], writes=[p], inc=(i == HB - 1))
                            evac(flat(hview(t["XT"], u)), t["XT"], p, HB * 64)
                            p = ps3[ctr["p3"] % 3]
                            ctr["p3"] += 1
                            for i, hd in enumerate(hs):
                                S.op("pe", lambda e, p=p, i=i, hd=hd, t=t: e.matmul(
                                    p[0:64, i * 64:(i + 1) * 64], t["X"][:, hd, :], t["XT"][:, hd, :], start=True, stop=True),
                                    reads=[t["X"], t["XT"]], writes=[p], inc=(i == HB - 1))
                            S.op("dve", lambda e, p=p, t=t, u=u: e.tensor_scalar(
                                flat(hview(t["nS"], u)), p[0:64, 0:HB * 64], -1.0, None, ALU.mult), reads=[p], writes=[t["nS"]])
                            p = ps3[ctr["p3"] % 3]
                            ctr["p3"] += 1
                            for i, hd in enumerate(hs):
                                S.op("pe", lambda e, p=p, i=i, hd=hd, t=t: e.matmul(
                                    p[0:64, i * 64:(i + 1) * 64], t["ST"][:, hd, :], R[:, hd, :], start=True, stop=False),
                                    reads=[t["ST"], R], writes=[p], inc=False)
                                S.op("pe", lambda e, p=p, i=i, hd=hd, t=t: e.matmul(
                                    p[0:64, i * 64:(i + 1) * 64], t["Vt"][:, hd, :], t["Hk"][:, hd, :], start=False, stop=False),
                                    reads=[t["Vt"], t["Hk"]], writes=[p], inc=False)
                                S.op("pe", lambda e, p=p, i=i, hd=hd, t=t: e.matmul(
                                    p[0:64, i * 64:(i + 1) * 64], t["nS"][:, hd, :], t["Hb"][:, hd, :], start=False, stop=True),
                                    reads=[t["nS"], t["Hb"]], writes=[p], inc=(i == HB - 1))
                            evac(flat(hview(t["Y"], u)), t["Y"], p, HB * 64)
                            p = ps3[ctr["p3"] % 3]
                            ctr["p3"] += 1
                            for i, hd in enumerate(hs):
                                S.op("pe", lambda e, p=p, i=i, hd=hd, t=t: e.matmul(
                                    p[0:64, i * 64:(i + 1) * 64], t["Kt"][:, hd, :], t["Vt"][:, hd, :], start=True, stop=False),
                                    reads=[t["Kt"], t["Vt"]], writes=[p], inc=False)
                                S.op("pe", lambda e, p=p, i=i, hd=hd, t=t: e.matmul(
                                    p[0:64, i * 64:(i + 1) * 64], t["Bt"][:, hd, :], t["nS"][:, hd, :], start=False, stop=True),
                                    reads=[t["Bt"], t["nS"]], writes=[p], inc=(i == HB - 1))
                            S.op("dve", lambda e, p=p, t=t, u=u: e.tensor_tensor(hview(t["ST"], u), hview(t["ST"], u), pview(p), ALU.add),
                                 reads=[p, t["ST"]], writes=[t["ST"]])
                            S.op("dve", lambda e, t=t, u=u, n=n: e.tensor_tensor(
                                hview(t["ST"], u), hview(t["ST"], u), t["P"][:, u * HB:(u + 1) * HB, n:n + 1].to_broadcast([64, HB, 64]), ALU.mult),
                                reads=[t["P"], t["ST"]], writes=[t["ST"]])
                        cs = slice(n * CK, (n + 1) * CK)
                        S.dma("sp", rwY[d][:, cs].rearrange("(h v) t -> v h t", v=64), t["Y"][:], reads=[t["Y"]])
                        tpos = (n + 1) * CK if d == 0 else n * CK
                        if tpos % c.SEQ == 0:
                            seg = tpos // c.SEQ - 1 if d == 0 else tpos // c.SEQ
                            S.dma("sp", statesT[seg, l, d].rearrange("g (j k) v -> k (g j) v", j=2), t["ST"][:], reads=[t["ST"]])
                S.barrier()

        def phase_rw_post(l):
            with ExitStack() as es:
                prm = alloc(es, "qprm", [128, 5, RG])
                bones = alloc(es, "qbones", [128, 128])
                y0 = [alloc(es, f"qy0{i}", [128, T]) for i in range(2)]
                y1 = [alloc(es, f"qy1{i}", [128, T]) for i in range(2)]
                bon = [alloc(es, f"qbon{i}", [128, T]) for i in range(2)]
                gat = [alloc(es, f"qgat{i}", [128, T]) for i in range(2)]
                mu = alloc(es, "qmu", [128, T])
                sq = alloc(es, "qsq", [128, T])
                ps = [palloc(es, f"qps{i}", [128, TT]) for i in range(4)]
                S.dma("sp", prm[:], rwp_d[l], writes=[prm])
                S.dma("sp", bones[:], bones_d, writes=[bones])
                pk = 0
                for g in range(RG):
                    gs = slice(g * 128, (g + 1) * 128)
                    a, b, bo, ga = y0[g % 2], y1[g % 2], bon[g % 2], gat[g % 2]
                    S.dma("sp", a[:], rwY[0][gs, :], writes=[a])
                    S.dma("sp", b[:], rwY[1][gs, :], writes=[b])
                    S.dma("sp", bo[:], rwBon[gs, :], writes=[bo])
                    S.dma("sp", ga[:], rwGate[gs, :], writes=[ga])
                    S.op("dve", lambda e, a=a, b=b: e.tensor_tensor(a[:], a[:], b[:], ALU.add), reads=[a, b], writes=[a])
                    for tt in range(NT):
                        sl = slice(tt * TT, (tt + 1) * TT)
                        p = ps[pk % 4]
                        pk += 1
                        S.op("pe", lambda e, p=p, a=a, sl=sl: e.matmul(p[:], bones[:], a[:, sl], start=True, stop=True), reads=[bones, a], writes=[p])
                        S.op("act", lambda e, p=p, sl=sl: e.activation(mu[:, sl], p[:], AF.Identity, bias=0.0, scale=-1.0 / 64), reads=[p], writes=[mu])
                    S.op("dve", lambda e, a=a: e.tensor_tensor(a[:], a[:], mu[:], ALU.add), reads=[a, mu], writes=[a])
                    S.op("act", lambda e, a=a: e.activation(sq[:], a[:], AF.Square), reads=[a], writes=[sq])
                    for tt in range(NT):
                        sl = slice(tt * TT, (tt + 1) * TT)
                        p = ps[pk % 4]
                        pk += 1
                        S.op("pe", lambda e, p=p, sl=sl: e.matmul(p[:], bones[:], sq[:, sl], start=True, stop=True), reads=[bones, sq], writes=[p])
                        S.op("act", lambda e, p=p, sl=sl: e.activation(mu[:, sl], p[:], AF.Sqrt, bias=64e-5, scale=1.0 / 64), reads=[p], writes=[mu])
                    S.op("dve", lambda e: e.reciprocal(mu[:], mu[:]), reads=[mu], writes=[mu])
                    S.op("dve", lambda e, a=a: e.tensor_tensor(a[:], a[:], mu[:], ALU.mult), reads=[a, mu], writes=[a])
                    S.op("act", lambda e, a=a, g=g: e.activation(a[:], a[:], AF.Identity, bias=prm[:, 4, g:g + 1], scale=prm[:, 3, g:g + 1]),
                         reads=[a, prm], writes=[a])
                    S.op("dve", lambda e, a=a, bo=bo: e.tensor_tensor(a[:], a[:], bo[:], ALU.add), reads=[a, bo], writes=[a])
                    S.op("pool", lambda e, a=a, ga=ga: e.tensor_tensor(a[:], a[:], ga[:], ALU.mult), reads=[a, ga], writes=[a])
                    S.dma("sp", mixT_d[c.DA_W + g * 128:c.DA_W + (g + 1) * 128, :], a[:], reads=[a])
                S.barrier()

        def phase_rw_zero(l):
            with ExitStack() as es:
                z = alloc(es, "zz", [128, T])
                S.op("dve", lambda e: e.memset(z[:], 0.0), writes=[z])
                for g in range(c.RG):
                    S.dma("sp", mixT_d[c.DA_W + g * 128:c.DA_W + (g + 1) * 128, :], z[:], reads=[z])
                S.barrier()

        phase_mod()
        src = xT_in
        for l in range(L):
            if stop <= 1:
                break
            with ExitStack() as es:
                hT = alloc(es, "hT", [128, DC, T], BF16)
                phase_norm(es, src, a1[l], (l, SH1), hT)
                if stop <= 2:
                    break
                phase_proj(l, hT)
            if stop <= 3:
                break
            phase_attn(l)
            phase_cm(l)
            if rw:
                phase_rw_pre(l)
                if rwstop <= 1:
                    break
                phase_rw_scan(l)
                if rwstop <= 2:
                    break
                phase_rw_post(l)
            else:
                phase_rw_zero(l)
            with ExitStack() as es:
                mixT = alloc(es, "mixT", [128, DC, T], BF16)
                for ch in range(DC):
                    S.dma("pool", mixT[:, ch, :], mixT_d[ch * 128:(ch + 1) * 128, :], writes=[mixT])
                phase_wout(l, mixT, src, xs[0])
            if stop <= 4:
                break
            if debug and l == 0:
                with ExitStack() as es:
                    tmp = alloc(es, "dbgx", [128, T])
                    for ch in range(DC):
                        S.dma("sp", tmp[:], xs[0][ch * 128:(ch + 1) * 128, :], writes=[tmp])
                        S.dma("sp", dbg["x1"][ch * 128:(ch + 1) * 128, :], tmp[:], reads=[tmp])
                    S.barrier()
            with ExitStack() as es:
                hT = alloc(es, "hT2", [128, DC, T], BF16)
                phase_norm(es, xs[0], a2[l], (l, SH2), hT)
                if stop <= 5:
                    break
                phase_ffn_up(l, hT)
            if stop <= 6:
                break
            phase_ffn_down(l, xs[0], xs[1])
            src = xs[1]
        if stop > 7 and rwstop >= 9:
            phase_norm(None, src, fg, None, None, dst_dram=yT)
        S.barrier()
        es_glob.close()

    for si in range(NS):
        cur[0] = si
        emit_stream()
    return nc, S


def _pm(v, nchunk):
    return np.ascontiguousarray(np.asarray(v, np.float32).reshape(nchunk, 128).T)


def make_streams(cfg, BATCH, DEC_BATCH):
    st = [("lat", b) for b in range(DEC_BATCH)]
    per = cfg.T // cfg.SEQ
    for i in range(0, BATCH, per):
        st.append(("ctx", list(range(i, i + per))))
    return st


def prep_inputs(cfg, inp, core_streams):
    c = cfg
    L, DC, T = c.L, c.DC, c.T
    f32 = np.float32
    shared = {}
    shared["ident"] = np.eye(128, dtype=f32)
    shared["mod_w"] = np.ascontiguousarray(inp["mod_w"], f32)
    shared["mod_bT"] = np.stack([_pm(inp["mod_b"][l], 6 * DC) for l in range(L)])
    shared["n1gT"] = np.stack([_pm(inp["norm1_g"][l], DC) for l in range(L)])
    shared["n2gT"] = np.stack([_pm(inp["norm2_g"][l], DC) for l in range(L)])
    shared["fgT"] = _pm(inp["final_norm_g"], DC)
    shared["w_in"] = np.ascontiguousarray(inp["w_in"], f32)
    shared["w_out"] = np.ascontiguousarray(inp["w_out"], f32)
    shared["ffn_up"] = np.ascontiguousarray(inp["ffn_up"], f32)
    shared["ffn_down"] = np.ascontiguousarray(inp["ffn_down"], f32)
    cw = np.asarray(inp["ffn_conv_w"], f32)
    shared["ffn_cw"] = np.ascontiguousarray(cw.reshape(L, 3, 2 * c.KF, 128).transpose(0, 3, 2, 1))
    shared["ffn_cb"] = np.stack([_pm(inp["ffn_conv_b"][l], 2 * c.KF) for l in range(L)])
    shared["lam_rep"] = np.ascontiguousarray(np.broadcast_to(
        np.asarray(inp["da_lambda"], f32).reshape(L, 1, 512), (L, 128, 512)))
    shared["subg"] = np.stack([_pm(inp["da_subln_g"][l], 2) for l in range(L)])
    shared["cmg"] = np.stack([_pm(inp["cm_norm_g"][l], c.CM_W // 128) for l in range(L)])
    shared["wsT"] = np.ascontiguousarray(np.asarray(inp["cm_ws"], f32).transpose(0, 1, 3, 2))
    shared["bs_rep"] = np.ascontiguousarray(np.broadcast_to(
        np.asarray(inp["cm_bs"], f32)[:, :, None, :], (L, 4, 128, 128)))
    RG, RW_W, RW_IN = c.RG, c.RW_W, c.RW_IN
    NRC = (RW_IN + 127) // 128
    CK = 64
    NCH = T // CK
    rcw = np.zeros((L, 3, NRC * 128), f32)
    rcw[:, :, :RW_IN] = np.asarray(inp["rw_conv_w"], f32)
    shared["rw_cw"] = np.ascontiguousarray(rcw.reshape(L, 3, NRC, 128).transpose(0, 3, 2, 1))
    shared["w0T"] = np.ascontiguousarray(np.asarray(inp["rw_w0"], f32).reshape(L, 2, RG, 128).transpose(0, 3, 1, 2))
    shared["a0T"] = np.ascontiguousarray(np.asarray(inp["rw_a0"], f32).reshape(L, 2, RG, 128).transpose(0, 3, 1, 2))
    shared["rw_w2"] = np.ascontiguousarray(inp["rw_w2"], f32)
    shared["rw_a2"] = np.ascontiguousarray(inp["rw_a2"], f32)
    shared["rw_g2"] = np.ascontiguousarray(inp["rw_g2"], f32)
    prm = np.stack([np.asarray(inp[k], f32).reshape(L, RW_W) for k in ("rw_k_k", "rw_k_a", "rw_r_k", "rw_gn_g", "rw_gn_b")], axis=1)
    shared["rwp"] = np.ascontiguousarray(prm.reshape(L, 5, RG, 128).transpose(0, 3, 1, 2))
    bo = np.zeros((128, 128), f32)
    bo[:64, :64] = 1.0
    bo[64:, 64:] = 1.0
    shared["bones"] = bo
    tt_ = np.arange(T)
    shared["cmask"] = np.ascontiguousarray(np.broadcast_to((tt_ % CK != 0).astype(f32), (128, T)))
    ii = np.arange(64)
    tri = np.stack([(ii[:, None] < ii[None, :]), (ii[:, None] <= ii[None, :]),
                    (ii[:, None] > ii[None, :]), (ii[:, None] >= ii[None, :])], axis=1).astype(f32)
    shared["tri"] = np.ascontiguousarray(tri)
    maps = []
    for streams in core_streams:
        cm = dict(shared)
        for si, (kind, ident_) in enumerate(streams):
            m = _prep_stream(cfg, inp, kind, ident_)
            for k_, v_ in m.items():
                cm[f"{k_}_{si}"] = v_
        maps.append(cm)
    return maps


def _prep_stream(cfg, inp, kind, ident_):
    c = cfg
    L, DC, T = c.L, c.DC, c.T
    f32 = np.float32
    RG = c.RG
    CK = 64
    NCH = T // CK
    if True:
        m = {}
        if kind == "lat":
            b = ident_
            m["xT"] = np.ascontiguousarray(np.asarray(inp["x_sample"][b], f32).T)
            m["condT"] = _pm(inp["c"][b], DC)
            seqlen = T
        else:
            xs_ = np.concatenate([np.asarray(inp["x_prompt"][s], f32) for s in ident_], axis=0)
            m["xT"] = np.ascontiguousarray(xs_.T)
            m["condT"] = _pm(inp["c_ctx"], DC)
            seqlen = c.SEQ
        t = np.arange(T)
        mp = (t % seqlen != 0).astype(f32)
        mn = (t % seqlen != seqlen - 1).astype(f32)
        m["mprev"] = np.ascontiguousarray(np.broadcast_to(mp, (128, T)))
        m["mnext"] = np.ascontiguousarray(np.broadcast_to(mn, (128, T)))
        qa = np.zeros((8, T), f32)
        ka = np.zeros((8, c.NK), f32)
        sid = t // seqlen
        qa[sid, t] = MASK_BIG
        ka[sid, t] = 1.0
        rc = np.ones((128, T), np.float64)
        rsn = np.zeros((128, T), np.float64)
        if kind == "lat":
            ka[0, T:] = 1.0
            kc = np.asarray(inp["cache_da_k"][b], f32)
            m["kcT"] = np.ascontiguousarray(kc.transpose(0, 2, 3, 4, 1))
            vcc = np.asarray(inp["cache_da_v"][b], f32)
            m["vc"] = np.ascontiguousarray(vcc.transpose(0, 2, 1, 3))
            p = np.arange(128)
            i = p % 64
            j = i % 32
            inv = 10000.0 ** (-j.astype(np.float64) / 32)
            row = (t // c.GRID_W).astype(np.float64)
            col = (t % c.GRID_W).astype(np.float64)
            pos = np.where((p < 64)[:, None], row[None, :], col[None, :])
            ang = (pos.astype(np.float32) * inv.astype(np.float32)[:, None]).astype(np.float32)
            rc = np.cos(ang)
            sn = np.sin(ang)
            rsn = np.where((i < 32)[:, None], -sn, sn)
        else:
            m["kcT"] = np.zeros((L, c.H, 2, 128, c.PAST), f32)
            m["vc"] = np.zeros((L, c.H, c.PAST, 256), f32)
        m["qaug"] = qa
        m["kaug"] = ka
        kp = np.ones((2, NCH), f32)
        if kind == "lat":
            st = np.asarray(inp["state_rwkv"][b], f32)
            m["s0T"] = np.ascontiguousarray(st.transpose(0, 1, 2, 4, 3).reshape(L, 2, RG, 128, 64))
        else:
            m["s0T"] = np.zeros((L, 2, RG, 128, 64), f32)
            cpos = np.arange(NCH) * CK
            kp[0] = (cpos % c.SEQ != 0)
            kp[1] = ((cpos + CK) % c.SEQ != 0)
        m["keepc"] = np.ascontiguousarray(np.broadcast_to(kp, (128, 2, NCH)))
        m["ropeC"] = np.ascontiguousarray(rc, f32)
        m["ropeS"] = np.ascontiguousarray(rsn, f32)
    return m


def assemble(cfg, results, core_streams, BATCH, DEC_BATCH):
    c = cfg
    L, T, D = c.L, c.T, c.D
    f32 = np.float32
    y_prompt = np.zeros((BATCH, c.SEQ, D), f32)
    y_sample = np.zeros((DEC_BATCH, T, D), f32)
    new_k = np.zeros((BATCH, L, c.SEQ, c.H, 2, 128), f32)
    new_v = np.zeros((BATCH, L, c.SEQ, c.H, 256), f32)
    new_s = np.zeros((BATCH, L, 2, c.RH, 64, 64), f32)
    for ci, streams in enumerate(core_streams):
      for si, (kind, ident_) in enumerate(streams):
        r = {k_[:-len(f"_{si}")]: v_ for k_, v_ in results[ci].items() if k_.endswith(f"_{si}")}
        if kind == "lat":
            y_sample[ident_] = r["yT"].T
        else:
            y = r["yT"].T
            kv = r["kvT"]
            for j, s in enumerate(ident_):
                tsl = slice(j * c.SEQ, (j + 1) * c.SEQ)
                y_prompt[s] = y[tsl]
                for l in range(L):
                    new_k[s, l] = kv[l, 0:c.DA_W, tsl].T.reshape(c.SEQ, c.H, 2, 128)
                    new_v[s, l] = kv[l, c.DA_W:2 * c.DA_W, tsl].T.reshape(c.SEQ, c.H, 256)
                st = r["statesT"][j]
                new_s[s] = st.reshape(L, 2, c.RH, 64, 64).transpose(0, 1, 2, 4, 3)
    return y_prompt, y_sample, new_k, new_v, new_s


def plan_cores(streams, n_cores):
    ns = (len(streams) + n_cores - 1) // n_cores
    cs = []
    for ci in range(n_cores):
        cs.append([streams[min(ci * ns + j, len(streams) - 1)] for j in range(ns)])
    return cs, ns


N_CORES = 6


def kernel(**inputs):
    cfg = Cfg()
    inp = {k: np.asarray(v) for k, v in inputs.items()}
    BATCH = inp["x_prompt"].shape[0]
    DEC_BATCH = inp["x_sample"].shape[0]
    streams = make_streams(cfg, BATCH, DEC_BATCH)
    core_streams, ns = plan_cores(streams, N_CORES)
    maps = prep_inputs(cfg, inp, core_streams)
    nc, _ = build(cfg, NS=ns)
    res = run_bass_kernel_spmd(nc, maps, core_ids=list(range(N_CORES)))
    return assemble(cfg, res.results, core_streams, BATCH, DEC_BATCH)
```

```python
import math
from contextlib import ExitStack
import numpy as np
import concourse.bass as bass
import concourse.mybir as mybir
from concourse.bass_utils import run_bass_kernel_spmd

F32 = mybir.dt.float32
BF16 = mybir.dt.bfloat16
AF = mybir.ActivationFunctionType
ALU = mybir.AluOpType
AX = mybir.AxisListType

EPOCH = 30000
SAME_ENGINE_SYNC = True


class Buf:
    __slots__ = ("name", "w", "r")

    def __init__(self, name=""):
        self.name = name
        self.w = None
        self.r = {}


class Tile:
    def __init__(self, t, name=""):
        self.t = t
        self.b = Buf(name)

    def __getitem__(self, k):
        return self.t[k]


def _bufs(xs):
    out = []
    for x in xs:
        if x is None:
            continue
        out.append(x.b if isinstance(x, Tile) else x)
    return out


class Sched:
    def __init__(self, nc, ndma=8):
        self.nc = nc
        self.eng = {"pe": nc.tensor, "dve": nc.vector, "act": nc.scalar,
                    "pool": nc.gpsimd, "sp": nc.sync}
        self.csem = {}
        self.cnt = {}
        self.epoch = {}
        self.seen = {e: {} for e in self.eng}
        for e in self.eng:
            self.epoch[e] = 0
            self.cnt[e] = 0
            self.csem[(e, 0)] = nc.alloc_semaphore(f"c_{e}_0")
        self.ndma = ndma
        self.dsem = {}
        self.dval = {}
        self.dnext = {}
        self.n_instr = 0
        self.pending = {}

    def _wait(self, e, tok):
        if tok is None:
            return
        if tok[0] == "c":
            _, fe, ep, c = tok
            if fe == e and (e == "pe" or not SAME_ENGINE_SYNC):
                return
            key = ("c", fe, ep)
            if self.seen[e].get(key, 0) >= c:
                return
            self.eng[e].wait_ge(self.csem[(fe, ep)], c)
            self.seen[e][key] = c
        else:
            _, q, i, v = tok
            key = ("d", q, i)
            if self.seen[e].get(key, 0) >= v:
                return
            self.eng[e].wait_ge(self.dsem[(q, i)], v)
            self.seen[e][key] = v

    def _deps(self, e, reads, writes):
        for b in reads:
            self._wait(e, b.w)
        for b in writes:
            self._wait(e, b.w)
            for t in b.r.values():
                self._wait(e, t)

    def _mark(self, key, tok, reads, writes):
        for b in reads:
            b.r[key] = tok
        for b in writes:
            b.w = tok
            b.r = {}

    def _next_tok(self, e):
        if self.cnt[e] >= EPOCH and not self.pending.get(e):
            self.epoch[e] += 1
            self.cnt[e] = 0
            self.csem[(e, self.epoch[e])] = self.nc.alloc_semaphore(f"c_{e}_{self.epoch[e]}")
        self.cnt[e] += 1
        return ("c", e, self.epoch[e], self.cnt[e])

    def op(self, e, fn, reads=(), writes=(), inc=True):
        reads = _bufs(reads)
        writes = _bufs(writes)
        self._deps(e, reads, writes)
        if not inc:
            self.pending[e] = True
            tok = ("c", e, self.epoch[e], self.cnt[e] + 1)
            fn(self.eng[e])
        else:
            self.pending[e] = False
            tok = self._next_tok(e)
            ins = fn(self.eng[e])
            ins.then_inc(self.csem[(e, tok[2])], 1)
        self._mark(e, tok, reads, writes)
        self.n_instr += 1
        return tok

    def dma(self, q, out, in_, reads=(), writes=(), **kw):
        reads = _bufs(reads)
        writes = _bufs(writes)
        self._deps(q, reads, writes)
        i = self.dnext.get(q, 0)
        self.dnext[q] = (i + 1) % self.ndma
        if (q, i) not in self.dsem:
            self.dsem[(q, i)] = self.nc.alloc_semaphore(f"d_{q}_{i}")
            self.dval[(q, i)] = 0
        if self.dval[(q, i)] > 0:
            self._wait(q, ("d", q, i, self.dval[(q, i)]))
        self.dval[(q, i)] += 16
        tok = ("d", q, i, self.dval[(q, i)])
        self.eng[q].dma_start(out=out, in_=in_, **kw).then_inc(self.dsem[(q, i)], 16)
        self._mark(("d", q, i), tok, reads, writes)
        self.n_instr += 1
        return tok

    def barrier(self):
        toks = []
        for e in self.eng:
            if self.cnt[e] > 0:
                toks.append(("c", e, self.epoch[e], self.cnt[e]))
        for (q, i), v in self.dval.items():
            if v > 0:
                toks.append(("d", q, i, v))
        for e in self.eng:
            for t in toks:
                if t[0] == "c" and t[1] == e and (e == "pe" or not SAME_ENGINE_SYNC):
                    continue
                self._wait(e, t)


class Cfg:
    def __init__(self, D=4096, T=2048, DFF=11008, PAST=256, SEQ=256, GRID_W=64, L=2):
        self.D, self.T, self.DFF, self.PAST, self.SEQ, self.GRID_W, self.L = D, T, DFF, PAST, SEQ, GRID_W, L
        self.DC = D // 128
        self.DA_W = D // 2
        self.H = self.DA_W // 256
        self.RW_W = D // 4
        self.RH = self.RW_W // 64
        self.RG = self.RW_W // 128
        self.RW_IN = 3 * self.RW_W + 128 + 128 + 160
        self.CM_W = D // 4
        self.P_IN = 3 * self.DA_W + self.RW_IN + 2 * self.CM_W
        self.TT = min(512, T)
        self.NT = T // self.TT
        self.NB = T // 128
        self.KF = DFF // 128
        self.NSEG = T // SEQ
        self.RW0 = 3 * self.DA_W
        self.UV0 = self.RW0 + self.RW_IN
        self.NK = T + PAST
        self.NKB = self.NK // 128


NORM_EPS = 1e-6
PER_STREAM = {"xT", "condT", "mprev", "mnext", "kcT", "vc", "qaug", "kaug", "ropeC", "ropeS", "keepc", "s0T"}
MASK_BIG = 1024.0


def build(cfg, debug=False, stop=99, rw=True, rwstop=9, NS=1):
    c = cfg
    D, T, DC, L, TT, NT = c.D, c.T, c.DC, c.L, c.TT, c.NT
    nc = bass.Bass("TRN2", target_bir_lowering=False)
    S = Sched(nc)

    cache = {}
    cur = [0]

    def _mk(name, shape, dt, kind, per):
        key = f"{name}_{cur[0]}" if per else name
        if key not in cache:
            cache[key] = nc.dram_tensor(key, list(shape), dt, kind=kind).ap()
        return cache[key]

    def din(name, shape, dt=F32):
        return _mk(name, shape, dt, "ExternalInput", name in PER_STREAM)

    def dout(name, shape, dt=F32):
        return _mk(name, shape, dt, "ExternalOutput", True)

    def dscr(name, shape, dt=F32):
        return _mk(name, shape, dt, "Internal", False)

    uid = [0]

    def emit_stream():
        xT_in = din("xT", [D, T])
        condT = din("condT", [128, DC])
        ident_d = din("ident", [128, 128])
        mod_w = din("mod_w", [L, D, 6 * D])
        mod_bT = din("mod_bT", [L, 128, 6 * DC])
        n1gT = din("n1gT", [L, 128, DC])
        n2gT = din("n2gT", [L, 128, DC])
        fgT = din("fgT", [128, DC])
        w_in = din("w_in", [L, D, c.P_IN])
        w_out = din("w_out", [L, D, D])
        ffn_up = din("ffn_up", [L, D, 2 * c.DFF])
        ffn_down = din("ffn_down", [L, c.DFF, D])
        ffn_cw = din("ffn_cw", [L, 128, 2 * c.KF, 3])
        ffn_cb = din("ffn_cb", [L, 128, 2 * c.KF])
        mprev_d = din("mprev", [128, T])
        mnext_d = din("mnext", [128, T])
        kcT_d = din("kcT", [L, c.H, 2, 128, c.PAST])
        vc_d = din("vc", [L, c.H, c.PAST, 256])
        qaug_d = din("qaug", [8, T])
        kaug_d = din("kaug", [8, c.NK])
        ropeC_d = din("ropeC", [128, T])
        ropeS_d = din("ropeS", [128, T])
        lam_d = din("lam_rep", [L, 128, 512])
        subg_d = din("subg", [L, 128, 2])
        CMC = c.CM_W // 128
        cmg_d = din("cmg", [L, 128, CMC])
        wsT_d = din("wsT", [L, 4, 128, 128])
        bsr_d = din("bs_rep", [L, 4, 128, 128])
        RG, RH, RW_W = c.RG, c.RH, c.RW_W
        CK = 64
        NCH = T // CK
        NRC = (c.RW_IN + 127) // 128
        rw_cw_d = din("rw_cw", [L, 128, NRC, 3])
        w0T_d = din("w0T", [L, 128, 2, RG])
        a0T_d = din("a0T", [L, 128, 2, RG])
        w2_d = din("rw_w2", [L, 2, 64, RW_W])
        a2_d = din("rw_a2", [L, 2, 64, RW_W])
        g2_d = din("rw_g2", [L, 160, RW_W])
        rwp_d = din("rwp", [L, 128, 5, RG])
        bones_d = din("bones", [128, 128])
        cmask_d = din("cmask", [128, T])
        tri_d = din("tri", [64, 4, 64])
        keep_d = din("keepc", [128, 2, NCH])
        s0T_d = din("s0T", [L, 2, RG, 128, 64])
        statesT = dout("statesT", [c.NSEG, L, 2, RG, 128, 64])

        yT = dout("yT", [D, T])
        kvT = dout("kvT", [L, 2 * c.DA_W, T])
        dbg = {}
        if debug:
            dbg["projT"] = dout("dbg_projT", [c.P_IN, T])
            dbg["modT"] = dout("dbg_modT", [L, 128, 6 * DC])
            dbg["x1"] = dout("dbg_x1", [D, T])

        xs = [dscr("xs0", [D, T]), dscr("xs1", [D, T])]
        projT = dbg["projT"] if debug else dscr("projT", [c.P_IN, T])
        actT = dscr("actT", [c.DFF, T], BF16)
        mixT_d = dscr("mixT_d", [D, T])
        dd = dout if (debug and rwstop < 9) else dscr
        rwA = [dd(f"rwA{d}", [RW_W, T]) for d in range(2)]
        rwB = [dd(f"rwB{d}", [RW_W, T]) for d in range(2)]
        rwK = [dd(f"rwK{d}", [RW_W, T]) for d in range(2)]
        rwR = [dd(f"rwR{d}", [RW_W, T]) for d in range(2)]
        rwP = [dd(f"rwP{d}", [RW_W, NCH]) for d in range(2)]
        rwV = dd("rwV", [RW_W, T])
        rwBon = dd("rwBon", [RW_W, T])
        rwGate = dd("rwGate", [RW_W, T])
        rwY = [dd(f"rwY{d}", [RW_W, T]) for d in range(2)]

        es_glob = ExitStack()


        def alloc(es, name, shape, dt=F32):
            uid[0] += 1
            return Tile(es.enter_context(nc.sbuf_tensor(f"s{uid[0]}_{name}", list(shape), dt)), name)

        def palloc(es, name, shape, dt=F32):
            uid[0] += 1
            return Tile(es.enter_context(nc.psum_tensor(f"p{uid[0]}_{name}", list(shape), dt)), name)

        ident = alloc(es_glob, "ident", [128, 128])
        ones_f = alloc(es_glob, "ones_f", [128, 128])
        ones_b = alloc(es_glob, "ones_b", [128, 128], BF16)
        modT = [alloc(es_glob, f"modT{l}", [128, 6 * DC]) for l in range(L)]
        a1 = [alloc(es_glob, f"a1_{l}", [128, DC]) for l in range(L)]
        a2 = [alloc(es_glob, f"a2_{l}", [128, DC]) for l in range(L)]
        fg = alloc(es_glob, "fg", [128, DC])
        S.dma("sp", ident[:], ident_d, writes=[ident])
        S.dma("sp", fg[:], fgT, writes=[fg])
        S.op("dve", lambda e: e.memset(ones_f[:], 1.0), writes=[ones_f])
        S.op("dve", lambda e: e.memset(ones_b[:], 1.0), writes=[ones_b])

        SH1, SC1, G1, SH2, SC2, G2 = range(6)

        def modcol(l, which, ch):
            return modT[l][:, which * DC + ch: which * DC + ch + 1]

        def phase_mod():
            NBW = 2048
            nblk = 6 * D // NBW
            with ExitStack() as es:
                cnd = alloc(es, "cnd", [128, DC])
                sc = alloc(es, "silu_c", [128, DC])
                one1 = alloc(es, "one1", [1, 1])
                wbuf = [alloc(es, f"mw{i}", [128, NBW]) for i in range(3)]
                row = [alloc(es, f"mrow{i}", [1, NBW]) for i in range(2)]
                mb = alloc(es, "mb", [128, 6 * DC])
                gt = alloc(es, "gt", [128, DC])
                ps = [palloc(es, f"mps{i}", [128, NBW]) for i in range(1)]
                psT = palloc(es, "mpsT", [128, 6 * DC])
                S.dma("sp", cnd[:], condT, writes=[cnd])
                S.op("act", lambda e: e.activation(sc[:], cnd[:], AF.Silu), reads=[cnd], writes=[sc])
                S.op("dve", lambda e: e.memset(one1[:], 1.0), writes=[one1])
                k = 0
                for l in range(L):
                    for nb in range(nblk):
                        p = ps[0]
                        for ch in range(DC):
                            w = wbuf[k % 3]
                            k += 1
                            S.dma("sp", w[:], mod_w[l, ch * 128:(ch + 1) * 128, nb * NBW:(nb + 1) * NBW], writes=[w])
                            for j in range(NBW // 512):
                                S.op("pe", lambda e, j=j, w=w, ch=ch, p=p: e.matmul(
                                    p[0:1, j * 512:(j + 1) * 512], sc[:, ch:ch + 1], w[:, j * 512:(j + 1) * 512],
                                    start=(ch == 0), stop=(ch == DC - 1)), reads=[sc, w], writes=[p], inc=(j == NBW // 512 - 1))
                        r = row[nb % 2]
                        S.op("act", lambda e, r=r, p=p: e.activation(r[:], p[0:1, :], AF.Copy), reads=[p], writes=[r])
                        for j in range(NBW // 128):
                            col = nb * (NBW // 128) + j
                            S.op("pe", lambda e, r=r, j=j, col=col: e.matmul(
                                psT[:, col:col + 1], r[0:1, j * 128:(j + 1) * 128], one1[0:1, 0:1],
                                start=True, stop=True), reads=[r, one1], writes=[psT], inc=(j == NBW // 128 - 1))
                    S.dma("sp", mb[:], mod_bT[l], writes=[mb])
                    S.op("dve", lambda e, l=l: e.tensor_tensor(modT[l][:], psT[:], mb[:], ALU.add),
                         reads=[psT, mb], writes=[modT[l]])
                    for (dst, src, which) in ((a1[l], n1gT, SC1), (a2[l], n2gT, SC2)):
                        S.dma("sp", gt[:], src[l], writes=[gt])
                        S.op("dve", lambda e, dst=dst, which=which, l=l: e.scalar_tensor_tensor(
                            dst[:], modT[l][:, which * DC:(which + 1) * DC], 1.0, gt[:], ALU.add, ALU.mult),
                            reads=[modT[l], gt], writes=[dst])
                    if debug:
                        S.dma("sp", dbg["modT"][l], modT[l][:], reads=[modT[l]])
                S.barrier()

        def phase_norm(es_outer, src, a_t, sh_l_which, hT, dst_dram=None):
            with ExitStack() as es:
                rstd = alloc(es, "rstd", [128, T])
                xt = [alloc(es, f"nx{i}", [128, TT]) for i in range(3)]
                sq = [alloc(es, f"nsq{i}", [128, TT]) for i in range(2)]
                ps = [palloc(es, f"nps{i}", [128, TT]) for i in range(2)]
                k = 0
                for tt in range(NT):
                    p = ps[tt % 2]
                    for ch in range(DC):
                        x = xt[k % 3]
                        q = sq[k % 2]
                        k += 1
                        S.dma("sp", x[:], src[ch * 128:(ch + 1) * 128, tt * TT:(tt + 1) * TT], writes=[x])
                        S.op("act", lambda e, q=q, x=x: e.activation(q[:], x[:], AF.Square), reads=[x], writes=[q])
                        S.op("pe", lambda e, p=p, q=q, ch=ch: e.matmul(p[:], ones_f[:], q[:], start=(ch == 0), stop=(ch == DC - 1)),
                             reads=[ones_f, q], writes=[p])
                    sl = slice(tt * TT, (tt + 1) * TT)
                    S.op("act", lambda e, p=p, sl=sl: e.activation(rstd[:, sl], p[:], AF.Sqrt, bias=NORM_EPS, scale=1.0 / D),
                         reads=[p], writes=[rstd])
                    S.op("dve", lambda e, sl=sl: e.reciprocal(rstd[:, sl], rstd[:, sl]), reads=[rstd], writes=[rstd])
                with ExitStack() as es2:
                    x2 = [alloc(es2, f"nxx{i}", [128, T]) for i in range(2)]
                    tm = [alloc(es2, f"ntm{i}", [128, T]) for i in range(2)]
                    for ch in range(DC):
                        x = x2[ch % 2]
                        t_ = tm[ch % 2]
                        S.dma("sp", x[:], src[ch * 128:(ch + 1) * 128, :], writes=[x])
                        S.op("dve", lambda e, t_=t_, x=x: e.tensor_tensor(t_[:], x[:], rstd[:], ALU.mult), reads=[x, rstd], writes=[t_])
                        if dst_dram is None:
                            l, which = sh_l_which
                            S.op("act", lambda e, t_=t_, ch=ch, l=l, which=which: e.activation(
                                hT[:, ch, :], t_[:], AF.Identity, bias=modcol(l, which, ch), scale=a_t[:, ch:ch + 1]),
                                reads=[t_, a_t, modT[l]], writes=[hT])
                        else:
                            S.op("act", lambda e, x=x, t_=t_, ch=ch: e.activation(
                                x[:], t_[:], AF.Identity, bias=0.0, scale=a_t[:, ch:ch + 1]), reads=[t_, a_t], writes=[x])
                            S.dma("sp", dst_dram[ch * 128:(ch + 1) * 128, :], x[:], reads=[x])
                    S.barrier()

        def dense(es, hT, KC, W, cols, epilogue, tag, nw=3, nps=4, pre=None):
            wb = [alloc(es, f"{tag}w{i}", [128, KC, 128], BF16) for i in range(nw)]
            ps = [palloc(es, f"{tag}p{i}", [128, TT]) for i in range(nps)]
            k = 0
            for i, (c0, mw) in enumerate(cols):
                w = wb[i % nw]
                S.dma("pool", w[:, :, 0:mw], W[:, c0:c0 + mw].rearrange("(c p) m -> p c m", p=128), writes=[w])
                if pre is not None:
                    pre(i)
                for tt in range(NT):
                    p = ps[k % nps]
                    k += 1
                    for ch in range(KC):
                        S.op("pe", lambda e, p=p, w=w, ch=ch, mw=mw, tt=tt: e.matmul(
                            p[0:mw, :], w[:, ch, 0:mw], hT[:, ch, tt * TT:(tt + 1) * TT],
                            start=(ch == 0), stop=(ch == KC - 1)), reads=[w, hT], writes=[p], inc=(ch == KC - 1))
                    epilogue(i, tt, p, mw)

        def chunks(n0, n):
            out = []
            x = n0
            while x < n0 + n:
                w = min(128, n0 + n - x)
                out.append((x, w))
                x += w
            return out

        def phase_proj(l, hT):
            with ExitStack() as es:
                stage = [alloc(es, f"pst{i}", [128, T]) for i in range(2)]
                cols = chunks(0, c.P_IN)

                def epi(i, tt, p, mw):
                    st = stage[i % 2]
                    eng = "act" if (i * NT + tt) % 2 == 0 else "dve"
                    sl = slice(tt * TT, (tt + 1) * TT)
                    if eng == "act":
                        S.op("act", lambda e: e.activation(st[0:mw, sl], p[0:mw, :], AF.Copy), reads=[p], writes=[st])
                    else:
                        S.op("dve", lambda e: e.tensor_copy(st[0:mw, sl], p[0:mw, :]), reads=[p], writes=[st])
                    if tt == NT - 1:
                        c0 = cols[i][0]
                        S.dma("sp", projT[c0:c0 + mw, :], st[0:mw, :], reads=[st])
                        if c.DA_W <= c0 < 3 * c.DA_W:
                            S.dma("sp", kvT[l, c0 - c.DA_W:c0 - c.DA_W + mw, :], st[0:mw, :], reads=[st])

                dense(es, hT, DC, w_in[l], cols, epi, "pj")
                S.barrier()

        def phase_wout(l, mixT, src, dst):
            with ExitStack() as es:
                xo = [alloc(es, f"wxo{i}", [128, T]) for i in range(2)]
                xn = [alloc(es, f"wxn{i}", [128, T]) for i in range(2)]
                cols = chunks(0, D)

                def pre(i):
                    S.dma("sp", xo[i % 2][:], src[i * 128:(i + 1) * 128, :], writes=[xo[i % 2]])

                def epi(i, tt, p, mw):
                    sl = slice(tt * TT, (tt + 1) * TT)
                    S.op("dve", lambda e: e.scalar_tensor_tensor(
                        xn[i % 2][:, sl], p[:], modcol(l, G1, i), xo[i % 2][:, sl], ALU.mult, ALU.add),
                        reads=[p, modT[l], xo[i % 2]], writes=[xn[i % 2]])
                    if tt == NT - 1:
                        S.dma("sp", dst[i * 128:(i + 1) * 128, :], xn[i % 2][:], reads=[xn[i % 2]])

                dense(es, mixT, DC, w_out[l], cols, epi, "wo", pre=pre)
                S.barrier()

        def phase_ffn_up(l, hT):
            KF = c.KF
            with ExitStack() as es:
                mprev = alloc(es, "mprev", [128, T], BF16)
                mnext = alloc(es, "mnext", [128, T], BF16)
                cw = alloc(es, "fcw", [128, 2 * KF, 3])
                cb = alloc(es, "fcb", [128, 2 * KF])
                up = [alloc(es, f"fup{i}", [128, T]) for i in range(2)]
                o = [alloc(es, f"fo{i}", [128, T]) for i in range(2)]
                xm1 = alloc(es, "fxm", [128, T])
                xm = [xm1, xm1]
                ab = [alloc(es, f"fab{i}", [128, T], BF16) for i in range(2)]
                S.dma("pool", mprev[:], mprev_d, writes=[mprev])
                S.dma("pool", mnext[:], mnext_d, writes=[mnext])
                S.dma("sp", cw[:], ffn_cw[l], writes=[cw])
                S.dma("sp", cb[:], ffn_cb[l], writes=[cb])
                cols = []
                for m in range(KF):
                    cols.append((m * 128, 128))
                    cols.append((c.DFF + m * 128, 128))

                def epi(i, tt, p, mw):
                    br = i % 2
                    m = i // 2
                    cidx = m if br == 0 else KF + m
                    u = up[br]
                    sl = slice(tt * TT, (tt + 1) * TT)
                    ob = o[br]
                    S.op("act", lambda e: e.activation(u[:, sl], p[:], AF.Copy), reads=[p], writes=[u])
                    S.op("act", lambda e: e.activation(ob[:, sl], p[:], AF.Identity, bias=0.0, scale=cw[:, cidx, 1:2]),
                         reads=[p, cw], writes=[ob])
                    if tt < NT - 1:
                        return
                    S.op("dve", lambda e: e.tensor_tensor(xm[0][:, 0:T - 1], u[:, 0:T - 1], mprev[:, 1:T], ALU.mult),
                         reads=[u, mprev], writes=[xm[0]])
                    S.op("dve", lambda e: e.scalar_tensor_tensor(ob[:, 1:T], xm[0][:, 0:T - 1], cw[:, cidx, 0:1], ob[:, 1:T], ALU.mult, ALU.add),
                         reads=[xm[0], cw, ob], writes=[ob])
                    S.op("pool", lambda e: e.tensor_tensor(xm[1][:, 0:T - 1], u[:, 1:T], mnext[:, 0:T - 1], ALU.mult),
                         reads=[u, mnext], writes=[xm[1]])
                    S.op("dve", lambda e: e.scalar_tensor_tensor(ob[:, 0:T - 1], xm[1][:, 0:T - 1], cw[:, cidx, 2:3], ob[:, 0:T - 1], ALU.mult, ALU.add),
                         reads=[xm[1], cw, ob], writes=[ob])
                    if br == 0:
                        S.op("act", lambda e: e.activation(ob[:], ob[:], AF.Silu, bias=cb[:, cidx:cidx + 1]), reads=[ob, cb], writes=[ob])
                    else:
                        a = ab[m % 2]
                        S.op("dve", lambda e: e.scalar_tensor_tensor(a[:], ob[:], cb[:, cidx:cidx + 1], o[0][:], ALU.add, ALU.mult),
                             reads=[ob, cb, o[0]], writes=[a])
                        S.dma("sp", actT[m * 128:(m + 1) * 128, :], a[:], reads=[a])

                dense(es, hT, DC, ffn_up[l], cols, epi, "fu", nw=2)
                S.barrier()

        def phase_ffn_down(l, src, dst):
            KF = c.KF
            with ExitStack() as es:
                at = alloc(es, "fdact", [128, KF, TT], BF16)
                wdn_bf = dscr("wdn_bf", [DC, 128, KF * 128], BF16)
                wdn_b = [Buf() for _ in range(DC)]
                wb = [alloc(es, f"fdw{i}", [128, KF, 128], BF16) for i in range(2)]
                xo = [alloc(es, f"fdxo{i}", [128, TT]) for i in range(3)]
                xn = [alloc(es, f"fdxn{i}", [128, TT]) for i in range(3)]
                ps = [palloc(es, f"fdp{i}", [128, TT]) for i in range(3)]
                k = 0
                for tt in range(NT):
                    sl = slice(tt * TT, (tt + 1) * TT)
                    S.dma("sp", at[:], actT[:, sl].rearrange("(c p) t -> p c t", p=128), writes=[at])
                    for m in range(DC):
                        w = wb[k % 2]
                        p = ps[k % 3]
                        x_o = xo[k % 3]
                        x_n = xn[k % 3]
                        k += 1
                        if tt == 0:
                            S.dma("pool", w[:], ffn_down[l][:, m * 128:(m + 1) * 128].rearrange("(c p) m -> p c m", p=128), writes=[w])
                            if NT > 1:
                                S.dma("sp", wdn_bf[m], w[:].rearrange("p c m -> p (c m)"), reads=[w], writes=[wdn_b[m]])
                        else:
                            S.dma("act", w[:].rearrange("p c m -> p (c m)"), wdn_bf[m], reads=[wdn_b[m]], writes=[w])
                        S.dma("sp", x_o[:], src[m * 128:(m + 1) * 128, sl], writes=[x_o])
                        for ch in range(KF):
                            S.op("pe", lambda e, p=p, w=w, ch=ch: e.matmul(p[:], w[:, ch, :], at[:, ch, :], start=(ch == 0), stop=(ch == KF - 1)),
                                 reads=[w, at], writes=[p], inc=(ch == KF - 1))
                        S.op("dve", lambda e, p=p, x_o=x_o, x_n=x_n, m=m: e.scalar_tensor_tensor(
                            x_n[:], p[:], modcol(l, G2, m), x_o[:], ALU.mult, ALU.add), reads=[p, modT[l], x_o], writes=[x_n])
                        S.dma("sp", dst[m * 128:(m + 1) * 128, sl], x_n[:], reads=[x_n])
                S.barrier()


        def phase_attn(l):
            H, NB, NK, NKB = c.H, c.NB, c.NK, c.NKB
            lam_init = 0.8 - 0.6 * math.exp(-0.3 * l)
            scale = 128.0 ** -0.5
            with ExitStack() as es:
                ropeC = alloc(es, "ropeC", [128, T])
                ropeS = alloc(es, "ropeS", [128, T])
                qaug = alloc(es, "qaug", [8, T], BF16)
                kaug = alloc(es, "kaug", [8, NK], BF16)
                lamt = alloc(es, "lamt", [128, 512])
                lamp = alloc(es, "lamp", [128, 512])
                lr = alloc(es, "lr", [128, 4])
                nlam = alloc(es, "nlam", [128, 1])
                sg = alloc(es, "sg", [128, 2])
                identb = alloc(es, "identb", [128, 128], BF16)
                ld = [alloc(es, f"ald{i}", [128, TT]) for i in range(4)]
                qr = [[alloc(es, f"qr{m}{i}", [128, TT], BF16) for i in range(2)] for m in range(2)]
                kr = [alloc(es, f"kr{m}", [128, NK], BF16) for m in range(2)]
                vt = [alloc(es, f"vt{i}", [128, TT]) for i in range(2)]
                vtok = alloc(es, "vtok", [128, NKB, 256], BF16)
                pt = [alloc(es, f"pt{i}", [128, TT], BF16) for i in range(3)]
                ym = [alloc(es, f"ym{m}", [128, 2, TT]) for m in range(2)]
                yd = alloc(es, "yd", [128, 2, TT])
                sq = alloc(es, "asq", [128, 2, TT])
                rden = alloc(es, "rden", [128, TT])
                rs = alloc(es, "ars", [128, TT])
                tq = alloc(es, "atq", [128, TT])
                ob = [alloc(es, f"aob{i}", [128, TT]) for i in range(2)]
                po = [palloc(es, f"apo{i}", [128, TT]) for i in range(2)]
                pden = palloc(es, "apden", [128, TT])
                pss = [palloc(es, f"apss{i}", [128, TT]) for i in range(2)]
                pss2 = palloc(es, "apss2", [128, TT])
                pst = palloc(es, "apst", [128, 128])
                S.dma("sp", ropeC[:], ropeC_d, writes=[ropeC])
                S.dma("sp", ropeS[:], ropeS_d, writes=[ropeS])
                S.dma("pool", qaug[:], qaug_d, writes=[qaug])
                S.dma("pool", kaug[:], kaug_d, writes=[kaug])
                S.dma("sp", lamt[:], lam_d[l], writes=[lamt])
                S.dma("sp", sg[:], subg_d[l], writes=[sg])
                S.op("dve", lambda e: e.tensor_copy(identb[:], ident[:]), reads=[ident], writes=[identb])
                S.op("dve", lambda e: e.tensor_tensor(lamp[:, 0:128], lamt[:, 0:128], lamt[:, 128:256], ALU.mult), reads=[lamt], writes=[lamp])
                S.op("dve", lambda e: e.tensor_tensor(lamp[:, 128:256], lamt[:, 256:384], lamt[:, 384:512], ALU.mult), reads=[lamt], writes=[lamp])
                S.op("dve", lambda e: e.tensor_reduce(lr[:, 0:2], lamp[:, 0:256].rearrange("p (a b) -> p a b", a=2), AX.X, ALU.add), reads=[lamp], writes=[lr])
                S.op("act", lambda e: e.activation(lr[:, 2:4], lr[:, 0:2], AF.Exp), reads=[lr], writes=[lr])
                S.op("dve", lambda e: e.tensor_tensor(nlam[:], lr[:, 3:4], lr[:, 2:3], ALU.subtract), reads=[lr], writes=[nlam])
                S.op("dve", lambda e: e.tensor_scalar(nlam[:], nlam[:], -lam_init, None, ALU.add), reads=[nlam], writes=[nlam])
                S.op("dve", lambda e: e.tensor_scalar(sg[:], sg[:], 1.0 - lam_init, None, ALU.mult), reads=[sg], writes=[sg])

                def rope_tile(rows0, tt, dst_ap, dst_tile, k):
                    sl = slice(tt * TT, (tt + 1) * TT)
                    a = ld[(2 * k) % 4]
                    b = ld[(2 * k + 1) % 4]
                    S.dma("sp", a[:], projT[rows0:rows0 + 128, sl], writes=[a])
                    for blk in range(4):
                        pb = blk ^ 1
                        S.dma("sp", b[blk * 32:(blk + 1) * 32, :], projT[rows0 + pb * 32:rows0 + (pb + 1) * 32, sl], writes=[b])
                    S.op("dve", lambda e: e.tensor_tensor(a[:], a[:], ropeC[:, sl], ALU.mult), reads=[a, ropeC], writes=[a])
                    S.op("pool", lambda e: e.tensor_tensor(b[:], b[:], ropeS[:, sl], ALU.mult), reads=[b, ropeS], writes=[b])
                    S.op("dve", lambda e: e.tensor_tensor(dst_ap, a[:], b[:], ALU.add), reads=[a, b], writes=[dst_tile])

                kctr = 0
                for h in range(H):
                    for m in range(2):
                        for tt in range(NT):
                            rope_tile(c.DA_W + h * 256 + m * 128, tt, kr[m][:, tt * TT:(tt + 1) * TT], kr[m], kctr)
                            kctr += 1
                        S.dma("pool", kr[m][:, T:NK], kcT_d[l, h, m], writes=[kr[m]])
                    vk = 0
                    for e_ in range(2):
                        for tt in range(NT):
                            v = vt[vk % 2]
                            vk += 1
                            r0 = 2 * c.DA_W + h * 256 + e_ * 128
                            S.dma("sp", v[:], projT[r0:r0 + 128, tt * TT:(tt + 1) * TT], writes=[v])
                            for j in range(TT // 128):
                                kb = tt * (TT // 128) + j
                                S.op("pe", lambda e, v=v, j=j: e.transpose(pst[:], v[:, j * 128:(j + 1) * 128], ident[:]),
                                     reads=[v, ident], writes=[pst])
                                S.op("act", lambda e, kb=kb, e_=e_: e.activation(vtok[:, kb, e_ * 128:(e_ + 1) * 128], pst[:], AF.Copy),
                                     reads=[pst], writes=[vtok])
                    S.dma("pool", vtok[:, NB:NKB, :], vc_d[l, h].rearrange("(j p) e -> p j e", p=128), writes=[vtok])
                    for tt in range(NT):
                        sl = slice(tt * TT, (tt + 1) * TT)
                        for m in range(2):
                            q = qr[m][tt % 2]
                            rope_tile(h * 256 + m * 128, tt, q[:], q, kctr)
                            kctr += 1
                            for kb in range(NKB):
                                ps_ = pss[kb % 2]
                                ksl = slice(kb * 128, (kb + 1) * 128)
                                S.op("pe", lambda e, ps_=ps_, ksl=ksl, q=q, m=m: e.matmul(ps_[:], kr[m][:, ksl], q[:], start=True, stop=False),
                                     reads=[kr[m], q], writes=[ps_], inc=False)
                                S.op("pe", lambda e, ps_=ps_, ksl=ksl, sl=sl: e.matmul(ps_[:], kaug[:, ksl], qaug[:, sl], start=False, stop=True),
                                     reads=[kaug, qaug], writes=[ps_])
                                p_ = pt[kb % 3]
                                S.op("act", lambda e, p_=p_, ps_=ps_: e.activation(p_[:], ps_[:], AF.Exp, bias=-scale * MASK_BIG, scale=scale),
                                     reads=[ps_], writes=[p_])
                                st, sp_ = (kb == 0), (kb == NKB - 1)
                                S.op("pe", lambda e, p_=p_, kb=kb, st=st, sp_=sp_: e.matmul(po[0][:], vtok[:, kb, 0:128], p_[:], start=st, stop=sp_),
                                     reads=[vtok, p_], writes=[po[0]], inc=False)
                                S.op("pe", lambda e, p_=p_, kb=kb, st=st, sp_=sp_: e.matmul(po[1][:], vtok[:, kb, 128:256], p_[:], start=st, stop=sp_),
                                     reads=[vtok, p_], writes=[po[1]], inc=False)
                                S.op("pe", lambda e, p_=p_, st=st, sp_=sp_: e.matmul(pden[:], ones_b[:], p_[:], start=st, stop=sp_),
                                     reads=[ones_b, p_], writes=[pden])
                            S.op("dve", lambda e: e.reciprocal(rden[:], pden[:]), reads=[pden], writes=[rden])
                            for e_ in range(2):
                                S.op("dve", lambda e, e_=e_, m=m: e.tensor_tensor(ym[m][:, e_, :], po[e_][:], rden[:], ALU.mult),
                                     reads=[po[e_], rden], writes=[ym[m]])
                        S.op("dve", lambda e: e.scalar_tensor_tensor(yd[:], ym[1][:], nlam[:, 0:1], ym[0][:], ALU.mult, ALU.add),
                             reads=[ym[0], ym[1], nlam], writes=[yd])
                        S.op("act", lambda e: e.activation(sq[:], yd[:], AF.Square), reads=[yd], writes=[sq])
                        for e_ in range(2):
                            S.op("pe", lambda e, e_=e_: e.matmul(pss2[:], ones_f[:], sq[:, e_, :], start=(e_ == 0), stop=(e_ == 1)),
                                 reads=[ones_f, sq], writes=[pss2])
                        S.op("act", lambda e: e.activation(rs[:], pss2[:], AF.Sqrt, bias=NORM_EPS, scale=1.0 / 256), reads=[pss2], writes=[rs])
                        S.op("dve", lambda e: e.reciprocal(rs[:], rs[:]), reads=[rs], writes=[rs])
                        for e_ in range(2):
                            o_ = ob[e_]
                            S.op("dve", lambda e, e_=e_: e.tensor_tensor(tq[:], yd[:, e_, :], rs[:], ALU.mult), reads=[yd, rs], writes=[tq])
                            S.op("act", lambda e, e_=e_, o_=o_: e.activation(o_[:], tq[:], AF.Identity, bias=0.0, scale=sg[:, e_:e_ + 1]),
                                 reads=[tq, sg], writes=[o_])
                            r0 = h * 256 + e_ * 128
                            S.dma("sp", mixT_d[r0:r0 + 128, sl], o_[:], reads=[o_])
                S.barrier()

        def phase_cm(l):
            NB = c.NB
            U0 = c.UV0
            V0 = c.UV0 + c.CM_W
            M0 = c.DA_W + c.RW_W
            CG = c.CM_W // 4
            cbw = min(128, CG)
            with ExitStack() as es:
                rstd = alloc(es, "crstd", [128, T])
                cmg = alloc(es, "cmg", [128, CMC])
                identb = alloc(es, "cidentb", [128, 128], BF16)
                wsb = alloc(es, "wsb", [128, 4, 128], BF16)
                bsr = alloc(es, "bsr", [128, 4, 128])
                xt = [alloc(es, f"cx{i}", [128, TT]) for i in range(3)]
                sq = [alloc(es, f"csq{i}", [128, TT]) for i in range(2)]
                vfull = [alloc(es, f"cv{i}", [128, T]) for i in range(2)]
                zb = [alloc(es, f"czb{i}", [128, T], BF16) for i in range(2)]
                ztok = alloc(es, "ztok", [128, NB, c.CM_W], BF16)
                ut = [alloc(es, f"cu{i}", [128, T]) for i in range(2)]
                ot = [alloc(es, f"co{i}", [128, T]) for i in range(2)]
                ps = [palloc(es, f"cps{i}", [128, TT]) for i in range(2)]
                pstb = [palloc(es, f"cpst{i}", [128, 128], BF16) for i in range(2)]
                pm = [palloc(es, f"cpm{i}", [128, 512]) for i in range(2)]
                S.dma("sp", cmg[:], cmg_d[l], writes=[cmg])
                S.dma("pool", wsb[:], wsT_d[l].rearrange("g q p -> q g p"), writes=[wsb])
                S.dma("sp", bsr[:], bsr_d[l].rearrange("g q p -> q g p"), writes=[bsr])
                S.op("dve", lambda e: e.tensor_copy(identb[:], ident[:]), reads=[ident], writes=[identb])
                k = 0
                for tt in range(NT):
                    p = ps[tt % 2]
                    for ch in range(CMC):
                        x = xt[k % 3]
                        q = sq[k % 2]
                        k += 1
                        S.dma("sp", x[:], projT[V0 + ch * 128:V0 + (ch + 1) * 128, tt * TT:(tt + 1) * TT], writes=[x])
                        S.op("act", lambda e, q=q, x=x: e.activation(q[:], x[:], AF.Square), reads=[x], writes=[q])
                        S.op("pe", lambda e, p=p, q=q, ch=ch: e.matmul(p[:], ones_f[:], q[:], start=(ch == 0), stop=(ch == CMC - 1)),
                             reads=[ones_f, q], writes=[p])
                    sl = slice(tt * TT, (tt + 1) * TT)
                    S.op("act", lambda e, p=p, sl=sl: e.activation(rstd[:, sl], p[:], AF.Sqrt, bias=NORM_EPS, scale=1.0 / c.CM_W),
                         reads=[p], writes=[rstd])
                    S.op("dve", lambda e, sl=sl: e.reciprocal(rstd[:, sl], rstd[:, sl]), reads=[rstd], writes=[rstd])
                k = 0
                for ch in range(CMC):
                    v = vfull[ch % 2]
                    z = zb[ch % 2]
                    S.dma("sp", v[:], projT[V0 + ch * 128:V0 + (ch + 1) * 128, :], writes=[v])
                    S.op("dve", lambda e, v=v: e.tensor_tensor(v[:], v[:], rstd[:], ALU.mult), reads=[v, rstd], writes=[v])
                    S.op("act", lambda e, v=v, z=z, ch=ch: e.activation(z[:], v[:], AF.Identity, bias=0.0, scale=cmg[:, ch:ch + 1]),
                         reads=[v, cmg], writes=[z])
                    for n in range(NB):
                        pb = pstb[k % 2]
                        k += 1
                        S.op("pe", lambda e, pb=pb, z=z, n=n: e.transpose(pb[:], z[:, n * 128:(n + 1) * 128], identb[:]),
                             reads=[z, identb], writes=[pb])
                        if k % 2 == 0:
                            S.op("dve", lambda e, pb=pb, n=n, ch=ch: e.tensor_copy(ztok[:, n, ch * 128:(ch + 1) * 128], pb[:]),
                                 reads=[pb], writes=[ztok])
                        else:
                            S.op("act", lambda e, pb=pb, n=n, ch=ch: e.activation(ztok[:, n, ch * 128:(ch + 1) * 128], pb[:], AF.Copy),
                                 reads=[pb], writes=[ztok])
                bi = 0
                k = 0
                for g in range(4):
                    for sub in range(CG // cbw):
                        cb0 = g * CG + sub * cbw
                        u = ut[bi % 2]
                        o = ot[bi % 2]
                        bi += 1
                        S.dma("sp", u[0:cbw, :], projT[U0 + cb0:U0 + cb0 + cbw, :], writes=[u])
                        for n0 in range(0, NB, 4):
                            p = pm[k % 2]
                            k += 1
                            nn = min(4, NB - n0)
                            for j in range(nn):
                                n = n0 + j
                                S.op("pe", lambda e, p=p, j=j, n=n, cb0=cb0, g=g: e.matmul(
                                    p[0:cbw, j * 128:(j + 1) * 128], ztok[:, n, cb0:cb0 + cbw], wsb[:, g, :], start=True, stop=True),
                                    reads=[ztok, wsb], writes=[p])
                            for j in range(nn):
                                n = n0 + j
                                S.op("dve", lambda e, p=p, j=j, n=n, o=o, g=g: e.tensor_tensor(
                                    o[0:cbw, n * 128:(n + 1) * 128], p[0:cbw, j * 128:(j + 1) * 128], bsr[0:cbw, g, :], ALU.add),
                                    reads=[p, bsr], writes=[o])
                        S.op("pool", lambda e, o=o, u=u: e.tensor_tensor(o[0:cbw, :], o[0:cbw, :], u[0:cbw, :], ALU.mult), reads=[o, u], writes=[o])
                        S.dma("sp", mixT_d[M0 + cb0:M0 + cb0 + cbw, :], o[0:cbw, :], reads=[o])
                S.barrier()


        def phase_rw_pre(l):
            R0 = c.RW0
            with ExitStack() as es:
                mprev = alloc(es, "rmprev", [128, T], BF16)
                mnext = alloc(es, "rmnext", [128, T], BF16)
                cmask = alloc(es, "rcmask", [128, T])
                cw = alloc(es, "rcw", [128, NRC, 3])
                w0 = alloc(es, "rw0", [128, 2, RG])
                a0 = alloc(es, "ra0", [128, 2, RG])
                prm = alloc(es, "rprm", [128, 5, RG])
                bones = alloc(es, "rbones", [128, 128])
                w2 = alloc(es, "rw2", [128, RW_W])
                a2 = alloc(es, "ra2", [128, RW_W])
                g2a = alloc(es, "rg2a", [128, RW_W])
                g2b = alloc(es, "rg2b", [32, RW_W])
                raw = [alloc(es, f"rraw{i}", [128, T]) for i in range(2)]
                xm = alloc(es, "rxm", [128, T])
                tdec = alloc(es, "rtdec", [128, T])
                aac = alloc(es, "raac", [128, T])
                sgl = alloc(es, "rsgl", [128, T])
                sgl2 = alloc(es, "rsgl2", [32, T])
                rt = alloc(es, "rrt", [128, T])
                kt = alloc(es, "rkt", [128, T])
                vt_ = alloc(es, "rvt", [128, T])
                kk = alloc(es, "rkk", [128, T])
                t1 = alloc(es, "rt1", [128, T])
                t2 = alloc(es, "rt2", [128, T])
                t3 = alloc(es, "rt3", [128, T])
                lw = alloc(es, "rlw", [128, T])
                av = alloc(es, "rav", [128, T])
                kdsum = alloc(es, "rkds", [128, T])
                pend = alloc(es, "rpend", [128, NCH])
                ltc = alloc(es, "rltc", [128, NCH, 1])
                ps = [palloc(es, f"rps{i}", [128, TT]) for i in range(4)]
                S.dma("pool", mprev[:], mprev_d, writes=[mprev])
                S.dma("pool", mnext[:], mnext_d, writes=[mnext])
                S.dma("sp", cmask[:], cmask_d, writes=[cmask])
                S.dma("sp", cw[:], rw_cw_d[l], writes=[cw])
                S.dma("sp", w0[:], w0T_d[l], writes=[w0])
                S.dma("sp", a0[:], a0T_d[l], writes=[a0])
                S.dma("sp", prm[:], rwp_d[l], writes=[prm])
                cka = alloc(es, "rcka", [128, RG])
                S.op("dve", lambda e: e.tensor_scalar(cka[:], prm[:, 1, :], -1.0, None, ALU.mult), reads=[prm], writes=[cka])
                S.op("dve", lambda e: e.tensor_scalar(cka[:], cka[:], 1.0, None, ALU.add), reads=[cka], writes=[cka])
                S.dma("sp", bones[:], bones_d, writes=[bones])
                S.dma("sp", w2[:], w2_d[l].rearrange("d r c -> (d r) c"), writes=[w2])
                S.dma("sp", a2[:], a2_d[l].rearrange("d r c -> (d r) c"), writes=[a2])
                S.dma("sp", g2a[:], g2_d[l, 0:128, :], writes=[g2a])
                S.dma("sp", g2b[:], g2_d[l, 128:160, :], writes=[g2b])
                pk = [0]
                rk = [0]

                def conv_rows(row0, nrows, cidx, dst):
                    u = raw[rk[0] % 2]
                    rk[0] += 1
                    n = nrows
                    S.dma("sp", u[0:n, :], projT[row0:row0 + n, :], writes=[u])
                    S.op("act", lambda e: e.activation(dst[0:n, :], u[0:n, :], AF.Identity, bias=0.0, scale=cw[0:n, cidx, 1:2]), reads=[u, cw], writes=[dst])
                    S.op("dve", lambda e: e.tensor_tensor(xm[0:n, 0:T - 1], u[0:n, 0:T - 1], mprev[0:n, 1:T], ALU.mult), reads=[u, mprev], writes=[xm])
                    S.op("dve", lambda e: e.scalar_tensor_tensor(dst[0:n, 1:T], xm[0:n, 0:T - 1], cw[0:n, cidx, 0:1], dst[0:n, 1:T], ALU.mult, ALU.add),
                         reads=[xm, cw, dst], writes=[dst])
                    S.op("dve", lambda e: e.tensor_tensor(xm[0:n, 0:T - 1], u[0:n, 1:T], mnext[0:n, 0:T - 1], ALU.mult), reads=[u, mnext], writes=[xm])
                    S.op("dve", lambda e: e.scalar_tensor_tensor(dst[0:n, 0:T - 1], xm[0:n, 0:T - 1], cw[0:n, cidx, 2:3], dst[0:n, 0:T - 1], ALU.mult, ALU.add),
                         reads=[xm, cw, dst], writes=[dst])

                cdec = 3 * RG
                conv_rows(R0 + 3 * RW_W, 128, cdec, tdec)
                S.op("act", lambda e: e.activation(tdec[:], tdec[:], AF.Tanh), reads=[tdec], writes=[tdec])
                conv_rows(R0 + 3 * RW_W + 128, 128, cdec + 1, aac)
                conv_rows(R0 + 3 * RW_W + 256, 128, cdec + 2, sgl)
                S.op("act", lambda e: e.activation(sgl[:], sgl[:], AF.Sigmoid), reads=[sgl], writes=[sgl])
                conv_rows(R0 + 3 * RW_W + 384, 32, cdec + 3, sgl2)
                S.op("act", lambda e: e.activation(sgl2[:], sgl2[:], AF.Sigmoid), reads=[sgl2], writes=[sgl2])

                def headsum(dst, src):
                    for tt in range(NT):
                        p = ps[pk[0] % 4]
                        pk[0] += 1
                        sl = slice(tt * TT, (tt + 1) * TT)
                        S.op("pe", lambda e, p=p, sl=sl: e.matmul(p[:], bones[:], src[:, sl], start=True, stop=True), reads=[bones, src], writes=[p])
                        S.op("act", lambda e, p=p, sl=sl: e.activation(dst[:, sl], p[:], AF.Copy), reads=[p], writes=[dst])

                for g in range(RG):
                    gs = slice(g * 128, (g + 1) * 128)
                    conv_rows(R0 + g * 128, 128, g, rt)
                    conv_rows(R0 + RW_W + g * 128, 128, RG + g, kt)
                    conv_rows(R0 + 2 * RW_W + g * 128, 128, 2 * RG + g, vt_)
                    S.dma("sp", rwV[gs, :], vt_[:], reads=[vt_])
                    for tt in range(NT):
                        p = ps[pk[0] % 4]
                        pk[0] += 1
                        sl = slice(tt * TT, (tt + 1) * TT)
                        S.op("pe", lambda e, p=p, sl=sl: e.matmul(p[:], g2a[:, gs], sgl[:, sl], start=True, stop=False), reads=[g2a, sgl], writes=[p])
                        S.op("pe", lambda e, p=p, sl=sl: e.matmul(p[:], g2b[:, gs], sgl2[:, sl], start=False, stop=True), reads=[g2b, sgl2], writes=[p])
                        S.op("act", lambda e, p=p, sl=sl: e.activation(t1[:, sl], p[:], AF.Copy), reads=[p], writes=[t1])
                    S.dma("sp", rwGate[gs, :], t1[:], reads=[t1])
                    S.op("pool", lambda e: e.tensor_scalar(kk[:], kt[:], prm[:, 0, g:g + 1], None, ALU.mult), reads=[kt, prm], writes=[kk])
                    S.op("act", lambda e: e.activation(t2[:], kk[:], AF.Square), reads=[kk], writes=[t2])
                    headsum(t3, t2)
                    S.op("act", lambda e: e.activation(t3[:], t3[:], AF.Sqrt, bias=1e-12, scale=1.0), reads=[t3], writes=[t3])
                    S.op("dve", lambda e: e.reciprocal(t3[:], t3[:]), reads=[t3], writes=[t3])
                    S.op("dve", lambda e: e.tensor_tensor(kk[:], kk[:], t3[:], ALU.mult), reads=[kk, t3], writes=[kk])
                    for d in range(2):
                        ds_ = slice(d * 64, (d + 1) * 64)
                        for tt in range(NT):
                            sl = slice(tt * TT, (tt + 1) * TT)
                            p = ps[pk[0] % 4]
                            pk[0] += 1
                            S.op("pe", lambda e, p=p, sl=sl: e.matmul(p[:], w2[ds_, gs], tdec[ds_, sl], start=True, stop=True), reads=[w2, tdec], writes=[p])
                            S.op("act", lambda e, p=p, sl=sl: e.activation(lw[:, sl], p[:], AF.Sigmoid, bias=w0[:, d, g:g + 1], scale=1.0),
                                 reads=[p, w0], writes=[lw])
                            p = ps[pk[0] % 4]
                            pk[0] += 1
                            S.op("pe", lambda e, p=p, sl=sl: e.matmul(p[:], a2[ds_, gs], aac[ds_, sl], start=True, stop=True), reads=[a2, aac], writes=[p])
                            S.op("act", lambda e, p=p, sl=sl: e.activation(av[:, sl], p[:], AF.Sigmoid, bias=a0[:, d, g:g + 1], scale=1.0),
                                 reads=[p, a0], writes=[av])
                        S.op("pool", lambda e: e.tensor_scalar(lw[:], lw[:], -math.exp(-0.5), None, ALU.mult), reads=[lw], writes=[lw])
                        S.op("act", lambda e: e.activation(t1[:], av[:], AF.Identity, bias=cka[:, g:g + 1], scale=prm[:, 1, g:g + 1]),
                             reads=[av, prm, cka], writes=[t1])
                        S.op("dve", lambda e: e.tensor_tensor(t1[:], t1[:], kt[:], ALU.mult), reads=[t1, kt], writes=[t1])
                        if d == 0:
                            S.op("pool", lambda e: e.tensor_copy(kdsum[:], t1[:]), reads=[t1], writes=[kdsum])
                        else:
                            S.op("pool", lambda e: e.tensor_tensor(kdsum[:], kdsum[:], t1[:], ALU.add), reads=[t1, kdsum], writes=[kdsum])
                        S.op("dve", lambda e: e.tensor_tensor_scan(t2[:], cmask[:], lw[:], 0.0, ALU.mult, ALU.add), reads=[cmask, lw], writes=[t2])
                        S.op("dve", lambda e: e.tensor_copy(ltc[:], t2[:].rearrange("p (n c) -> p n c", c=CK)[:, :, CK - 1:CK]), reads=[t2], writes=[ltc])
                        ltot = ltc[:]
                        S.op("act", lambda e: e.activation(pend[:].rearrange("p (n o) -> p n o", o=1), ltot, AF.Exp), reads=[ltc], writes=[pend])
                        S.dma("sp", rwP[d][gs, :], pend[:], reads=[pend])
                        if d == 1:
                            S.op("dve", lambda e: e.tensor_tensor(t3[:], lw[:], t2[:], ALU.subtract), reads=[lw, t2], writes=[t3])
                            S.op("dve", lambda e: e.tensor_tensor(
                                t2[:].rearrange("p (n c) -> p n c", c=CK), t3[:].rearrange("p (n c) -> p n c", c=CK),
                                ltot.to_broadcast([128, NCH, CK]), ALU.add), reads=[t3, ltc], writes=[t2])
                        S.op("act", lambda e: e.activation(t3[:], t2[:], AF.Exp), reads=[t2], writes=[t3])
                        S.op("dve", lambda e: e.tensor_tensor(t3[:], t3[:], rt[:], ALU.mult), reads=[t3, rt], writes=[t3])
                        S.dma("sp", rwR[d][gs, :], t3[:], reads=[t3])
                        S.op("act", lambda e: e.activation(t3[:], t2[:], AF.Exp, scale=-1.0), reads=[t2], writes=[t3])
                        S.op("dve", lambda e: e.tensor_tensor(t1[:], t1[:], t3[:], ALU.mult), reads=[t3, t1], writes=[t1])
                        S.dma("sp", rwK[d][gs, :], t1[:], reads=[t1])
                        S.op("dve", lambda e: e.tensor_tensor(t3[:], t3[:], kk[:], ALU.mult), reads=[t3, kk], writes=[t3])
                        S.op("dve", lambda e: e.tensor_tensor(t3[:], t3[:], av[:], ALU.mult), reads=[t3, av], writes=[t3])
                        S.dma("sp", rwB[d][gs, :], t3[:], reads=[t3])
                        S.op("dve", lambda e: e.tensor_tensor(t2[:], t2[:], lw[:], ALU.subtract), reads=[t2, lw], writes=[t2])
                        S.op("act", lambda e: e.activation(t2[:], t2[:], AF.Exp), reads=[t2], writes=[t2])
                        S.op("dve", lambda e: e.tensor_tensor(t2[:], t2[:], kk[:], ALU.mult), reads=[t2, kk], writes=[t2])
                        S.dma("sp", rwA[d][gs, :], t2[:], reads=[t2])
                    S.op("dve", lambda e: e.scalar_tensor_tensor(kdsum[:], kdsum[:], prm[:, 2, g:g + 1], rt[:], ALU.mult, ALU.mult),
                         reads=[kdsum, prm, rt], writes=[kdsum])
                    headsum(t1, kdsum)
                    S.op("dve", lambda e: e.tensor_tensor(t1[:], t1[:], vt_[:], ALU.mult), reads=[t1, vt_], writes=[t1])
                    S.dma("sp", rwBon[gs, :], t1[:], reads=[t1])
                S.barrier()

        def phase_rw_scan(l):
            HB = min(8, RH)
            NU = RH // HB
            with ExitStack() as es:
                tri = alloc(es, "tri", [64, 4, 64])
                keep = alloc(es, "keepc", [128, 2, NCH])
                S.dma("sp", tri[:], tri_d, writes=[tri])
                S.dma("sp", keep[:], keep_d, writes=[keep])
                SU, U_, SL, LW = 0, 1, 2, 3
                mk = {0: dict(N=SU, NT=SL, G=SU, H=U_), 1: dict(N=SL, NT=SU, G=SL, H=LW)}
                D_ = {}
                for d in range(2):
                    t = {}
                    for nm in ("A", "B", "K", "R", "V", "Kt", "Bt", "Vt", "N", "NT", "G", "Hk", "Hb", "X", "Pa", "PaT", "Pb", "PbT",
                               "XT", "nS", "Y", "ST"):
                        t[nm] = alloc(es, f"s{nm}{d}", [64, RH, 64])
                    t["P"] = alloc(es, f"sP{d}", [64, RH, NCH])
                    D_[d] = t
                    S.dma("sp", t["ST"][:], s0T_d[l, d].rearrange("g (j k) v -> k (g j) v", j=2), writes=[t["ST"]])
                    S.dma("sp", t["P"][:], rwP[d].rearrange("(h k) n -> k h n", k=64), writes=[t["P"]])
                psT = palloc(es, "spsT", [64, 512])
                ps1 = [palloc(es, f"sps1{i}", [64, 512]) for i in range(2)]
                ps2 = [palloc(es, f"sps2{i}", [64, 512]) for i in range(2)]
                ps3 = [palloc(es, f"sps3{i}", [64, 512]) for i in range(3)]
                ctr = dict(p1=0, p2=0, p3=0, ev=0)
                I64 = ident[0:64, 0:64]

                def flat(ap):
                    return ap.rearrange("p h c -> p (h c)")

                def evac(dst_ap, dst_t, p, cols):
                    ctr["ev"] += 1
                    if ctr["ev"] % 2 == 0:
                        S.op("act", lambda e: e.activation(dst_ap, p[0:64, 0:cols], AF.Copy), reads=[p], writes=[dst_t])
                    else:
                        S.op("dve", lambda e: e.tensor_copy(dst_ap, p[0:64, 0:cols]), reads=[p], writes=[dst_t])

                def hview(tile_, u):
                    return tile_[:, u * HB:(u + 1) * HB, :]

                def pview(p):
                    return p[0:64, 0:HB * 64].rearrange("p (h c) -> p h c", c=64)

                def load_chunk(d, n):
                    t = D_[d]
                    cs = slice(n * CK, (n + 1) * CK)
                    for nm, src in (("A", rwA[d]), ("B", rwB[d]), ("K", rwK[d]), ("R", rwR[d]), ("V", rwV)):
                        S.dma("sp", t[nm][:], src[:, cs].rearrange("(h k) t -> k h t", k=64), writes=[t[nm]])

                order = {0: list(range(NCH)), 1: list(range(NCH - 1, -1, -1))}
                for step in range(NCH):
                    for d in range(2):
                        t = D_[d]
                        n = order[d][step]
                        load_chunk(d, n)
                        A, B, K_, R, V = (t[x] for x in ("A", "B", "K", "R", "V"))
                        m = mk[d]
                        S.op("dve", lambda e, t=t, n=n, d=d: e.tensor_scalar(flat(t["ST"][:]), flat(t["ST"][:]), keep[0:64, d, n:n + 1], None, ALU.mult),
                             reads=[t["ST"], keep], writes=[t["ST"]])
                        for u in range(NU):
                            hs = [u * HB + i for i in range(HB)]
                            for nm_src, nm_dst in ((K_, "Kt"), (B, "Bt"), (V, "Vt")):
                                for i, hd in enumerate(hs):
                                    S.op("pe", lambda e, i=i, hd=hd, nm_src=nm_src: e.transpose(
                                        psT[0:64, i * 64:(i + 1) * 64], nm_src[:, hd, :], I64), reads=[nm_src, ident], writes=[psT], inc=(i == HB - 1))
                                evac(flat(hview(t[nm_dst], u)), t[nm_dst], psT, HB * 64)
                            for (dst, lt, rt_, mask) in (("N", B, A, m["N"]), ("NT", A, B, m["NT"]), ("G", K_, A, m["G"]),
                                                          ("Hk", K_, R, m["H"]), ("Hb", B, R, m["H"])):
                                p = ps1[ctr["p1"] % 2]
                                ctr["p1"] += 1
                                for i, hd in enumerate(hs):
                                    S.op("pe", lambda e, p=p, i=i, hd=hd, lt=lt, rt_=rt_: e.matmul(
                                        p[0:64, i * 64:(i + 1) * 64], lt[:, hd, :], rt_[:, hd, :], start=True, stop=True),
                                        reads=[lt, rt_], writes=[p], inc=(i == HB - 1))
                                S.op("dve", lambda e, p=p, dst=dst, mask=mask, t=t, u=u: e.tensor_tensor(
                                    hview(t[dst], u), pview(p), tri[:, mask:mask + 1, :].to_broadcast([64, HB, 64]), ALU.mult),
                                    reads=[p, tri], writes=[t[dst]])
                            S.op("dve", lambda e, t=t, u=u: e.scalar_tensor_tensor(
                                hview(t["X"], u), hview(t["N"], u), -1.0, I64.unsqueeze(1).to_broadcast([64, HB, 64]), ALU.mult, ALU.add),
                                reads=[t["N"], ident], writes=[t["X"]])
                            cur, curT = "N", "NT"
                            nlev = 5
                            for lev in range(nlev):
                                nxt, nxtT = ("Pa", "PaT") if lev % 2 == 0 else ("Pb", "PbT")
                                p = ps2[ctr["p2"] % 2]
                                ctr["p2"] += 1
                                for i, hd in enumerate(hs):
                                    S.op("pe", lambda e, p=p, i=i, hd=hd, t=t, cur=cur, curT=curT: e.matmul(
                                        p[0:64, i * 64:(i + 1) * 64], t[cur][:, hd, :], t[curT][:, hd, :], start=True, stop=True),
                                        reads=[t[cur], t[curT]], writes=[p], inc=(i == HB - 1))
                                evac(flat(hview(t[nxtT], u)), t[nxtT], p, HB * 64)
                                if lev < nlev - 1:
                                    p = ps2[ctr["p2"] % 2]
                                    ctr["p2"] += 1
                                    for i, hd in enumerate(hs):
                                        S.op("pe", lambda e, p=p, i=i, hd=hd, t=t, cur=cur, curT=curT: e.matmul(
                                            p[0:64, i * 64:(i + 1) * 64], t[curT][:, hd, :], t[cur][:, hd, :], start=True, stop=True),
                                            reads=[t[cur], t[curT]], writes=[p], inc=(i == HB - 1))
                                    evac(flat(hview(t[nxt], u)), t[nxt], p, HB * 64)
                                p = ps2[ctr["p2"] % 2]
                                ctr["p2"] += 1
                                for i, hd in enumerate(hs):
                                    S.op("pe", lambda e, p=p, i=i, hd=hd, t=t, nxtT=nxtT: e.matmul(
                                        p[0:64, i * 64:(i + 1) * 64], t[nxtT][:, hd, :], t["X"][:, hd, :], start=True, stop=True),
                                        reads=[t[nxtT], t["X"]], writes=[p], inc=(i == HB - 1))
                                S.op("dve", lambda e, p=p, t=t, u=u: e.tensor_tensor(
                                    hview(t["X"], u), hview(t["X"], u), pview(p), ALU.add), reads=[p, t["X"]], writes=[t["X"]])
                                cur, curT = nxt, nxtT
                            p = ps3[ctr["p3"] % 3]
                            ctr["p3"] += 1
                            for i, hd in enumerate(hs):
                                S.op("pe", lambda e, p=p, i=i, hd=hd, t=t: e.matmul(
                                    p[0:64, i * 64:(i + 1) * 64], A[:, hd, :], t["ST"][:, hd, :], start=True, stop=False),
                                    reads=[A, t["ST"]], writes=[p], inc=False)
                                S.op("pe", lambda e, p=p, i=i, hd=hd, t=t: e.matmul(
                                    p[0:64, i * 64:(i + 1) * 64], t["G"][:, hd, :], t["Vt"][:, hd, :], start=False, stop=True),
                                    reads=[t["G"], t["Vt"]], writes=[p], inc=(i == HB - 1))
                            evac(flat(hview(t["XT"], u)), t["XT"], p, HB * 64)
                            p = ps3[ctr["p3"] % 3]
                            ctr["p3"] += 1
                            for i, hd in enumerate(hs):
                                S.op("pe", lambda e, p=p, i=i, hd=hd, t=t: e.matmul(
                                    p[0:64, i * 64:(i + 1) * 64], t["X"][:, hd, :], t["XT"][:, hd, :], start=True, stop=True),
                                    reads=[t["X"], t["XT"]], writes=[p], inc=(i == HB - 1))
                            S.op("dve", lambda e, p=p, t=t, u=u: e.tensor_scalar(
                                flat(hview(t["nS"], u)), p[0:64, 0:HB * 64], -1.0, None, ALU.mult), reads=[p], writes=[t["nS"]])
                            p = ps3[ctr["p3"] % 3]
                            ctr["p3"] += 1
                            for i, hd in enumerate(hs):
                                S.op("pe", lambda e, p=p, i=i, hd=hd, t=t: e.matmul(
                                    p[0:64, i * 64:(i + 1) * 64], t["ST"][:, hd, :], R[:, hd, :], start=True, stop=False),
                                    reads=[t["ST"], R], writes=[p], inc=False)
                                S.op("pe", lambda e, p=p, i=i, hd=hd, t=t: e.matmul(
                                    p[0:64, i * 64:(i + 1) * 64], t["Vt"][:, hd, :], t["Hk"][:, hd, :], start=False, stop=False),
                                    reads=[t["Vt"], t["Hk"]], writes=[p], inc=False)
                                S.op("pe", lambda e, p=p, i=i, hd=hd, t=t: e.matmul(
                                    p[0:64, i * 64:(i + 1) * 64], t["nS"][:, hd, :], t["Hb"][:, hd, :], start=False, stop=True),
                                    reads=[t["nS"], t["Hb"]], writes=[p], inc=(i == HB - 1))
                            evac(flat(hview(t["Y"], u)), t["Y"], p, HB * 64)
                            p = ps3[ctr["p3"] % 3]
                            ctr["p3"] += 1
                            for i, hd in enumerate(hs):
                                S.op("pe", lambda e, p=p, i=i, hd=hd, t=t: e.matmul(
                                    p[0:64, i * 64:(i + 1) * 64], t["Kt"][:, hd, :], t["Vt"][:, hd, :], start=True, stop=False),
                                    reads=[t["Kt"], t["Vt"]], writes=[p], inc=False)
                                S.op("pe", lambda e, p=p, i=i, hd=hd, t=t: e.matmul(
                                    p[0:64, i * 64:(i + 1) * 64], t["Bt"][:, hd, :], t["nS"][:, hd, :], start=False, stop=True),
                                    reads=[t["Bt"], t["nS"]], writes=[p], inc=(i == HB - 1))
                            S.op("dve", lambda e, p=p, t=t, u=u: e.tensor_tensor(hview(t["ST"], u), hview(t["ST"], u), pview(p), ALU.add),
                                 reads=[p, t["ST"]], writes=[t["ST"]])
                            S.op("dve", lambda e, t=t, u=u, n=n: e.tensor_tensor(
                                hview(t["ST"], u), hview(t["ST"], u), t["P"][:, u * HB:(u + 1) * HB, n:n + 1].to_broadcast([64, HB, 64]), ALU.mult),
                                reads=[t["P"], t["ST"]], writes=[t["ST"]])
                        cs = slice(n * CK, (n + 1) * CK)
                        S.dma("sp", rwY[d][:, cs].rearrange("(h v) t -> v h t", v=64), t["Y"][:], reads=[t["Y"]])
                        tpos = (n + 1) * CK if d == 0 else n * CK
                        if tpos % c.SEQ == 0:
                            seg = tpos // c.SEQ - 1 if d == 0 else tpos // c.SEQ
                            S.dma("sp", statesT[seg, l, d].rearrange("g (j k) v -> k (g j) v", j=2), t["ST"][:], reads=[t["ST"]])
                S.barrier()

        def phase_rw_post(l):
            with ExitStack() as es:
                prm = alloc(es, "qprm", [128, 5, RG])
                bones = alloc(es, "qbones", [128, 128])
                y0 = [alloc(es, f"qy0{i}", [128, T]) for i in range(2)]
                y1 = [alloc(es, f"qy1{i}", [128, T]) for i in range(2)]
                bon = [alloc(es, f"qbon{i}", [128, T]) for i in range(2)]
                gat = [alloc(es, f"qgat{i}", [128, T]) for i in range(2)]
                mu = alloc(es, "qmu", [128, T])
                sq = alloc(es, "qsq", [128, T])
                ps = [palloc(es, f"qps{i}", [128, TT]) for i in range(4)]
                S.dma("sp", prm[:], rwp_d[l], writes=[prm])
                S.dma("sp", bones[:], bones_d, writes=[bones])
                pk = 0
                for g in range(RG):
                    gs = slice(g * 128, (g + 1) * 128)
                    a, b, bo, ga = y0[g % 2], y1[g % 2], bon[g % 2], gat[g % 2]
                    S.dma("sp", a[:], rwY[0][gs, :], writes=[a])
                    S.dma("sp", b[:], rwY[1][gs, :], writes=[b])
                    S.dma("sp", bo[:], rwBon[gs, :], writes=[bo])
                    S.dma("sp", ga[:], rwGate[gs, :], writes=[ga])
                    S.op("dve", lambda e, a=a, b=b: e.tensor_tensor(a[:], a[:], b[:], ALU.add), reads=[a, b], writes=[a])
                    for tt in range(NT):
                        sl = slice(tt * TT, (tt + 1) * TT)
                        p = ps[pk % 4]
                        pk += 1
                        S.op("pe", lambda e, p=p, a=a, sl=sl: e.matmul(p[:], bones[:], a[:, sl], start=True, stop=True), reads=[bones, a], writes=[p])
                        S.op("act", lambda e, p=p, sl=sl: e.activation(mu[:, sl], p[:], AF.Identity, bias=0.0, scale=-1.0 / 64), reads=[p], writes=[mu])
                    S.op("dve", lambda e, a=a: e.tensor_tensor(a[:], a[:], mu[:], ALU.add), reads=[a, mu], writes=[a])
                    S.op("act", lambda e, a=a: e.activation(sq[:], a[:], AF.Square), reads=[a], writes=[sq])
                    for tt in range(NT):
                        sl = slice(tt * TT, (tt + 1) * TT)
                        p = ps[pk % 4]
                        pk += 1
                        S.op("pe", lambda e, p=p, sl=sl: e.matmul(p[:], bones[:], sq[:, sl], start=True, stop=True), reads=[bones, sq], writes=[p])
                        S.op("act", lambda e, p=p, sl=sl: e.activation(mu[:, sl], p[:], AF.Sqrt, bias=64e-5, scale=1.0 / 64), reads=[p], writes=[mu])
                    S.op("dve", lambda e: e.reciprocal(mu[:], mu[:]), reads=[mu], writes=[mu])
                    S.op("dve", lambda e, a=a: e.tensor_tensor(a[:], a[:], mu[:], ALU.mult), reads=[a, mu], writes=[a])
                    S.op("act", lambda e, a=a, g=g: e.activation(a[:], a[:], AF.Identity, bias=prm[:, 4, g:g + 1], scale=prm[:, 3, g:g + 1]),
                         reads=[a, prm], writes=[a])
                    S.op("dve", lambda e, a=a, bo=bo: e.tensor_tensor(a[:], a[:], bo[:], ALU.add), reads=[a, bo], writes=[a])
                    S.op("pool", lambda e, a=a, ga=ga: e.tensor_tensor(a[:], a[:], ga[:], ALU.mult), reads=[a, ga], writes=[a])
                    S.dma("sp", mixT_d[c.DA_W + g * 128:c.DA_W + (g + 1) * 128, :], a[:], reads=[a])
                S.barrier()

        def phase_rw_zero(l):
            with ExitStack() as es:
                z = alloc(es, "zz", [128, T])
                S.op("dve", lambda e: e.memset(z[:], 0.0), writes=[z])
                for g in range(c.RG):
                    S.dma("sp", mixT_d[c.DA_W + g * 128:c.DA_W + (g + 1) * 128, :], z[:], reads=[z])
                S.barrier()

        phase_mod()
        src = xT_in
        for l in range(L):
            if stop <= 1:
                break
            with ExitStack() as es:
                hT = alloc(es, "hT", [128, DC, T], BF16)
                phase_norm(es, src, a1[l], (l, SH1), hT)
                if stop <= 2:
                    break
                phase_proj(l, hT)
            if stop <= 3:
                break
            phase_attn(l)
            phase_cm(l)
            if rw:
                phase_rw_pre(l)
                if rwstop <= 1:
                    break
                phase_rw_scan(l)
                if rwstop <= 2:
                    break
                phase_rw_post(l)
            else:
                phase_rw_zero(l)
            with ExitStack() as es:
                mixT = alloc(es, "mixT", [128, DC, T], BF16)
                for ch in range(DC):
                    S.dma("pool", mixT[:, ch, :], mixT_d[ch * 128:(ch + 1) * 128, :], writes=[mixT])
                phase_wout(l, mixT, src, xs[0])
            if stop <= 4:
                break
            if debug and l == 0:
                with ExitStack() as es:
                    tmp = alloc(es, "dbgx", [128, T])
                    for ch in range(DC):
                        S.dma("sp", tmp[:], xs[0][ch * 128:(ch + 1) * 128, :], writes=[tmp])
                        S.dma("sp", dbg["x1"][ch * 128:(ch + 1) * 128, :], tmp[:], reads=[tmp])
                    S.barrier()
            with ExitStack() as es:
                hT = alloc(es, "hT2", [128, DC, T], BF16)
                phase_norm(es, xs[0], a2[l], (l, SH2), hT)
                if stop <= 5:
                    break
                phase_ffn_up(l, hT)
            if stop <= 6:
                break
            phase_ffn_down(l, xs[0], xs[1])
            src = xs[1]
        if stop > 7 and rwstop >= 9:
            phase_norm(None, src, fg, None, None, dst_dram=yT)
        S.barrier()
        es_glob.close()

    for si in range(NS):
        cur[0] = si
        emit_stream()
    return nc, S


def _pm(v, nchunk):
    return np.ascontiguousarray(np.asarray(v, np.float32).reshape(nchunk, 128).T)


def make_streams(cfg, BATCH, DEC_BATCH):
    st = [("lat", b) for b in range(DEC_BATCH)]
    per = cfg.T // cfg.SEQ
    for i in range(0, BATCH, per):
        st.append(("ctx", list(range(i, i + per))))
    return st


def prep_inputs(cfg, inp, core_streams):
    c = cfg
    L, DC, T = c.L, c.DC, c.T
    f32 = np.float32
    shared = {}
    shared["ident"] = np.eye(128, dtype=f32)
    shared["mod_w"] = np.ascontiguousarray(inp["mod_w"], f32)
    shared["mod_bT"] = np.stack([_pm(inp["mod_b"][l], 6 * DC) for l in range(L)])
    shared["n1gT"] = np.stack([_pm(inp["norm1_g"][l], DC) for l in range(L)])
    shared["n2gT"] = np.stack([_pm(inp["norm2_g"][l], DC) for l in range(L)])
    shared["fgT"] = _pm(inp["final_norm_g"], DC)
    shared["w_in"] = np.ascontiguousarray(inp["w_in"], f32)
    shared["w_out"] = np.ascontiguousarray(inp["w_out"], f32)
    shared["ffn_up"] = np.ascontiguousarray(inp["ffn_up"], f32)
    shared["ffn_down"] = np.ascontiguousarray(inp["ffn_down"], f32)
    cw = np.asarray(inp["ffn_conv_w"], f32)
    shared["ffn_cw"] = np.ascontiguousarray(cw.reshape(L, 3, 2 * c.KF, 128).transpose(0, 3, 2, 1))
    shared["ffn_cb"] = np.stack([_pm(inp["ffn_conv_b"][l], 2 * c.KF) for l in range(L)])
    shared["lam_rep"] = np.ascontiguousarray(np.broadcast_to(
        np.asarray(inp["da_lambda"], f32).reshape(L, 1, 512), (L, 128, 512)))
    shared["subg"] = np.stack([_pm(inp["da_subln_g"][l], 2) for l in range(L)])
    shared["cmg"] = np.stack([_pm(inp["cm_norm_g"][l], c.CM_W // 128) for l in range(L)])
    shared["wsT"] = np.ascontiguousarray(np.asarray(inp["cm_ws"], f32).transpose(0, 1, 3, 2))
    shared["bs_rep"] = np.ascontiguousarray(np.broadcast_to(
        np.asarray(inp["cm_bs"], f32)[:, :, None, :], (L, 4, 128, 128)))
    RG, RW_W, RW_IN = c.RG, c.RW_W, c.RW_IN
    NRC = (RW_IN + 127) // 128
    CK = 64
    NCH = T // CK
    rcw = np.zeros((L, 3, NRC * 128), f32)
    rcw[:, :, :RW_IN] = np.asarray(inp["rw_conv_w"], f32)
    shared["rw_cw"] = np.ascontiguousarray(rcw.reshape(L, 3, NRC, 128).transpose(0, 3, 2, 1))
    shared["w0T"] = np.ascontiguousarray(np.asarray(inp["rw_w0"], f32).reshape(L, 2, RG, 128).transpose(0, 3, 1, 2))
    shared["a0T"] = np.ascontiguousarray(np.asarray(inp["rw_a0"], f32).reshape(L, 2, RG, 128).transpose(0, 3, 1, 2))
    shared["rw_w2"] = np.ascontiguousarray(inp["rw_w2"], f32)
    shared["rw_a2"] = np.ascontiguousarray(inp["rw_a2"], f32)
    shared["rw_g2"] = np.ascontiguousarray(inp["rw_g2"], f32)
    prm = np.stack([np.asarray(inp[k], f32).reshape(L, RW_W) for k in ("rw_k_k", "rw_k_a", "rw_r_k", "rw_gn_g", "rw_gn_b")], axis=1)
    shared["rwp"] = np.ascontiguousarray(prm.reshape(L, 5, RG, 128).transpose(0, 3, 1, 2))
    bo = np.zeros((128, 128), f32)
    bo[:64, :64] = 1.0
    bo[64:, 64:] = 1.0
    shared["bones"] = bo
    tt_ = np.arange(T)
    shared["cmask"] = np.ascontiguousarray(np.broadcast_to((tt_ % CK != 0).astype(f32), (128, T)))
    ii = np.arange(64)
    tri = np.stack([(ii[:, None] < ii[None, :]), (ii[:, None] <= ii[None, :]),
                    (ii[:, None] > ii[None, :]), (ii[:, None] >= ii[None, :])], axis=1).astype(f32)
    shared["tri"] = np.ascontiguousarray(tri)
    maps = []
    for streams in core_streams:
        cm = dict(shared)
        for si, (kind, ident_) in enumerate(streams):
            m = _prep_stream(cfg, inp, kind, ident_)
            for k_, v_ in m.items():
                cm[f"{k_}_{si}"] = v_
        maps.append(cm)
    return maps


def _prep_stream(cfg, inp, kind, ident_):
    c = cfg
    L, DC, T = c.L, c.DC, c.T
    f32 = np.float32
    RG = c.RG
    CK = 64
    NCH = T // CK
    if True:
        m = {}
        if kind == "lat":
            b = ident_
            m["xT"] = np.ascontiguousarray(np.asarray(inp["x_sample"][b], f32).T)
            m["condT"] = _pm(inp["c"][b], DC)
            seqlen = T
        else:
            xs_ = np.concatenate([np.asarray(inp["x_prompt"][s], f32) for s in ident_], axis=0)
            m["xT"] = np.ascontiguousarray(xs_.T)
            m["condT"] = _pm(inp["c_ctx"], DC)
            seqlen = c.SEQ
        t = np.arange(T)
        mp = (t % seqlen != 0).astype(f32)
        mn = (t % seqlen != seqlen - 1).astype(f32)
        m["mprev"] = np.ascontiguousarray(np.broadcast_to(mp, (128, T)))
        m["mnext"] = np.ascontiguousarray(np.broadcast_to(mn, (128, T)))
        qa = np.zeros((8, T), f32)
        ka = np.zeros((8, c.NK), f32)
        sid = t // seqlen
        qa[sid, t] = MASK_BIG
        ka[sid, t] = 1.0
        rc = np.ones((128, T), np.float64)
        rsn = np.zeros((128, T), np.float64)
        if kind == "lat":
            ka[0, T:] = 1.0
            kc = np.asarray(inp["cache_da_k"][b], f32)
            m["kcT"] = np.ascontiguousarray(kc.transpose(0, 2, 3, 4, 1))
            vcc = np.asarray(inp["cache_da_v"][b], f32)
            m["vc"] = np.ascontiguousarray(vcc.transpose(0, 2, 1, 3))
            p = np.arange(128)
            i = p % 64
            j = i % 32
            inv = 10000.0 ** (-j.astype(np.float64) / 32)
            row = (t // c.GRID_W).astype(np.float64)
            col = (t % c.GRID_W).astype(np.float64)
            pos = np.where((p < 64)[:, None], row[None, :], col[None, :])
            ang = (pos.astype(np.float32) * inv.astype(np.float32)[:, None]).astype(np.float32)
            rc = np.cos(ang)
            sn = np.sin(ang)
            rsn = np.where((i < 32)[:, None], -sn, sn)
        else:
            m["kcT"] = np.zeros((L, c.H, 2, 128, c.PAST), f32)
            m["vc"] = np.zeros((L, c.H, c.PAST, 256), f32)
        m["qaug"] = qa
        m["kaug"] = ka
        kp = np.ones((2, NCH), f32)
        if kind == "lat":
            st = np.asarray(inp["state_rwkv"][b], f32)
            m["s0T"] = np.ascontiguousarray(st.transpose(0, 1, 2, 4, 3).reshape(L, 2, RG, 128, 64))
        else:
            m["s0T"] = np.zeros((L, 2, RG, 128, 64), f32)
            cpos = np.arange(NCH) * CK
            kp[0] = (cpos % c.SEQ != 0)
            kp[1] = ((cpos + CK) % c.SEQ != 0)
        m["keepc"] = np.ascontiguousarray(np.broadcast_to(kp, (128, 2, NCH)))
        m["ropeC"] = np.ascontiguousarray(rc, f32)
        m["ropeS"] = np.ascontiguousarray(rsn, f32)
    return m


def assemble(cfg, results, core_streams, BATCH, DEC_BATCH):
    c = cfg
    L, T, D = c.L, c.T, c.D
    f32 = np.float32
    y_prompt = np.zeros((BATCH, c.SEQ, D), f32)
    y_sample = np.zeros((DEC_BATCH, T, D), f32)
    new_k = np.zeros((BATCH, L, c.SEQ, c.H, 2, 128), f32)
    new_v = np.zeros((BATCH, L, c.SEQ, c.H, 256), f32)
    new_s = np.zeros((BATCH, L, 2, c.RH, 64, 64), f32)
    for ci, streams in enumerate(core_streams):
      for si, (kind, ident_) in enumerate(streams):
        r = {k_[:-len(f"_{si}")]: v_ for k_, v_ in results[ci].items() if k_.endswith(f"_{si}")}
        if kind == "lat":
            y_sample[ident_] = r["yT"].T
        else:
            y = r["yT"].T
            kv = r["kvT"]
            for j, s in enumerate(ident_):
                tsl = slice(j * c.SEQ, (j + 1) * c.SEQ)
                y_prompt[s] = y[tsl]
                for l in range(L):
                    new_k[s, l] = kv[l, 0:c.DA_W, tsl].T.reshape(c.SEQ, c.H, 2, 128)
                    new_v[s, l] = kv[l, c.DA_W:2 * c.DA_W, tsl].T.reshape(c.SEQ, c.H, 256)
                st = r["statesT"][j]
                new_s[s] = st.reshape(L, 2, c.RH, 64, 64).transpose(0, 1, 2, 4, 3)
    return y_prompt, y_sample, new_k, new_v, new_s


def plan_cores(streams, n_cores):
    ns = (len(streams) + n_cores - 1) // n_cores
    cs = []
    for ci in range(n_cores):
        cs.append([streams[min(ci * ns + j, len(streams) - 1)] for j in range(ns)])
    return cs, ns


N_CORES = 6


def kernel(**inputs):
    cfg = Cfg()
    inp = {k: np.asarray(v) for k, v in inputs.items()}
    BATCH = inp["x_prompt"].shape[0]
    DEC_BATCH = inp["x_sample"].shape[0]
    streams = make_streams(cfg, BATCH, DEC_BATCH)
    core_streams, ns = plan_cores(streams, N_CORES)
    maps = prep_inputs(cfg, inp, core_streams)
    nc, _ = build(cfg, NS=ns)
    res = run_bass_kernel_spmd(nc, maps, core_ids=list(range(N_CORES)))
    return assemble(cfg, res.results, core_streams, BATCH, DEC_BATCH)
```
